# Optimizing a Trainium2 kernel written in Bass

```python
import jax, jax.numpy as jnp
from jax import lax
import numpy as np

D_MODEL = 1024
BATCH = 32
SEQ = 2048
DEPTH = 1

N_META = 16
CHUNK = 128
EPS = 1e-6
RET_HEADS = 4
RET_DK = 256
RET_DV = 512
RET_QK = RET_HEADS * RET_DK
RET_V = RET_HEADS * RET_DV
ROPE_BASE = 10000.0
SSD_DINNER = 2 * D_MODEL
SSD_HEADDIM = 64
SSD_HEADS = SSD_DINNER // SSD_HEADDIM
SSD_GROUPS = 8
SSD_HPG = SSD_HEADS // SSD_GROUPS
SSD_STATE = 128
SSD_CONV = 4
SSD_XBC = SSD_DINNER + 2 * SSD_GROUPS * SSD_STATE
SSD_NORM_GROUP = SSD_DINNER // SSD_GROUPS
N_BRANCH = 2
D_FF = -(-8 * D_MODEL // (3 * 256)) * 256
IN_WIDTHS = (RET_QK, RET_QK, RET_V, RET_V, SSD_DINNER, SSD_XBC, SSD_HEADS, N_BRANCH * D_MODEL)
IN_WIDTH = 2 * RET_QK + 2 * RET_V + SSD_DINNER + SSD_XBC + SSD_HEADS + N_BRANCH * D_MODEL

kernel_name = 'hybrid_retention_ssd_block'


def _rmsnorm(x, g):
    xf = x.astype(jnp.float32)
    y = xf * lax.rsqrt(jnp.mean(xf * xf, axis=-1, keepdims=True) + EPS)
    return (y * g.astype(jnp.float32)).astype(x.dtype)


def _rotary(t, pos):
    half = RET_DK // 2
    inv = ROPE_BASE ** (-jnp.arange(half, dtype=jnp.float32) / half)
    ang = pos[:, None] * inv[None, :]
    cos = jnp.cos(ang)[None, :, None, :]
    sin = jnp.sin(ang)[None, :, None, :]
    t1, t2 = t[..., :half], t[..., half:]
    return jnp.concatenate([t1 * cos - t2 * sin, t1 * sin + t2 * cos], axis=-1)


def _retention(q, k, v):
    b, l = q.shape[0], q.shape[1]
    pad = CHUNK - N_META
    lp = l + pad
    nc = lp // CHUNK

    def chunks(t):
        t = jnp.pad(t, ((0, 0), (pad, 0), (0, 0), (0, 0)))
        t = t.reshape(b, nc, CHUNK, t.shape[2], t.shape[3])
        return jnp.transpose(t, (1, 0, 3, 2, 4))

    log_g = jnp.log1p(-jnp.exp2(-5.0 - jnp.arange(RET_HEADS, dtype=jnp.float32)))
    idx = jnp.arange(CHUNK, dtype=jnp.float32)
    diff = idx[:, None] - idx[None, :]
    dmask = jnp.where(diff >= 0, jnp.exp(log_g[:, None, None] * jnp.maximum(diff, 0.0)), 0.0)
    xi = jnp.exp(log_g[:, None] * (idx[None, :] + 1.0))[..., None]
    zeta = jnp.exp(log_g[:, None] * (CHUNK - 1.0 - idx[None, :]))[..., None]
    g_chunk = jnp.exp(log_g * CHUNK)[:, None, None]

    def step(state, qkv):
        qc, kc, vc = qkv
        s = jnp.einsum('bhtd,bhsd->bhts', qc, kc) * dmask
        out = (jnp.einsum('bhts,bhse->bhte', s, vc)
               + jnp.einsum('bhtd,bhde->bhte', qc, state) * xi)
        state = state * g_chunk + jnp.einsum('bhsd,bhse->bhde', kc * zeta, vc)
        return state, out

    state0 = jnp.zeros((b, RET_HEADS, RET_DK, RET_DV), jnp.float32)
    _, out = lax.scan(step, state0, (chunks(q), chunks(k), chunks(v)))
    out = jnp.transpose(out, (1, 0, 3, 2, 4)).reshape(b, lp, RET_HEADS, RET_DV)
    return out[:, pad:]


def _ssd(x, dt, bm, cm, a):
    b, l = x.shape[0], x.shape[1]
    pad = CHUNK - N_META
    lp = l + pad
    nc = lp // CHUNK

    def chunks(t):
        t = jnp.pad(t, ((0, 0), (pad, 0)) + ((0, 0),) * (t.ndim - 2))
        t = t.reshape((b, nc, CHUNK) + t.shape[2:])
        return jnp.moveaxis(t, 1, 0)

    causal = jnp.tril(jnp.ones((CHUNK, CHUNK), dtype=bool))[None, :, :, None, None]

    def step(h, inp):
        xc, dtc, bc, cc = inp
        cs = jnp.cumsum(dtc * a, axis=1)
        seg = cs[:, :, None] - cs[:, None, :]
        lmat = jnp.exp(jnp.where(causal, seg, -jnp.inf))
        cb = jnp.einsum('btgn,bsgn->btsg', cc, bc)
        y = jnp.einsum('btsg,btsgh,bsgh,bsghp->btghp', cb, lmat, dtc, xc)
        y = y + jnp.einsum('btgn,bghpn->btghp', cc, h) * jnp.exp(cs)[..., None]
        dec = jnp.exp(cs[:, -1:] - cs) * dtc
        h = (h * jnp.exp(cs[:, -1])[..., None, None]
             + jnp.einsum('bsgn,bsgh,bsghp->bghpn', bc, dec, xc))
        return h, y

    h0 = jnp.zeros((b, SSD_GROUPS, SSD_HPG, SSD_HEADDIM, SSD_STATE), jnp.float32)
    _, y = lax.scan(step, h0, (chunks(x), chunks(dt), chunks(bm), chunks(cm)))
    y = jnp.moveaxis(y, 0, 1).reshape(b, lp, SSD_GROUPS, SSD_HPG, SSD_HEADDIM)
    return y[:, pad:]


def _causal_dwconv(u, w, bias):
    out = lax.conv_general_dilated(
        u, w[:, None, :].astype(u.dtype), window_strides=(1,),
        padding=[(SSD_CONV - 1, 0)], dimension_numbers=('NWC', 'WIO', 'NWC'),
        feature_group_count=u.shape[-1])
    return out + bias


def _mixer(u, pos, w_in, conv_w, conv_b, dt_bias, a_log, d_skip, ssd_norm,
           w_ret_branch, w_ssd_branch, w_out):
    b, l = u.shape[0], u.shape[1]
    proj = u @ w_in
    offsets = np.cumsum(IN_WIDTHS)[:-1].tolist()
    q, k, v, g_ret, z, xbc, dt, gates = jnp.split(proj, offsets, axis=-1)

    q = _rotary(q.reshape(b, l, RET_HEADS, RET_DK), pos)
    k = _rotary(k.reshape(b, l, RET_HEADS, RET_DK), pos) * (RET_DK ** -0.5)
    v = v.reshape(b, l, RET_HEADS, RET_DV)
    ret = _retention(q, k, v)
    ret = ret * lax.rsqrt(jnp.mean(ret * ret, axis=-1, keepdims=True) + EPS)
    ret = ret.reshape(b, l, RET_V) * jax.nn.silu(g_ret)
    branch_a = ret @ w_ret_branch

    xbc = jax.nn.silu(_causal_dwconv(xbc, conv_w, conv_b))
    xs, bm, cm = jnp.split(xbc, [SSD_DINNER, SSD_DINNER + SSD_GROUPS * SSD_STATE], axis=-1)
    xs = xs.reshape(b, l, SSD_GROUPS, SSD_HPG, SSD_HEADDIM)
    bm = bm.reshape(b, l, SSD_GROUPS, SSD_STATE)
    cm = cm.reshape(b, l, SSD_GROUPS, SSD_STATE)
    dt = jax.nn.softplus(dt.astype(jnp.float32) + dt_bias.astype(jnp.float32))
    dt = dt.reshape(b, l, SSD_GROUPS, SSD_HPG)
    a = -jnp.exp(a_log.astype(jnp.float32)).reshape(SSD_GROUPS, SSD_HPG)
    y = _ssd(xs, dt, bm, cm, a) + d_skip.reshape(SSD_GROUPS, SSD_HPG, 1) * xs
    y = y.reshape(b, l, SSD_DINNER) * jax.nn.silu(z)
    yg = y.reshape(b, l, SSD_GROUPS, SSD_NORM_GROUP).astype(jnp.float32)
    yg = yg * lax.rsqrt(jnp.mean(yg * yg, axis=-1, keepdims=True) + EPS)
    y = yg.reshape(b, l, SSD_DINNER) * ssd_norm
    branch_b = y @ w_ssd_branch

    g_a, g_b = jnp.split(gates, N_BRANCH, axis=-1)
    merged = jax.nn.sigmoid(g_a) * branch_a + jax.nn.sigmoid(g_b) * branch_b
    return merged @ w_out


def _swiglu(u, w_gate, w_up, w_down):
    return (jax.nn.silu(u @ w_gate) * (u @ w_up)) @ w_down


def setup_inputs(seed: int = 0) -> dict:
    key = jax.random.key(seed)
    ks = jax.random.split(key, 20)
    f32 = jnp.float32

    def nrm(k, shape, scale):
        return jax.random.normal(k, shape, f32) * scale

    x = nrm(ks[0], (BATCH, SEQ, D_MODEL), 1.0)
    meta_tokens = nrm(ks[1], (N_META, D_MODEL), 1.0)
    norm_mix_pre = 1.0 + nrm(ks[2], (DEPTH, D_MODEL), 0.05)
    w_in = nrm(ks[3], (DEPTH, D_MODEL, IN_WIDTH), D_MODEL ** -0.5)
    conv_w = nrm(ks[4], (DEPTH, SSD_CONV, SSD_XBC), SSD_CONV ** -0.5)
    conv_b = nrm(ks[5], (DEPTH, SSD_XBC), 0.02)
    dt0 = jnp.exp(jax.random.uniform(ks[6], (DEPTH, SSD_HEADS), f32,
                                     minval=np.log(1e-3), maxval=np.log(1e-1)))
    dt_bias = dt0 + jnp.log(-jnp.expm1(-dt0))
    a_log = jnp.log(jax.random.uniform(ks[7], (DEPTH, SSD_HEADS), f32, minval=1.0, maxval=16.0))
    d_skip = 1.0 + nrm(ks[8], (DEPTH, SSD_HEADS), 0.05)
    ssd_norm = 1.0 + nrm(ks[9], (DEPTH, SSD_DINNER), 0.05)
    w_ret_branch = nrm(ks[10], (DEPTH, RET_V, D_MODEL), RET_V ** -0.5)
    w_ssd_branch = nrm(ks[11], (DEPTH, SSD_DINNER, D_MODEL), SSD_DINNER ** -0.5)
    w_out = nrm(ks[12], (DEPTH, D_MODEL, D_MODEL), D_MODEL ** -0.5)
    norm_mix_post = 1.0 + nrm(ks[13], (DEPTH, D_MODEL), 0.05)
    norm_ffn_pre = 1.0 + nrm(ks[14], (DEPTH, D_MODEL), 0.05)
    w_gate = nrm(ks[15], (DEPTH, D_MODEL, D_FF), D_MODEL ** -0.5)
    w_up = nrm(ks[16], (DEPTH, D_MODEL, D_FF), D_MODEL ** -0.5)
    w_down = nrm(ks[17], (DEPTH, D_FF, D_MODEL), D_FF ** -0.5)
    norm_ffn_post = 1.0 + nrm(ks[18], (DEPTH, D_MODEL), 0.05)
    return {'x': x, 'meta_tokens': meta_tokens, 'norm_mix_pre': norm_mix_pre, 'w_in': w_in,
            'conv_w': conv_w, 'conv_b': conv_b, 'dt_bias': dt_bias, 'a_log': a_log,
            'd_skip': d_skip, 'ssd_norm': ssd_norm, 'w_ret_branch': w_ret_branch,
            'w_ssd_branch': w_ssd_branch, 'w_out': w_out, 'norm_mix_post': norm_mix_post,
            'norm_ffn_pre': norm_ffn_pre, 'w_gate': w_gate, 'w_up': w_up, 'w_down': w_down,
            'norm_ffn_post': norm_ffn_post}


def reference(x, meta_tokens, norm_mix_pre, w_in, conv_w, conv_b, dt_bias, a_log, d_skip,
              ssd_norm, w_ret_branch, w_ssd_branch, w_out, norm_mix_post, norm_ffn_pre,
              w_gate, w_up, w_down, norm_ffn_post):
    b = x.shape[0]
    meta = jnp.broadcast_to(meta_tokens[None].astype(x.dtype), (b, N_META, D_MODEL))
    h = jnp.concatenate([meta, x], axis=1)
    pos = jnp.arange(h.shape[1], dtype=jnp.float32)
    for i in range(DEPTH):
        mix = _mixer(_rmsnorm(h, norm_mix_pre[i]), pos, w_in[i], conv_w[i], conv_b[i],
                     dt_bias[i], a_log[i], d_skip[i], ssd_norm[i], w_ret_branch[i],
                     w_ssd_branch[i], w_out[i])
        h = h + _rmsnorm(mix, norm_mix_post[i]).astype(h.dtype)
        f = _swiglu(_rmsnorm(h, norm_ffn_pre[i]), w_gate[i], w_up[i], w_down[i])
        h = h + _rmsnorm(f, norm_ffn_post[i]).astype(h.dtype)
    return h[:, N_META:]
```

```python
import math
from contextlib import ExitStack

import numpy as np
import concourse.bass as bass
import concourse.mybir as mybir
from concourse.bass_utils import run_bass_kernel_spmd

F32 = mybir.dt.float32
BF16 = mybir.dt.bfloat16
I32 = mybir.dt.int32
AF = mybir.ActivationFunctionType
ALU = mybir.AluOpType

D = 1024
NMETA = 16
SEQ = 2048
IN_W = 14368
DFF = 2816
EPS = 1e-6
C_Q, C_K, C_V, C_G, C_Z, C_XBC, C_DT, C_GATE = 0, 1024, 2048, 4096, 6144, 8192, 12288, 12320
LOGG = [math.log1p(-2.0 ** (-5.0 - h)) for h in range(4)]
NW = 3
NDMA_SEM = 8
PREFETCH = True


class Tk:
    __slots__ = ("name", "w", "r", "rw")

    def __init__(self, name, rw=False):
        self.name = name
        self.w = None
        self.r = {}
        self.rw = rw


class Eng:
    def __init__(self, name, e, sem, kind):
        self.name, self.e, self.sem, self.kind = name, e, sem, kind
        self.cnt = 0
        self.seen = {}
        self.dma_sems = []
        self.dma_n = 0


class K:
    def __init__(self, nc, es):
        self.nc, self.es = nc, es
        self.pe = self._eng("pe", nc.tensor, "pe")
        self.act = self._eng("act", nc.scalar, "c")
        self.dve = self._eng("dve", nc.vector, "c")
        self.pool = self._eng("pool", nc.gpsimd, "c")
        self.sp = self._eng("sp", nc.sync, "q")
        for q, n in ((self.sp, NDMA_SEM), (self.pool, 64)):
            q.dma_sems = [es.enter_context(nc.semaphore(f"d_{q.name}{i}")) for i in range(n)]
        self.n_instr = 0
        self.tag = "setup"
        self.tagmap = {}

    def _eng(self, name, e, kind):
        sem = self.es.enter_context(self.nc.semaphore("s_" + name))
        return Eng(name, e, sem, kind)

    def _wait(self, eng, dep):
        sem, val, src = dep
        if src is eng and eng.kind == "pe":
            return
        k = id(sem)
        if eng.seen.get(k, 0) >= val:
            return
        eng.e.wait_ge(sem, val)
        eng.seen[k] = val

    def _deps(self, eng, reads, writes):
        for t in reads:
            if t.w is not None:
                self._wait(eng, t.w)
            if t.rw:
                for d in t.r.values():
                    self._wait(eng, d)
        for t in writes:
            if t.w is not None:
                self._wait(eng, t.w)
            for d in t.r.values():
                if d[2] is eng and eng.kind != "q" and not t.rw:
                    continue
                self._wait(eng, d)

    def _record(self, reads, writes, dep):
        k = id(dep[0])
        for t in reads:
            if t.rw:
                t.w = dep
                t.r = {}
            else:
                t.r[k] = dep
        for t in writes:
            t.w = dep
            t.r = {}

    def op(self, eng, reads, writes, fn, signal=True):
        self._deps(eng, reads, writes)
        ins = fn(eng.e)
        self.n_instr += 1
        self.tagmap[ins.ins.name] = self.tag
        if signal:
            eng.cnt += 1
            ins.then_inc(eng.sem, 1)
            dep = (eng.sem, eng.cnt, eng)
        else:
            dep = (eng.sem, eng.cnt + 1, eng)
        self._record(reads, writes, dep)
        return ins

    def dma(self, q, out, in_, reads, writes, **kw):
        i = q.dma_n
        q.dma_n += 1
        ns = len(q.dma_sems)
        sem = q.dma_sems[i % ns]
        prev = 16 * (i // ns)
        if prev > 0:
            self._wait(q, (sem, prev, None))
        self._deps(q, reads, writes)
        q.e.dma_start(out=out, in_=in_, **kw).then_inc(sem, 16)
        self.n_instr += 1
        dep = (sem, prev + 16, None)
        self._record(reads, writes, dep)

    def handoff(self, olds, news):
        for n in news:
            for o in olds:
                for d in ([o.w] if o.w is not None else []) + list(o.r.values()):
                    k = id(d[0])
                    if k not in n.r or n.r[k][1] < d[1]:
                        n.r[k] = d

    def finish(self, tks):
        for t in tks:
            if t.w is not None:
                self._wait(self.sp, t.w)
            for d in t.r.values():
                self._wait(self.sp, d)


def build(NSEQ, NGRP, G, debug=()):
    T = 128 * G
    cur = {"G": G, "T": T}
    nc = bass.Bass("TRN2", target_bir_lowering=False)
    es = ExitStack()
    k = K(nc, es)
    pe, act, dve, pool, sp = k.pe, k.act, k.dve, k.pool, k.sp
    LSEQ = NGRP * T

    def din(name, shape):
        return nc.dram_tensor(name, list(shape), F32, kind="ExternalInput").ap()

    x = din("x", [NSEQ, LSEQ, D])
    meta = din("meta_tokens", [NMETA, D])
    g_mix_pre = din("norm_mix_pre", [1, D])
    w_in = din("w_in", [D, IN_W])
    conv_w = din("conv_w", [4, 4096])
    conv_b = din("conv_b", [1, 4096])
    dt_bias = din("dt_bias", [1, 32])
    a_log = din("a_log", [1, 32])
    d_skip = din("d_skip", [1, 32])
    ssd_norm = din("ssd_norm", [1, 2048])
    w_ret = din("w_ret_branch", [2048, D])
    w_ssd = din("w_ssd_branch", [2048, D])
    w_out = din("w_out", [D, D])
    g_mix_post = din("norm_mix_post", [1, D])
    g_ffn_pre = din("norm_ffn_pre", [1, D])
    w_gate = din("w_gate", [D, DFF])
    w_up = din("w_up", [D, DFF])
    w_down = din("w_down", [DFF, D])
    g_ffn_post = din("norm_ffn_post", [1, D])
    y = nc.dram_tensor("y", [NSEQ, LSEQ, D], F32, kind="ExternalOutput").ap()

    def dscr(name, shape, dt=BF16):
        return nc.dram_tensor(name, list(shape), dt, kind="Internal").ap()

    win_b = dscr("win_b", [D, IN_W])
    wret_b = dscr("wret_b", [2048, D])
    wssd_b = dscr("wssd_b", [2048, D])
    wout_b = dscr("wout_b", [D, D])
    wgate_b = dscr("wgate_b", [D, DFF])
    wup_b = dscr("wup_b", [D, DFF])
    wdown_b = dscr("wdown_b", [DFF, D])
    st_r = dscr("st_r", [128, 8 * 512], F32)
    st_s = dscr("st_s", [128, 2048], F32)
    st_h = dscr("st_h", [128, 32 * 4], BF16)
    dbg_out = {}

    def sb(name, shape, dt):
        return es.enter_context(nc.sbuf_tensor(name, list(shape), dt))

    def dump(name, ap, tks, shape=None):
        if name not in debug:
            return
        cnt = sum(1 for n in dbg_out if n.startswith(name + "#"))
        nm = f"{name}#{cnt}"
        shp = list(ap.shape)
        dt = ap.dtype
        t = nc.dram_tensor("dbg_" + nm.replace("#", "_"), shp, dt, kind="ExternalOutput").ap()
        dbg_out[nm] = t
        k.dma(sp, t, ap, tks, [dbgtk])

    dbgtk = Tk("dbg")

    ident = sb("ident", [128, 128], BF16)
    identf = sb("identf", [128, 128], F32)
    triM = sb("triM", [128, 128], F32)
    Umat = sb("Umat", [128, 128], F32)
    ones = sb("ones", [128, 128], F32)
    causT = sb("causT", [128, 128], F32)
    DT = sb("DT", [128, 4, 128], F32)
    xiT = sb("xiT", [128, 4, 128], F32)
    zeta = sb("zeta", [128, 4], F32)
    invf = sb("invf", [128, 1], F32)
    iot = sb("iot", [128, 128], I32)
    iotf = sb("iotf", [128, 128], F32)
    ipart = sb("ipart", [128, 1], I32)
    ipartf = sb("ipartf", [128, 1], F32)
    gpre = sb("gpre", [128, 8], F32)
    gfpre = sb("gfpre", [128, 8], F32)
    gpost = sb("gpost", [128, D], F32)
    gfpost = sb("gfpost", [128, D], F32)
    ssdn = sb("ssdn", [128, 16], F32)
    cw = sb("cw", [128, 32, 4], F32)
    cwraw = sb("cwraw", [128, 4, 32], F32)
    cb = sb("cb", [128, 32], F32)
    dtb = sb("dtb", [128, 32], F32)
    a_b = sb("a_b", [128, 32], F32)
    dsk = sb("dsk", [128, 16], F32)
    Dd = sb("Dd", [128, 16, 128], BF16)
    mmask = sb("mmask", [128, 1], F32)
    wdt = sb("wdt", [128, 8, 32], BF16)
    ctk = Tk("const")
    Umat_b = sb("Umat_b", [128, 128], BF16)
    triM_b = sb("triM_b", [128, 128], BF16)
    dta_hi = sb("dta_hi", [128, G, 32], BF16)
    dta_lo = sb("dta_lo", [128, G, 32], BF16)

    def c_op(eng, fn, extra_r=(), extra_w=()):
        k.op(eng, [ctk] + list(extra_r), [ctk] + list(extra_w), fn)

    c_op(pool, lambda e: e.memset(identf[:], 0.0))
    c_op(pool, lambda e: e.affine_select(out=identf[:], in_=identf[:], compare_op=ALU.not_equal, fill=1.0,
                                         base=0, pattern=[[-1, 128]], channel_multiplier=1))
    c_op(dve, lambda e: e.tensor_copy(ident[:], identf[:]))
    c_op(pool, lambda e: e.memset(triM[:], 1.0))
    c_op(pool, lambda e: e.affine_select(out=triM[:], in_=triM[:], compare_op=ALU.is_ge, fill=0.0,
                                         base=0, pattern=[[1, 128]], channel_multiplier=-1))
    c_op(dve, lambda e: e.tensor_copy(causT[:], triM[:]))
    c_op(pool, lambda e: e.memset(Umat[:], 1.0))
    c_op(pool, lambda e: e.affine_select(out=Umat[:], in_=Umat[:], compare_op=ALU.is_gt, fill=0.0,
                                         base=0, pattern=[[-1, 128]], channel_multiplier=1))
    c_op(pool, lambda e: e.memset(ones[:], 1.0))
    c_op(dve, lambda e: e.tensor_copy(Umat_b[:], Umat[:]))
    c_op(dve, lambda e: e.tensor_copy(triM_b[:], triM[:]))
    c_op(pool, lambda e: e.iota(iot[:], pattern=[[1, 128]], base=0, channel_multiplier=-1))
    c_op(dve, lambda e: e.tensor_copy(iotf[:], iot[:]))
    c_op(pool, lambda e: e.iota(ipart[:], pattern=[[0, 1]], base=0, channel_multiplier=1))
    c_op(dve, lambda e: e.tensor_copy(ipartf[:], ipart[:]))
    for h in range(4):
        c_op(act, lambda e, h=h: e.activation(out=DT[:, h, :], in_=iotf[:], func=AF.Exp, scale=LOGG[h]))
        c_op(dve, lambda e, h=h: e.scalar_tensor_tensor(DT[:, h, :], DT[:, h, :], 1.0 / 16, causT[:],
                                                        ALU.mult, ALU.mult))
    tix = sb("tix", [128, 128], I32)
    tixf = sb("tixf", [128, 128], F32)
    c_op(pool, lambda e: e.iota(tix[:], pattern=[[1, 128]], base=1, channel_multiplier=0))
    c_op(dve, lambda e: e.tensor_copy(tixf[:], tix[:]))
    for h in range(4):
        c_op(act, lambda e, h=h: e.activation(out=xiT[:, h, :], in_=tixf[:], func=AF.Exp, scale=LOGG[h]))
        c_op(act, lambda e, h=h: e.activation(out=zeta[:, h:h + 1], in_=ipartf[:], func=AF.Exp, scale=-LOGG[h]))
        c_op(dve, lambda e, h=h: e.tensor_scalar(zeta[:, h:h + 1], zeta[:, h:h + 1],
                                                 math.exp(127 * LOGG[h]) / 16.0, None, ALU.mult))
    c_op(act, lambda e: e.activation(out=invf[:], in_=ipartf[:], func=AF.Exp, scale=-math.log(10000.0) / 128))
    invf2 = sb("invf2", [128, 1], F32)
    c_op(dve, lambda e: e.tensor_scalar(invf2[:], invf[:], 1.0 / (2 * math.pi), None, ALU.mult))
    c_op(pool, lambda e: e.memset(mmask[:], 1.0))
    c_op(pool, lambda e: e.affine_select(out=mmask[:], in_=mmask[:], compare_op=ALU.is_ge, fill=0.0,
                                         base=-(128 - NMETA), pattern=[[0, 1]], channel_multiplier=1))
    k.dma(sp, gpre[:], g_mix_pre[0].rearrange("(k p) -> p k", p=128), [], [ctk], allow_slow_non_contiguous=True)
    k.dma(sp, gfpre[:], g_ffn_pre[0].rearrange("(k p) -> p k", p=128), [], [ctk], allow_slow_non_contiguous=True)
    k.dma(sp, ssdn[:], ssd_norm[0].rearrange("(k p) -> p k", p=128), [], [ctk], allow_slow_non_contiguous=True)
    k.dma(sp, cb[:], conv_b[0].rearrange("(k p) -> p k", p=128), [], [ctk], allow_slow_non_contiguous=True)
    for j in range(4):
        k.dma(sp, cwraw[:, j, :], conv_w[j].rearrange("(k p) -> p k", p=128), [], [ctk],
              allow_slow_non_contiguous=True)
    k.dma(sp, gpost[:], g_mix_post[0].partition_broadcast(128), [], [ctk])
    k.dma(sp, gfpost[:], g_ffn_post[0].partition_broadcast(128), [], [ctk])
    k.dma(sp, dtb[:], dt_bias[0].partition_broadcast(128), [], [ctk])
    k.dma(sp, a_b[:], a_log[0].partition_broadcast(128), [], [ctk])
    for hh in range(2):
        k.dma(sp, dsk[hh * 64:(hh + 1) * 64, :], d_skip[0].rearrange("(b two) -> two b", two=2)[hh]
              .partition_broadcast(64), [], [ctk], allow_slow_non_contiguous=True)
    c_op(dve, lambda e: e.tensor_copy(cw[:].rearrange("p b j -> p j b"), cwraw[:]))
    c_op(act, lambda e: e.activation(out=a_b[:], in_=a_b[:], func=AF.Exp))
    c_op(dve, lambda e: e.tensor_scalar(a_b[:], a_b[:], -1.0, None, ALU.mult))
    for b in range(16):
        c_op(dve, lambda e, b=b: e.tensor_scalar(Dd[:, b, :], identf[:], dsk[:, b:b + 1], None, ALU.mult))

    wcast = {}

    def cast(dst, src, rows, cols):
        RB = 256
        lst = []
        for r0 in range(0, rows, RB):
            r1 = min(rows, r0 + RB)
            t = Tk(f"wc{len(wcast)}_{r0}")
            k.dma(pool, dst[r0:r1, :], src[r0:r1, :], [], [t])
            lst.append(t)
        wcast[dst.tensor.name] = lst

    wcast_cols = []

    def cast_cols(c0, c1, d0=None):
        d0 = c0 if d0 is None else d0
        t = Tk(f"wcc{d0}")
        k.dma(pool, win_b[:, d0:d0 + (c1 - c0)], w_in[:, c0:c1], [ctk] if not wcast_cols else [], [t])
        wcast_cols.append((d0, d0 + (c1 - c0), t))

    for h in range(4):
        cast_cols(C_Q + h * 256, C_Q + (h + 1) * 256, h * 512)
        cast_cols(C_K + h * 256, C_K + (h + 1) * 256, h * 512 + 256)
        cast_cols(C_V + h * 512, C_V + (h + 1) * 512)
        cast_cols(C_G + h * 512, C_G + (h + 1) * 512)
    cast_cols(C_DT, C_DT + 32)
    for g in range(8):
        cast_cols(C_XBC + g * 256, C_XBC + (g + 1) * 256, C_XBC + g * 512)
        cast_cols(C_XBC + 2048 + g * 128, C_XBC + 2048 + (g + 1) * 128, C_XBC + g * 512 + 256)
        cast_cols(C_XBC + 3072 + g * 128, C_XBC + 3072 + (g + 1) * 128, C_XBC + g * 512 + 384)
    cast_cols(C_Z, C_Z + 2048)
    cast_cols(C_GATE, C_GATE + 2048)

    def wdeps(src):
        if src.tensor.name == win_b.tensor.name:
            c0 = int(src.offset) % IN_W
            c1 = c0 + src.shape[1]
            return [t for (a_, b_, t) in wcast_cols if a_ < c1 and b_ > c0]
        return wcast[src.tensor.name]
    cast(wret_b, w_ret, 2048, D)
    cast(wssd_b, w_ssd, 2048, D)
    cast(wout_b, w_out, D, D)
    cast(wgate_b, w_gate, D, DFF)
    cast(wup_b, w_up, D, DFF)
    cast(wdown_b, w_down, DFF, D)
    wdt_tk = Tk("wdt")
    wdt_loaded = [False]

    def load_wdt():
        if not wdt_loaded[0]:
            wdt_loaded[0] = True
            k.dma(sp, wdt[:], win_b[:, C_DT:C_DT + 32].rearrange("(k p) n -> p k n", p=128),
                  wdeps(win_b[:, C_DT:C_DT + 32]), [wdt_tk])

    rstate_f = sb("rstate_f", [128, 8, 512], F32)
    rstate_b = sb("rstate_b", [128, 8, 512], BF16)
    sstate_f = sb("sstate_f", [128, 2048], F32)
    sstate_b = sb("sstate_b", [128, 2048], BF16)
    halo = sb("halo", [128, 32, 4], BF16)
    rs_tk = [Tk(f"rs{i}") for i in range(8)]
    rsb_tk = [Tk(f"rsb{i}") for i in range(8)]
    ss_tk = [Tk(f"ss{i}") for i in range(8)]
    ssb_tk = [Tk(f"ssb{i}") for i in range(8)]
    halo_tk = [Tk(f"halo{i}") for i in range(32)]

    wslot = [sb(f"wslot{i}", [128, 8, 512], BF16) for i in range(NW)]
    wslot_tk = [Tk(f"wslot{i}") for i in range(NW)]
    wctr = [0]

    def wload(pieces):
        i = wctr[0] % NW
        wctr[0] += 1
        s, tk = wslot[i], wslot_tk[i]
        for src, off in pieces:
            nk = src.shape[0] // 128
            ncol = src.shape[1]
            k.dma(sp, s[:, 0:nk, off:off + ncol], src.rearrange("(k p) n -> p k n", p=128), wdeps(src), [tk])
        return s, tk

    hbuf = sb("hbuf", [128, G, D], F32)
    h_tk = [Tk(f"h{c}") for c in range(G)]
    uT = sb("uT", [128, 8, T], BF16)
    uT_tk = [Tk(f"uT{c}") for c in range(G)]
    un2 = [sb("un0", [128, D], BF16), sb("un1", [128, D], BF16)]
    un2_tk = [Tk("un0"), Tk("un1")]
    un, un_tk = un2[0], un2_tk[0]
    junk = sb("junk", [128, D], BF16)
    junk_tk = Tk("junk")
    stat = sb("stat", [128, 64], F32)
    stat_tk = Tk("stat")
    stat_slots = {}

    def stk_(col):
        if col not in stat_slots:
            stat_slots[col] = Tk(f"stat{col}")
        return stat_slots[col]
    tmpA = sb("tmpA", [128, 512], F32)
    tmpB = sb("tmpB", [128, 512], F32)
    rot_s = [(tmpA, Tk("tmpA")), (tmpB, Tk("tmpB"))]
    cosT = sb("cosT", [128, T], F32)
    sinT = sb("sinT", [128, T], F32)
    rope_tk = Tk("rope")
    retT = sb("retT", [128, 16, T], BF16)
    retT_tk = [[Tk(f"retT{b}_{c}") for c in range(G)] for b in range(16)]
    xn = retT[:].rearrange("p a b -> p (a b)").bitcast(F32).rearrange("p (c d) -> p c d", c=G)
    xn_tks = [t for l in retT_tk for t in l]
    xn_tkl = [xn_tks for _ in range(G)]
    merged = sb("merged", [128, G, D], F32)
    mg_tk = [Tk(f"mg{c}") for c in range(G)]
    sgate = sb("sgate", [128, G, D], BF16)
    sgate_tk = [Tk(f"sgate{c}") for c in range(G)]
    dtt = sb("dtt", [128, G, 32], F32)
    dta = sb("dta", [128, G, 32], F32)
    csb = sb("csb", [128, G, 32], F32)
    edec = sb("edec", [128, G, 32], F32)
    escs = sb("escs", [128, G, 32], F32)
    dchk = sb("dchk", [128, G, 32], F32)
    dtw = sb("dtw", [128, G, 32], F32)
    dt_tk = Tk("dt")

    XB = 40 * 1024
    regX = sb("regX", [128, XB // 2], BF16)

    def carve(off, shape, dt):
        n = int(np.prod(shape[1:]))
        bpe = 4 if dt in (F32, I32) else 2
        assert off % 4 == 0
        a = regX[:, off // 2: off // 2 + n * bpe // 2]
        if dt != BF16:
            a = a.bitcast(dt)
        if len(shape) == 3:
            a = a.rearrange("p (a b) -> p a b", a=shape[1])
        return a, off + n * bpe

    RH = []
    off = 0
    for i in range(2):
        d = {}
        d["qT"], off = carve(off, [128, 2, T], BF16)
        d["kT"], off = carve(off, [128, 2, T], BF16)
        d["qxT"], off = carve(off, [128, 2, T], BF16)
        d["kz"], off = carve(off, [128, G, 256], BF16)
        d["v"], off = carve(off, [128, G, 512], BF16)
        d["sg"], off = carve(off, [128, G, 512], BF16)
        d["tk"] = {n: Tk(f"rh{i}{n}") for n in ("qT", "kT", "qxT")}
        d["tkc"] = {n: [Tk(f"rh{i}{n}{c}") for c in range(G)] for n in ("kz", "v", "sg")}
        RH.append(d)
    rh_end = off
    rot = [(tmpA[:, 0:T], rot_s[0][1]), (tmpB[:, 0:T], rot_s[1][1])]
    for i in range(2):
        a, off = carve(off, [128, T], F32)
        rot.append((a, Tk(f"rot{i}")))
    STb = []
    for i in range(2):
        a, off = carve(off, [128, 128], BF16)
        STb.append((a, Tk(f"ST{i}")))
    retg = []
    for i in range(2):
        a, off = carve(off, [128, 512], BF16)
        retg.append((a, Tk(f"retg{i}")))
    assert off <= XB, off
    SG = []
    off = 0
    for i in range(2):
        d = {}
        d["xsT"], off = carve(off, [128, 2, T], BF16)
        d["bmT"], off = carve(off, [128, T], BF16)
        d["cmT"], off = carve(off, [128, T], BF16)
        d["xdt"], off = carve(off, [128, G, 256], BF16)
        d["xdec"], off = carve(off, [128, G, 256], BF16)
        d["bm"], off = carve(off, [128, G, 128], BF16)
        d["sz"], off = carve(off, [128, G, 256], BF16)
        d["tk"] = {n: Tk(f"sg{i}{n}") for n in ("xsT0", "xsT1", "bmT", "cmT", "sz")}
        d["tkc"] = {n: [Tk(f"sg{i}{n}{c}") for c in range(G)] for n in ("xdt", "xdec", "bm")}
        SG.append(d)
    xpre = []
    for i in range(3):
        a, off = carve(off, [128, T + 4], BF16)
        xpre.append((a, Tk(f"xpre{i}")))
    cdiag = []
    for i in range(2):
        a, off = carve(off, [128, 4, 128], BF16)
        cdiag.append((a, Tk(f"cdiag{i}")))
    sst = []
    for i in range(2):
        d = {}
        d["R"], off = carve(off, [128, 4, 128], F32)
        d["E"] = d["R"]
        d["CBm"], off = carve(off, [128, 128], F32)
        d["MT"], off = carve(off, [128, 4, 128], BF16)
        d["yc"], off = carve(off, [128, 256], F32)
        d["yg"], off = carve(off, [128, 256], F32)
        d["yn"], off = carve(off, [128, 256], BF16)
        d["tk"] = {n: Tk(f"sst{i}{n}") for n in ("R", "CBm", "MT", "yc", "yg", "yn")}
        d["tk"]["E"] = d["tk"]["R"]
        sst.append(d)
    assert off <= XB, off
    off = 0
    hidT, off = carve(off, [128, 22, T], BF16)
    hid_tk = [Tk(f"hid{b}") for b in range(22)]
    sgt = []
    for i in range(2):
        a, off = carve(off, [128, T], F32)
        sgt.append((a, Tk(f"sgt{i}")))
    assert off <= XB, off

    def all_tks_ret():
        r = []
        for d in RH:
            r += list(d["tk"].values()) + [t for l in d["tkc"].values() for t in l]
        return r + [t for _, t in rot] + [t for _, t in STb] + [t for _, t in retg]

    def all_tks_ssd():
        r = []
        for d in SG:
            r += list(d["tk"].values()) + [t for l in d["tkc"].values() for t in l]
        for d in sst:
            r += list(d["tk"].values())
        return r + [t for _, t in xpre] + [t for _, t in cdiag]

    def all_tks_ffn():
        return hid_tk + [t for _, t in sgt]

    psum = [es.enter_context(nc.psum_tensor(f"ps{i}", [128, 512], F32)) for i in range(8)]
    ps_tk = [Tk(f"ps{i}", rw=True) for i in range(8)]
    pctr = {"a": 0, "b": 0, "p2": 0, "p4": 0, "p3": 0}
    pools = {"a": [0, 1, 2, 3], "b": [4, 5, 6, 7], "p2": [0, 1], "p4": [0, 1, 2, 3], "p3": [0, 1, 2]}
    pmode = {"proj": "a"}

    def bank(pool_name="a"):
        if pool_name == "proj":
            pool_name = pmode["proj"]
        lst = pools[pool_name]
        i = lst[pctr[pool_name] % len(lst)]
        pctr[pool_name] += 1
        return psum[i], ps_tk[i]

    def mm(out, lhsT, rhs, reads, ptk, start, stop):
        k.op(pe, reads, [ptk], lambda e: e.matmul(out, lhsT, rhs, start=start, stop=stop), signal=stop)

    def tr(out, in_, reads, ptk, last):
        idn = ident if in_.dtype == BF16 else identf
        k.op(pe, reads, [ptk], lambda e: e.transpose(out, in_, idn[:]), signal=last)

    def cs(c):
        return slice(c * 128, (c + 1) * 128)

    def rstd(col_in, col_out, n, scale, tk=None):
        tk = tk or stat_tk
        k.op(act, [tk], [tk], lambda e: e.activation(
            out=stat[:, col_out:col_out + n], in_=stat[:, col_in:col_in + n], func=AF.Ln, scale=scale, bias=epsb[:]))
        k.op(act, [tk], [tk], lambda e: e.activation(
            out=stat[:, col_out:col_out + n], in_=stat[:, col_out:col_out + n], func=AF.Exp, scale=-0.5))

    epsb = sb("epsb", [128, 1], F32)
    oneb = sb("oneb", [128, 1], F32)
    c_op(pool, lambda e: e.memset(epsb[:], EPS))
    c_op(pool, lambda e: e.memset(oneb[:], 1.0))

    def _l(t):
        return list(t) if isinstance(t, list) else [t]

    def prenorm_stats(src, src_tks, slot=0):
        k.tag = "prenorm"
        Ga = cur["G"]
        for c in range(Ga):
            k.op(act, _l(src_tks[c]), [stk_(slot)], lambda e, c=c: e.activation(
                out=junk[:], in_=src[:, c, :], func=AF.Square, accum_out=stat[:, slot + c:slot + c + 1]))
        rstd(slot, slot + 4, Ga, 1.0 / D, stk_(slot))

    def prenorm_apply(gvec, src, src_tks, c, slot=0, on_dve=False):
        k.tag = "prenorm"
        un, un_tk = un2[c % 2], un2_tk[c % 2]
        if on_dve:
            k.op(dve, _l(src_tks[c]) + [stk_(slot)], [un_tk], lambda e, c=c, un=un: e.tensor_scalar(
                un[:], src[:, c, :], stat[:, slot + 4 + c:slot + 5 + c], None, ALU.mult))
        else:
            k.op(act, _l(src_tks[c]) + [stk_(slot)], [un_tk], lambda e, c=c, un=un: e.activation(
                out=un[:], in_=src[:, c, :], func=AF.Copy, scale=stat[:, slot + 4 + c:slot + 5 + c]))
        p, ptk = bank("a")
        pb = p[:].bitcast(BF16)
        for kc in range(8):
            tr(pb[:, kc * 128:(kc + 1) * 128], un[:, kc * 128:(kc + 1) * 128], [un_tk], ptk, kc == 7)
        k.op(dve, [ptk, ctk], [uT_tk[c]], lambda e, c=c, pb=pb: e.tensor_tensor(
            uT[:, :, cs(c)], pb.rearrange("p (a b) -> p a b", a=8),
            gvec[:, :].unsqueeze(2).to_broadcast([128, 8, 128]), ALU.mult))

    def prenorm_uT(gvec, src=None, src_tks=None):
        if src is None:
            src, src_tks = hbuf, h_tk
        prenorm_stats(src, src_tks, 0)
        for c in range(cur["G"]):
            prenorm_apply(gvec, src, src_tks, c, 0)

    def rope_tables(pos0):
        k.tag = "rope"
        Ta = cur["T"]
        posi = tmpA[:, 0:Ta].bitcast(I32)
        posf = tmpB[:, 0:Ta]
        tA, tB = rot_s[0][1], rot_s[1][1]
        k.op(pool, [], [tA], lambda e: e.iota(posi, pattern=[[1, Ta]], base=pos0, channel_multiplier=0))
        k.op(dve, [tA], [tB], lambda e: e.tensor_copy(posf, posi))
        posr = rot[2][0][:, 0:Ta]
        posn = rot[3][0][:, 0:Ta]
        tR, tN = rot[2][1], rot[3][1]
        for tab, off_turn in ((sinT, 0.0), (cosT, 0.25)):
            k.op(dve, [tB, ctk], [rope_tk], lambda e, tab=tab, off_turn=off_turn: e.tensor_scalar(
                tab[:, 0:Ta], posf, invf2[:, 0:1], off_turn + 32.0, ALU.mult, ALU.add))
            k.op(dve, [rope_tk], [tR], lambda e, tab=tab: e.tensor_copy(posr.bitcast(I32), tab[:, 0:Ta]))
            k.op(dve, [tR], [tN], lambda e: e.tensor_copy(posn, posr.bitcast(I32)))
            k.op(dve, [rope_tk, tN], [rope_tk], lambda e, tab=tab: e.tensor_tensor(
                tab[:, 0:Ta], tab[:, 0:Ta], posn, ALU.subtract))
            k.op(dve, [rope_tk], [tN], lambda e, tab=tab: e.tensor_scalar(
                posn, tab[:, 0:Ta], 0.5, None, ALU.is_gt))
            k.op(dve, [rope_tk, tN], [rope_tk], lambda e, tab=tab: e.tensor_tensor(
                tab[:, 0:Ta], tab[:, 0:Ta], posn, ALU.subtract))
            k.op(act, [rope_tk], [rope_tk], lambda e, tab=tab: e.activation(
                out=tab[:, 0:Ta], in_=tab[:, 0:Ta], func=AF.Sin, scale=2 * math.pi))

    def proj_fm(slot, stk, col, nblk_list, evac):
        Ta = cur["T"]
        for idx, co in nblk_list:
            p, ptk = bank("proj")
            for kc in range(8):
                mm(p[:, 0:Ta], slot[:, kc, co:co + 128], uT[:, kc, 0:Ta], [stk] + uT_tk, ptk, kc == 0, kc == 7)
            evac(p, ptk, idx)

    def proj_tm(slot, stk, co, ncol, c, evac):
        p, ptk = bank("proj")
        for kc in range(8):
            mm(p[:, 0:ncol], uT[:, kc, cs(c)], slot[:, kc, co:co + ncol], [stk, uT_tk[c]], ptk, kc == 0, kc == 7)
        evac(p, ptk)

    def rotary(pa, patk, pb_, pbtk, out, otk):
        Ta = cur["T"]
        (m1, t1), (m2, t2), (m3, t3), (m4, t4) = rot
        k.op(dve, [patk, rope_tk], [t1], lambda e: e.tensor_tensor(m1[:, 0:Ta], pa[:, 0:Ta], cosT[:, 0:Ta], ALU.mult))
        k.op(dve, [pbtk, rope_tk], [t2], lambda e: e.tensor_tensor(m2[:, 0:Ta], pb_[:, 0:Ta], sinT[:, 0:Ta], ALU.mult))
        k.op(dve, [patk, rope_tk], [t3], lambda e: e.tensor_tensor(m3[:, 0:Ta], pa[:, 0:Ta], sinT[:, 0:Ta], ALU.mult))
        k.op(dve, [pbtk, rope_tk], [t4], lambda e: e.tensor_tensor(m4[:, 0:Ta], pb_[:, 0:Ta], cosT[:, 0:Ta], ALU.mult))
        k.op(pool, [t1, t2], [otk], lambda e: e.tensor_tensor(out[:, 0, 0:Ta], m1[:, 0:Ta], m2[:, 0:Ta], ALU.subtract))
        k.op(pool, [t3, t4], [otk], lambda e: e.tensor_tensor(out[:, 1, 0:Ta], m3[:, 0:Ta], m4[:, 0:Ta], ALU.add))

    def ret_proj(h, d):
        k.tag = "ret_proj"
        Ga, Ta = cur["G"], cur["T"]
        s3, t3 = wload([(win_b[:, C_G + h * 512:C_G + (h + 1) * 512], 0)])
        s1, t1 = wload([(win_b[:, h * 512:(h + 1) * 512], 0)])
        s2, t2 = wload([(win_b[:, C_V + h * 512:C_V + (h + 1) * 512], 0)])
        held = {}

        def ev(p, ptk, idx):
            held[idx] = (p, ptk)

        for c in range(Ga):
            proj_tm(s3, t3, 0, 512, c, lambda p, ptk, c=c: k.op(
                act, [ptk], [d["tkc"]["sg"][c]], lambda e: e.activation(out=d["sg"][:, c, :], in_=p[:, :], func=AF.Silu)))
        proj_fm(s1, t1, 0, [(0, 0), (1, 128)], ev)
        rotary(held[0][0], held[0][1], held[1][0], held[1][1], d["qT"], d["tk"]["qT"])
        for j in range(2):
            k.op(pool, [d["tk"]["qT"], ctk], [d["tk"]["qxT"]], lambda e, j=j: e.tensor_tensor(
                d["qxT"][:, j, 0:Ta].rearrange("p (c t) -> p c t", t=128),
                d["qT"][:, j, 0:Ta].rearrange("p (c t) -> p c t", t=128),
                xiT[:, h, :].unsqueeze(1).to_broadcast([128, Ga, 128]), ALU.mult))
        proj_fm(s1, t1, 0, [(2, 256), (3, 384)], ev)
        rotary(held[2][0], held[2][1], held[3][0], held[3][1], d["kT"], d["tk"]["kT"])
        for c in range(Ga):
            proj_tm(s2, t2, 0, 512, c, lambda p, ptk, c=c: k.op(
                act, [ptk], [d["tkc"]["v"][c]], lambda e: e.activation(out=d["v"][:, c, :], in_=p[:, :], func=AF.Copy)))

    def ret_projB(h, d):
        k.tag = "ret_proj"
        Ga = cur["G"]
        for c in range(Ga):
            p, ptk = bank("proj")
            pb = p[:].bitcast(BF16)
            for j in range(2):
                tr(pb[:, j * 128:(j + 1) * 128], d["kT"][:, j, cs(c)], [d["tk"]["kT"]], ptk, j == 1)
            k.op(dve, [ptk, ctk], [d["tkc"]["kz"][c]], lambda e, c=c, pb=pb: e.tensor_scalar(
                d["kz"][:, c, :], pb[:, 0:256], zeta[:, h:h + 1], None, ALU.mult))

    def ret_R1(i, h, c, d):
        k.tag = "ret_chunk"
        X, Xtk = psum[2 + i % 2], ps_tk[2 + i % 2]
        for j in range(2):
            mm(X[:, 0:128], d["kT"][:, j, cs(c)], d["qT"][:, j, cs(c)], [d["tk"]["qT"], d["tk"]["kT"]], Xtk, j == 0, j == 1)
        ST, sttk = STb[i % 2]
        k.op(dve, [Xtk, ctk], [sttk], lambda e: e.tensor_tensor(ST[:, :], X[:, 0:128], DT[:, h, :], ALU.mult))

    def ret_Ru(i, h, c, d):
        k.tag = "ret_chunk"
        gch = math.exp(128 * LOGG[h])
        for j in range(2):
            U, Utk = psum[6 + j], ps_tk[6 + j]
            mm(U[:, :], d["kz"][:, c, j * 128:(j + 1) * 128], d["v"][:, c, :],
               [d["tkc"]["kz"][c], d["tkc"]["v"][c]], Utk, True, True)
        return gch

    def ret_Ru2(i, h, c, d, gch):
        k.tag = "ret_chunk"
        for j in range(2):
            U, Utk = psum[6 + j], ps_tk[6 + j]
            ii = h * 2 + j
            k.op(dve, [Utk, rs_tk[ii]], [rs_tk[ii]], lambda e, ii=ii, U=U: e.scalar_tensor_tensor(
                rstate_f[:, ii, :], rstate_f[:, ii, :], gch, U[:, :], ALU.mult, ALU.add))
            k.op(act, [rs_tk[ii]], [rsb_tk[ii]], lambda e, ii=ii: e.activation(
                out=rstate_b[:, ii, :], in_=rstate_f[:, ii, :], func=AF.Copy))

    def ret_R2o(i, h, c, d):
        k.tag = "ret_chunk"
        O, Otk = psum[4 + i % 2], ps_tk[4 + i % 2]
        ST, sttk = STb[i % 2]
        for j in range(2):
            mm(O[:, :], d["qxT"][:, j, cs(c)], rstate_b[:, h * 2 + j, :], [d["tk"]["qxT"], rsb_tk[h * 2 + j]], Otk, j == 0, False)
        mm(O[:, :], ST[:, :], d["v"][:, c, :], [sttk, d["tkc"]["v"][c]], Otk, False, True)

    def ret_R2a(i, h, c, d):
        k.tag = "ret_chunk"
        O, Otk = psum[4 + i % 2], ps_tk[4 + i % 2]
        scol = 16 + 2 * (i % 8)
        stt = stk_(scol)
        k.op(act, [Otk], [stt], lambda e: e.activation(
            out=junk[:, 0:512], in_=O[:, :], func=AF.Square, accum_out=stat[:, scol:scol + 1]))
        rstd(scol, scol + 1, 1, 1.0 / 512, stt)

    def ret_R2(i, h, c, d):
        k.tag = "ret_chunk"
        O, Otk = psum[4 + i % 2], ps_tk[4 + i % 2]
        scol = 16 + 2 * (i % 8)
        stt = stk_(scol)
        rg, rgtk = retg[i % 2]
        k.op(dve, [Otk, stt, d["tkc"]["sg"][c]], [rgtk], lambda e: e.scalar_tensor_tensor(
            rg[:, :], O[:, :], stat[:, scol + 1:scol + 2], d["sg"][:, c, :], ALU.mult, ALU.mult))

    def ret_R3(i, h, c, d):
        k.tag = "ret_chunk"
        X, Xtk = psum[2 + i % 2], ps_tk[2 + i % 2]
        rg, rgtk = retg[i % 2]
        pb = X[:].bitcast(BF16)
        for e4 in range(4):
            tr(pb[:, 512 + e4 * 128:512 + (e4 + 1) * 128], rg[:, e4 * 128:(e4 + 1) * 128], [rgtk], Xtk, e4 == 3)
        k.op(dve, [Xtk], [retT_tk[h * 4 + e4][c] for e4 in range(4)], lambda e: e.tensor_copy(
            retT[:, h * 4:(h + 1) * 4, cs(c)], pb[:, 512:1024].rearrange("p (a b) -> p a b", a=4)))

    def retention():
        Ga = cur["G"]
        pmode["proj"] = "p2"
        iters = [(h, c) for h in range(4) for c in range(Ga)]
        N = len(iters)
        done = set()
        doneB = set()

        def ensure(h):
            if h < 4 and h not in done:
                done.add(h)
                ret_proj(h, RH[h % 2])

        def ensureB(h):
            ensure(h)
            if h < 4 and h not in doneB:
                doneB.add(h)
                ret_projB(h, RH[h % 2])

        ensure(0)

        def A(i):
            return (i, iters[i][0], iters[i][1], RH[iters[i][0] % 2])

        for r in range(-1, N + 2):
            if 0 <= r - 1 < N:
                ret_R2a(*A(r - 1))
            if 0 <= r < N:
                ensureB(iters[r][0])
                gch = ret_Ru(*A(r))
                ret_R2o(*A(r))
                ret_Ru2(*A(r), gch)
            if 0 <= r - 1 < N:
                ret_R2(*A(r - 1))
            if 0 <= r + 1 < N:
                ensure(iters[r + 1][0])
                ret_R1(*A(r + 1))
            if 0 <= r - 2 < N:
                ret_R3(*A(r - 2))
            if 0 <= r < N and iters[r][1] == 0:
                ensure(iters[r][0] + 1)
            if 0 <= r < N and iters[r][1] == min(2, Ga - 1):
                ensureB(iters[r][0] + 1)
        pmode["proj"] = "a"

    def gates_proj(col0):
        k.tag = "gates"
        for nt in range(2):
            s, stk = wload([(win_b[:, col0 + nt * 512:col0 + (nt + 1) * 512], 0)])
            for c in range(cur["G"]):
                proj_tm(s, stk, 0, 512, c, lambda p, ptk, c=c, nt=nt: k.op(
                    act, [ptk], [sgate_tk[c]], lambda e: e.activation(
                        out=sgate[:, c, nt * 512:(nt + 1) * 512], in_=p[:, :], func=AF.Sigmoid)))

    def branch(wsrc, first):
        k.tag = "branch"
        Ga = cur["G"]
        for nt in range(2):
            banks = [bank("a") for _ in range(Ga)]
            for half in range(2):
                s, stk = wload([(wsrc[half * 1024:(half + 1) * 1024, nt * 512:(nt + 1) * 512], 0)])
                for c in range(Ga):
                    p, ptk = banks[c]
                    for kb in range(8):
                        b = half * 8 + kb
                        mm(p[:, :], retT[:, b, cs(c)], s[:, kb, :], [stk, retT_tk[b][c]], ptk,
                           half == 0 and kb == 0, half == 1 and kb == 7)
            for c in range(Ga):
                p, ptk = banks[c]
                msl = merged[:, c, nt * 512:(nt + 1) * 512]
                gsl = sgate[:, c, nt * 512:(nt + 1) * 512]
                if first:
                    k.op(dve, [ptk, sgate_tk[c]], [mg_tk[c]], lambda e, p=p, msl=msl, gsl=gsl: e.tensor_tensor(
                        msl, p[:, :], gsl, ALU.mult))
                else:
                    (m1, t1) = rot_s[c % 2]
                    k.op(dve, [ptk, sgate_tk[c]], [t1], lambda e, p=p, m1=m1, gsl=gsl: e.tensor_tensor(
                        m1[:, 0:512], p[:, :], gsl, ALU.mult))
                    k.op(dve, [t1, mg_tk[c]], [mg_tk[c]], lambda e, m1=m1, msl=msl: e.tensor_tensor(
                        msl, msl, m1[:, 0:512], ALU.add))

    def ssd_dt(is_meta):
        k.tag = "ssd_dt"
        Ga = cur["G"]
        G32 = Ga * 32
        load_wdt()
        p, ptk = bank("a")
        for c in range(Ga):
            for kc in range(8):
                mm(p[:, c * 32:(c + 1) * 32], uT[:, kc, cs(c)], wdt[:, kc, :], [wdt_tk, uT_tk[c]], ptk, kc == 0, kc == 7)
        pv = p[:, 0:G32].rearrange("p (c n) -> p c n", c=Ga)
        k.op(dve, [ptk, ctk], [dt_tk], lambda e: e.tensor_tensor(
            dtw[:, 0:Ga, :], pv, dtb[:, :].unsqueeze(1).to_broadcast([128, Ga, 32]), ALU.add))
        k.op(act, [dt_tk], [dt_tk], lambda e: e.activation(out=dtw[:, 0:Ga, :], in_=dtw[:, 0:Ga, :], func=AF.Exp))
        k.op(act, [dt_tk, ctk], [dt_tk], lambda e: e.activation(
            out=dtt[:, 0:Ga, :], in_=dtw[:, 0:Ga, :], func=AF.Ln, bias=oneb[:]))
        if is_meta:
            k.op(dve, [dt_tk, ctk], [dt_tk], lambda e: e.tensor_scalar(
                dtt[:, 0:Ga, :], dtt[:, 0:Ga, :], mmask[:, 0:1], None, ALU.mult))
        k.op(dve, [dt_tk, ctk], [dt_tk], lambda e: e.tensor_tensor(
            dta[:, 0:Ga, :], dtt[:, 0:Ga, :], a_b[:, :].unsqueeze(1).to_broadcast([128, Ga, 32]), ALU.mult))
        k.op(dve, [dt_tk], [dt_tk], lambda e: e.tensor_copy(dta_hi[:, 0:Ga, :], dta[:, 0:Ga, :]))
        k.op(dve, [dt_tk], [dt_tk], lambda e: e.tensor_copy(dtw[:, 0:Ga, :], dta_hi[:, 0:Ga, :]))
        k.op(dve, [dt_tk], [dt_tk], lambda e: e.tensor_tensor(dtw[:, 0:Ga, :], dta[:, 0:Ga, :], dtw[:, 0:Ga, :], ALU.subtract))
        k.op(dve, [dt_tk], [dt_tk], lambda e: e.tensor_copy(dta_lo[:, 0:Ga, :], dtw[:, 0:Ga, :]))
        p1, p1tk = bank("a")
        flat = lambda t: t[:, 0:Ga, :].rearrange("p c n -> p (c n)")
        mm(p1[:, 0:G32], triM[:], flat(dta), [ctk, dt_tk], p1tk, True, True)
        p2, p2tk = bank("a")
        mm(p2[:, 0:G32], ones[:], flat(dta), [ctk, dt_tk], p2tk, True, True)
        k.op(act, [p1tk], [dt_tk], lambda e: e.activation(out=flat(csb), in_=p1[:, 0:G32], func=AF.Copy))
        k.op(act, [p1tk], [dt_tk], lambda e: e.activation(out=flat(escs), in_=p1[:, 0:G32], func=AF.Exp))
        k.op(act, [p2tk], [dt_tk], lambda e: e.activation(out=flat(dchk), in_=p2[:, 0:G32], func=AF.Exp))
        k.op(dve, [p2tk, dt_tk], [dt_tk], lambda e: e.tensor_tensor(flat(edec), p2[:, 0:G32], flat(csb), ALU.subtract))
        k.op(act, [dt_tk], [dt_tk], lambda e: e.activation(out=flat(edec), in_=flat(edec), func=AF.Exp))
        dump("dtt", flat(dtt), [dt_tk])
        dump("csb", flat(csb), [dt_tk])

    def conv_in(p, ptk, blk, first_grp, ci):
        k.tag = "conv"
        T = cur["T"]
        xp, xptk = xpre[ci % 3]
        if first_grp:
            k.op(pool, [], [xptk], lambda e: e.memset(xp[:, 0:4], 0.0))
        else:
            k.op(pool, [halo_tk[blk]], [xptk], lambda e: e.tensor_copy(xp[:, 0:4], halo[:, blk, :]))
        k.op(act, [ptk], [xptk], lambda e: e.activation(out=xp[:, 4:4 + T], in_=p[:, 0:T], func=AF.Copy))
        k.op(pool, [xptk], [halo_tk[blk]], lambda e: e.tensor_copy(halo[:, blk, :], xp[:, T:T + 4]))
        dg, dgtk = cdiag[ci % 2]
        for j in range(4):
            k.op(dve, [ctk], [dgtk], lambda e, j=j: e.tensor_scalar(dg[:, j, :], ident[:], cw[:, blk, j:j + 1], None, ALU.mult))
        k.tag = "ssd_proj"

    def conv_out(blk, out_ap, out_tk, ci):
        k.tag = "conv"
        T = cur["T"]
        xp, xptk = xpre[ci % 3]
        dg, dgtk = cdiag[ci % 2]
        cp, cptk = bank("proj")
        for j in range(4):
            mm(cp[:, 0:T], dg[:, j, :], xp[:, 1 + j:1 + j + T], [dgtk, xptk], cptk, j == 0, j == 3)
        k.op(act, [cptk, ctk], [out_tk], lambda e: e.activation(
            out=out_ap[:, 0:T], in_=cp[:, 0:T], func=AF.Silu, bias=cb[:, blk:blk + 1]))
        k.tag = "ssd_proj"

    def ssd_proj(g, d, first_grp):
        k.tag = "ssd_proj"
        Ga = cur["G"]
        s2, t2 = wload([(win_b[:, C_Z + g * 256:C_Z + (g + 1) * 256], 0)])
        s1, t1 = wload([(win_b[:, C_XBC + g * 512:C_XBC + (g + 1) * 512], 0)])
        outs = [(d["xsT"][:, 0, :], d["tk"]["xsT0"], g * 2), (d["xsT"][:, 1, :], d["tk"]["xsT1"], g * 2 + 1),
                (d["bmT"][:, :], d["tk"]["bmT"], 16 + g), (d["cmT"][:, :], d["tk"]["cmT"], 24 + g)]
        for c0 in range(0, Ga, 2):
            ncz = min(2, Ga - c0)
            p, ptk = bank("proj")
            for c in range(c0, c0 + ncz):
                for kc in range(8):
                    mm(p[:, (c - c0) * 256:(c - c0 + 1) * 256], uT[:, kc, cs(c)], s2[:, kc, 0:256],
                       [t2, uT_tk[c]], ptk, kc == 0, kc == 7)
            k.op(act, [ptk], [d["tk"]["sz"]], lambda e, c0=c0, p=p, ncz=ncz: e.activation(
                out=d["sz"][:, c0:c0 + ncz, :].rearrange("p c n -> p (c n)"), in_=p[:, 0:ncz * 256], func=AF.Silu))
        for b in range(5):
            if b < 4:
                def ev(p, ptk, idx, b=b):
                    conv_in(p, ptk, outs[b][2], first_grp, g * 4 + b)
                proj_fm(s1, t1, 0, [(b, b * 128)], ev)
            if b >= 1:
                conv_out(outs[b - 1][2], outs[b - 1][0], outs[b - 1][1], g * 4 + b - 1)

    def ssd_projB(g, d):
        k.tag = "ssd_proj"
        Ga = cur["G"]
        for c in range(Ga):
            p, ptk = bank("proj")
            pb = p[:].bitcast(BF16)
            for b in range(2):
                tr(pb[:, b * 128:(b + 1) * 128], d["xsT"][:, b, cs(c)], [d["tk"][f"xsT{b}"]], ptk, False)
            tr(pb[:, 256:384], d["bmT"][:, cs(c)], [d["tk"]["bmT"]], ptk, True)
            k.op(dve, [ptk, dt_tk], [d["tkc"]["xdt"][c]], lambda e, c=c, pb=pb: e.tensor_tensor(
                d["xdt"][:, c, :].rearrange("p (h q) -> p h q", h=4), pb[:, 0:256].rearrange("p (h q) -> p h q", h=4),
                dtt[:, c, g * 4:(g + 1) * 4].unsqueeze(2).to_broadcast([128, 4, 64]), ALU.mult))
            k.op(act, [ptk], [d["tkc"]["bm"][c]], lambda e, c=c, pb=pb: e.activation(
                out=d["bm"][:, c, :], in_=pb[:, 256:384], func=AF.Copy))
            k.op(pool, [d["tkc"]["xdt"][c], dt_tk], [d["tkc"]["xdec"][c]], lambda e, c=c: e.tensor_tensor(
                d["xdec"][:, c, :].rearrange("p (h q) -> p h q", h=4), d["xdt"][:, c, :].rearrange("p (h q) -> p h q", h=4),
                edec[:, c, g * 4:(g + 1) * 4].unsqueeze(2).to_broadcast([128, 4, 64]), ALU.mult))

    def ssd_P0(i, g, c, d):
        k.tag = "ssd_chunk"
        S = sst[i % 2]
        Rb = S["R"].rearrange("p h t -> p (h t)").bitcast(BF16)
        for w, src in ((0, dta_hi), (1, dta_lo)):
            k.op(pool, [ctk, dt_tk], [S["tk"]["R"]], lambda e, w=w, src=src: e.tensor_tensor(
                Rb[:, w * 512:(w + 1) * 512].rearrange("p (h t) -> p h t", h=4),
                triM_b[:, :].unsqueeze(1).to_broadcast([128, 4, 128]),
                src[:, c, g * 4:(g + 1) * 4].unsqueeze(2).to_broadcast([128, 4, 128]), ALU.mult))

    def ssd_P1(i, g, c, d):
        k.tag = "ssd_chunk"
        S = sst[i % 2]
        tk = S["tk"]
        A, Atk = psum[3], ps_tk[3]
        B, Btk = psum[4 + i % 2], ps_tk[4 + i % 2]
        mm(B[:, 0:128], d["bmT"][:, cs(c)], d["cmT"][:, cs(c)], [d["tk"]["bmT"], d["tk"]["cmT"]], Btk, True, True)
        Rb = S["R"].rearrange("p h t -> p (h t)").bitcast(BF16)
        mm(A[:, :], Umat_b[:], Rb[:, 0:512], [ctk, tk["R"]], Atk, True, False)
        mm(A[:, :], Umat_b[:], Rb[:, 512:1024], [ctk, tk["R"]], Atk, False, True)
        k.op(dve, [Btk, ctk], [tk["CBm"]], lambda e: e.tensor_tensor(S["CBm"][:, :], B[:, 0:128], causT[:], ALU.mult))
        k.op(act, [Atk], [tk["E"]], lambda e: e.activation(
            out=S["E"].rearrange("p h t -> p (h t)"), in_=A[:, :], func=AF.Exp))

    def ssd_P1b(i, g, c, d):
        k.tag = "ssd_chunk"
        S = sst[i % 2]
        tk = S["tk"]
        k.op(dve, [tk["E"], tk["CBm"]], [tk["MT"]], lambda e: e.tensor_tensor(
            S["MT"], S["E"], S["CBm"][:, :].unsqueeze(1).to_broadcast([128, 4, 128]), ALU.mult))

    def ssd_Ps(i, g, c, d):
        k.tag = "ssd_chunk"
        C, Ctk = psum[6 + i % 2], ps_tk[6 + i % 2]
        ssl = sstate_f[:, g * 256:(g + 1) * 256]
        k.op(pool, [ss_tk[g], dt_tk], [ss_tk[g]], lambda e: e.tensor_tensor(
            ssl.rearrange("p (h q) -> p h q", h=4), ssl.rearrange("p (h q) -> p h q", h=4),
            dchk[:, c, g * 4:(g + 1) * 4].unsqueeze(2).to_broadcast([128, 4, 64]), ALU.mult))
        mm(C[:, 0:256], d["bm"][:, c, :], d["xdec"][:, c, :], [d["tkc"]["bm"][c], d["tkc"]["xdec"][c]], Ctk, True, True)
        k.op(dve, [Ctk, ss_tk[g]], [ss_tk[g]], lambda e: e.tensor_tensor(ssl, ssl, C[:, 0:256], ALU.add))
        k.op(act, [ss_tk[g]], [ssb_tk[g]], lambda e: e.activation(
            out=sstate_b[:, g * 256:(g + 1) * 256], in_=ssl, func=AF.Copy))

    def ssd_P2(i, g, c, d):
        k.tag = "ssd_chunk"
        S = sst[i % 2]
        tk = S["tk"]
        B, Btk = psum[4 + i % 2], ps_tk[4 + i % 2]
        C, Ctk = psum[6 + i % 2], ps_tk[6 + i % 2]
        for b in range(2):
            mm(C[:, 256 + b * 128:256 + (b + 1) * 128], d["xsT"][:, b, cs(c)], Dd[:, g * 2 + b, :],
               [d["tk"][f"xsT{b}"], ctk], Ctk, True, False)
            for hh in (2 * b, 2 * b + 1):
                mm(C[:, 256 + hh * 64:256 + (hh + 1) * 64], S["MT"][:, hh, :], d["xdt"][:, c, hh * 64:(hh + 1) * 64],
                   [tk["MT"], d["tkc"]["xdt"][c]], Ctk, False, hh == 3)

    def ssd_Pyb(i, g, c, d):
        k.tag = "ssd_chunk"
        B, Btk = psum[4 + i % 2], ps_tk[4 + i % 2]
        mm(B[:, 128:384], d["cmT"][:, cs(c)], sstate_b[:, g * 256:(g + 1) * 256], [d["tk"]["cmT"], ssb_tk[g]], Btk, True, True)

    def ssd_P2b1(i, g, c, d):
        k.tag = "ssd_chunk"
        S = sst[i % 2]
        tk = S["tk"]
        B, Btk = psum[4 + i % 2], ps_tk[4 + i % 2]
        C, Ctk = psum[6 + i % 2], ps_tk[6 + i % 2]
        k.op(dve, [Btk, dt_tk], [tk["yc"]], lambda e: e.tensor_tensor(
            S["yc"].rearrange("p (h q) -> p h q", h=4), B[:, 128:384].rearrange("p (h q) -> p h q", h=4),
            escs[:, c, g * 4:(g + 1) * 4].unsqueeze(2).to_broadcast([128, 4, 64]), ALU.mult))
        k.op(dve, [Ctk, tk["yc"]], [tk["yg"]], lambda e: e.tensor_tensor(S["yg"][:, :], C[:, 256:512], S["yc"][:, :], ALU.add))

    def ssd_P2bg(i, g, c, d):
        k.tag = "ssd_chunk"
        S = sst[i % 2]
        tk = S["tk"]
        k.op(pool, [tk["yg"], d["tk"]["sz"]], [tk["yg"]], lambda e: e.tensor_tensor(
            S["yg"][:, :], S["yg"][:, :], d["sz"][:, c, :], ALU.mult))

    def ssd_P2b2(i, g, c, d):
        k.tag = "ssd_chunk"
        S = sst[i % 2]
        tk = S["tk"]
        scol = 32 + 2 * (i % 8)
        stt = stk_(scol)
        k.op(act, [tk["yg"]], [stt], lambda e: e.activation(
            out=junk[:, 0:256], in_=S["yg"][:, :], func=AF.Square, accum_out=stat[:, scol:scol + 1]))
        rstd(scol, scol + 1, 1, 1.0 / 256, stt)
        k.op(act, [tk["yg"], stt], [tk["yn"]], lambda e: e.activation(
            out=S["yn"][:, :], in_=S["yg"][:, :], func=AF.Copy, scale=stat[:, scol + 1:scol + 2]))

    def ssd_P3(i, g, c, d):
        k.tag = "ssd_chunk"
        S = sst[i % 2]
        tk = S["tk"]
        B, Btk = psum[4 + i % 2], ps_tk[4 + i % 2]
        pb = B[:].bitcast(BF16)
        for b in range(2):
            tr(pb[:, 768 + b * 128:768 + (b + 1) * 128], S["yn"][:, b * 128:(b + 1) * 128], [tk["yn"]], Btk, b == 1)
        k.op(dve, [Btk, ctk], [retT_tk[g * 2][c], retT_tk[g * 2 + 1][c]], lambda e: e.tensor_tensor(
            retT[:, g * 2:g * 2 + 2, cs(c)], pb[:, 768:1024].rearrange("p (a b) -> p a b", a=2),
            ssdn[:, g * 2:g * 2 + 2].unsqueeze(2).to_broadcast([128, 2, 128]), ALU.mult))

    def ssd(is_meta, first_grp):
        ssd_dt(is_meta)
        Ga = cur["G"]
        pmode["proj"] = "p3"
        iters = [(g, c) for g in range(8) for c in range(Ga)]
        N = len(iters)
        done = set()
        doneB = set()

        def ensure(g):
            if g < 8 and g not in done:
                done.add(g)
                ssd_proj(g, SG[g % 2], first_grp)

        def ensureB(g):
            ensure(g)
            if g < 8 and g not in doneB:
                doneB.add(g)
                ssd_projB(g, SG[g % 2])

        def args(i):
            g, c = iters[i]
            return (i, g, c, SG[g % 2])

        ensure(0)
        for r in range(-2, N + 2):
            if 0 <= r + 1 < N:
                ensure(iters[r + 1][0])
            if 0 <= r < N:
                ensureB(iters[r][0])
            if 0 <= r - 1 < N:
                ssd_P2b1(*args(r - 1))
            if 0 <= r < N:
                ssd_Pyb(*args(r))
                ssd_Ps(*args(r))
            if 0 <= r - 1 < N:
                ssd_P2bg(*args(r - 1))
            if 0 <= r + 2 < N:
                ssd_P0(*args(r + 2))
            if 0 <= r + 1 < N:
                ssd_P1(*args(r + 1))
            if 0 <= r - 1 < N:
                ssd_P2b2(*args(r - 1))
            if 0 <= r < N:
                ssd_P2(*args(r))
            if 0 <= r + 1 < N:
                ssd_P1b(*args(r + 1))
            if 0 <= r - 2 < N:
                ssd_P3(*args(r - 2))
            if 0 <= r < N and iters[r][1] == 0:
                ensure(iters[r][0] + 1)
            if 0 <= r < N and iters[r][1] == min(2, Ga - 1):
                ensureB(iters[r][0] + 1)
        pmode["proj"] = "a"

    def out_proj_and_residual(gvec_tile):
        k.tag = "outproj"
        Ga = cur["G"]
        for c in range(Ga):
            un, un_tk = un2[c % 2], un2_tk[c % 2]
            k.op(act, [mg_tk[c]], [un_tk], lambda e, c=c, un=un: e.activation(out=un[:], in_=merged[:, c, :], func=AF.Copy))
            p, ptk = bank("a")
            pb = p[:].bitcast(BF16)
            for kc in range(8):
                tr(pb[:, kc * 128:(kc + 1) * 128], un[:, kc * 128:(kc + 1) * 128], [un_tk], ptk, kc == 7)
            k.op(dve, [ptk], [uT_tk[c]], lambda e, c=c, pb=pb: e.tensor_copy(
                uT[:, :, cs(c)], pb.rearrange("p (a b) -> p a b", a=8)))
        slots = [wload([(wout_b[:, nt * 512:(nt + 1) * 512], 0)]) for nt in range(2)]
        for c in range(Ga):
            bks = []
            for nt in range(2):
                s, stk = slots[nt]
                p, ptk = bank("a")
                for kc in range(8):
                    mm(p[:, :], uT[:, kc, cs(c)], s[:, kc, :], [stk, uT_tk[c]], ptk, kc == 0, kc == 7)
                bks.append((p, ptk))
            norm_residual(bks, c, gvec_tile, 48 + 4 * (c % 4))

    def norm_residual(bks, c, gvec_tile, scol):
        k.tag = "norm_res"
        for nt in range(2):
            p, ptk = bks[nt]
            k.op(act, [ptk], [stk_(scol)], lambda e, p=p, nt=nt: e.activation(
                out=junk[:, 0:512], in_=p[:, :], func=AF.Square, accum_out=stat[:, scol + nt:scol + nt + 1]))
        k.op(dve, [stk_(scol)], [stk_(scol)], lambda e: e.tensor_tensor(
            stat[:, scol + 2:scol + 3], stat[:, scol:scol + 1], stat[:, scol + 1:scol + 2], ALU.add))
        rstd(scol + 2, scol + 3, 1, 1.0 / D, stk_(scol))
        for nt in range(2):
            p, ptk = bks[nt]
            (m1, t1) = rot_s[nt]
            k.op(dve, [ptk, stk_(scol), ctk], [t1], lambda e, p=p, m1=m1, nt=nt: e.scalar_tensor_tensor(
                m1[:, :], p[:, :], stat[:, scol + 3:scol + 4], gvec_tile[:, nt * 512:(nt + 1) * 512], ALU.mult, ALU.mult))
            hs = hbuf[:, c, nt * 512:(nt + 1) * 512]
            k.op(dve, [t1, h_tk[c]], [h_tk[c]], lambda e, m1=m1, hs=hs: e.tensor_tensor(hs, hs, m1[:, :], ALU.add))

    def ffn(nxt=None):
        prenorm_uT(gfpre)
        k.tag = "ffn_up"
        Ga, T = cur["G"], cur["T"]
        nfb = DFF // 128
        fb = 0
        while fb < nfb:
            nb = min(4, nfb - fb)
            sg_, sgtk_ = wload([(wgate_b[:, fb * 128:(fb + nb) * 128], 0)])
            su_, sutk_ = wload([(wup_b[:, fb * 128:(fb + nb) * 128], 0)])
            for b in range(nb):
                pg, pgtk = bank("a")
                for kc in range(8):
                    mm(pg[:, 0:T], sg_[:, kc, b * 128:(b + 1) * 128], uT[:, kc, 0:T], [sgtk_] + uT_tk, pgtk, kc == 0, kc == 7)
                pu, putk = bank("a")
                for kc in range(8):
                    mm(pu[:, 0:T], su_[:, kc, b * 128:(b + 1) * 128], uT[:, kc, 0:T], [sutk_] + uT_tk, putk, kc == 0, kc == 7)
                sg2, sg2tk = sgt[(fb + b) % 2]
                k.op(act, [pgtk], [sg2tk], lambda e, pg=pg, sg2=sg2: e.activation(out=sg2[:, 0:T], in_=pg[:, 0:T], func=AF.Silu))
                k.op(dve, [putk, sg2tk], [hid_tk[fb + b]], lambda e, pu=pu, sg2=sg2, i=fb + b: e.tensor_tensor(
                    hidT[:, i, 0:T], pu[:, 0:T], sg2[:, 0:T], ALU.mult))
            fb += nb
        segs = [(0, 8), (8, 8), (16, 6)]
        k.tag = "ffn_down"
        if nxt is not None:
            prenorm_stats(xn, xn_tkl, 8)
            k.tag = "ffn_down"
        napplied = 0
        for nt in range(2):
            banks = {c: bank("b") for c in range(Ga)}
            for si, (f0, nf) in enumerate(segs):
                s, stk = wload([(wdown_b[f0 * 128:(f0 + nf) * 128, nt * 512:(nt + 1) * 512], 0)])
                for c in range(Ga):
                    p, ptk = banks[c]
                    for kb in range(nf):
                        mm(p[:, :], hidT[:, f0 + kb, cs(c)], s[:, kb, :], [stk, hid_tk[f0 + kb]], ptk,
                           si == 0 and kb == 0, si == 2 and kb == nf - 1)
                if nxt is not None and napplied < Ga and not (nt == 0 and si == 0):
                    prenorm_apply(gpre, xn, xn_tkl, napplied, 8, on_dve=True)
                    napplied += 1
                    k.tag = "ffn_down"
            if nt == 1 and nxt is not None:
                while napplied < Ga:
                    prenorm_apply(gpre, xn, xn_tkl, napplied, 8, on_dve=True)
                    napplied += 1
                rope_tables(nxt)
                k.tag = "ffn_down"
            for c in range(Ga):
                p, ptk = banks[c]
                scol = 48 + 4 * (c % 4)
                k.op(act, [ptk], [stk_(scol)], lambda e, p=p, nt=nt, scol=scol: e.activation(
                    out=junk[:, 0:512], in_=p[:, :], func=AF.Square, accum_out=stat[:, scol + nt:scol + nt + 1]))
                k.op(act, [ptk], [mg_tk[c]], lambda e, p=p, nt=nt, c=c: e.activation(
                    out=merged[:, c, nt * 512:(nt + 1) * 512], in_=p[:, :], func=AF.Copy))
        for c in range(Ga):
            scol = 48 + 4 * (c % 4)
            k.op(dve, [stk_(scol)], [stk_(scol)], lambda e, scol=scol: e.tensor_tensor(
                stat[:, scol + 2:scol + 3], stat[:, scol:scol + 1], stat[:, scol + 1:scol + 2], ALU.add))
            rstd(scol + 2, scol + 3, 1, 1.0 / D, stk_(scol))
            k.op(dve, [mg_tk[c], stk_(scol), ctk], [mg_tk[c]], lambda e, c=c, scol=scol: e.scalar_tensor_tensor(
                merged[:, c, :], merged[:, c, :], stat[:, scol + 3:scol + 4], gfpost[:, :], ALU.mult, ALU.mult))
            k.op(dve, [mg_tk[c], h_tk[c]], [mg_tk[c]], lambda e, c=c: e.tensor_tensor(
                merged[:, c, :], merged[:, c, :], hbuf[:, c, :], ALU.add))

    out_tk = Tk("ydram")

    def zero_states():
        for i in range(8):
            k.op(pool, [], [rs_tk[i]], lambda e, i=i: e.memset(rstate_f[:, i, :], 0.0))
            k.op(pool, [], [rsb_tk[i]], lambda e, i=i: e.memset(rstate_b[:, i, :], 0.0))
            k.op(pool, [], [ss_tk[i]], lambda e, i=i: e.memset(sstate_f[:, i * 256:(i + 1) * 256], 0.0))
            k.op(pool, [], [ssb_tk[i]], lambda e, i=i: e.memset(sstate_b[:, i * 256:(i + 1) * 256], 0.0))

    def run_group(s, gi, is_meta, prefetched=False, nxt=None):
        first_grp = is_meta
        if is_meta:
            cur["G"], cur["T"] = 1, 128
            k.op(pool, [], [h_tk[0]], lambda e: e.memset(hbuf[:, 0, :], 0.0))
            k.dma(sp, hbuf[128 - NMETA:128, 0, :], meta[:, :], [], [h_tk[0]])
            pos0 = NMETA - 128
        else:
            cur["G"], cur["T"] = G, T
            r0 = gi * T
            if not prefetched:
                for c in range(G):
                    k.dma(sp, hbuf[:, c, :], x[s, r0 + c * 128:r0 + (c + 1) * 128, :], [], [h_tk[c]])
            pos0 = NMETA + r0
        if not prefetched:
            prenorm_uT(gpre)
            rope_tables(pos0)
        dump("uT", uT[:, 0, :], uT_tk)
        k.handoff(all_tks_ffn() + all_tks_ssd(), all_tks_ret())
        retention()
        dump("retT", retT[:, 0, :], [t for l in retT_tk for t in l])
        if not is_meta:
            gates_proj(C_GATE)
            branch(wret_b, True)
        dump("merged_a", merged[:, 0, :], mg_tk)
        k.handoff(all_tks_ret(), all_tks_ssd())
        ssd(is_meta, first_grp)
        dump("yT", retT[:, 0, :], [t for l in retT_tk for t in l])
        if is_meta:
            return
        gates_proj(C_GATE + 1024)
        branch(wssd_b, False)
        if nxt is not None:
            s2, gi2 = nxt
            for c in range(G):
                k.dma(sp, xn[:, c, :], x[s2, gi2 * T + c * 128:gi2 * T + (c + 1) * 128, :], [], xn_tks)
        dump("merged", merged[:, 0, :], mg_tk)
        out_proj_and_residual(gpost)
        dump("hmid", hbuf[:, 0, :], h_tk)
        k.handoff(all_tks_ssd(), all_tks_ffn())
        ffn(None if nxt is None else NMETA + nxt[1] * T)
        for c in range(G):
            k.dma(pool, y[s, r0 + c * 128:r0 + (c + 1) * 128, :], merged[:, c, :], [mg_tk[c]], [out_tk])
        if nxt is not None:
            for c in range(G):
                k.dma(pool, hbuf[:, c, :], xn[:, c, :], xn_tks, [h_tk[c]])

    sv_tk = Tk("stsave")
    for s in range(NSEQ):
        if s == 0:
            zero_states()
            run_group(s, 0, True)
            if NSEQ > 1:
                k.dma(sp, st_r[:, :], rstate_f[:].rearrange("p a b -> p (a b)"), rs_tk, [sv_tk])
                k.dma(sp, st_s[:, :], sstate_f[:, :], ss_tk, [sv_tk])
                k.dma(sp, st_h[:, :], halo[:].rearrange("p a b -> p (a b)"), halo_tk, [sv_tk])
        else:
            k.dma(sp, rstate_f[:].rearrange("p a b -> p (a b)"), st_r[:, :], [sv_tk], rs_tk)
            k.dma(sp, sstate_f[:, :], st_s[:, :], [sv_tk], ss_tk)
            k.dma(sp, halo[:].rearrange("p a b -> p (a b)"), st_h[:, :], [sv_tk], halo_tk)
            for i in range(8):
                k.op(pool, [rs_tk[i]], [rsb_tk[i]], lambda e, i=i: e.tensor_copy(rstate_b[:, i, :], rstate_f[:, i, :]))
                k.op(act, [ss_tk[i]], [ssb_tk[i]], lambda e, i=i: e.activation(
                    out=sstate_b[:, i * 256:(i + 1) * 256], in_=sstate_f[:, i * 256:(i + 1) * 256], func=AF.Copy))
        for gi in range(NGRP):
            idx = s * NGRP + gi
            nx = None
            if PREFETCH and idx + 1 < NSEQ * NGRP:
                nx = ((idx + 1) // NGRP, (idx + 1) % NGRP)
            run_group(s, gi, False, prefetched=(PREFETCH and idx > 0), nxt=nx)
    for q in (sp, pool):
        for i, sm in enumerate(q.dma_sems):
            n = (q.dma_n - i + len(q.dma_sems) - 1) // len(q.dma_sems)
            if n > 0:
                k._wait(sp, (sm, 16 * n, None))
    for e_ in (pe, act, dve, pool):
        if e_.cnt:
            k._wait(sp, (e_.sem, e_.cnt, e_))
    es.close()
    return nc, dbg_out, k


PARAM_NAMES = ["meta_tokens", "norm_mix_pre", "w_in", "conv_w", "conv_b", "dt_bias", "a_log", "d_skip", "ssd_norm",
               "w_ret_branch", "w_ssd_branch", "w_out", "norm_mix_post", "norm_ffn_pre", "w_gate", "w_up", "w_down",
               "norm_ffn_post"]


def _prep_params(inputs):
    p = {}
    for n in PARAM_NAMES:
        a = np.ascontiguousarray(np.asarray(inputs[n], dtype=np.float32))
        if n == "meta_tokens":
            p[n] = a
        elif a.ndim == 2:
            p[n] = a
        else:
            p[n] = np.ascontiguousarray(a[0])
    return p


def kernel(**inputs):
    x = np.asarray(inputs["x"], dtype=np.float32)
    B = x.shape[0]
    ncores = 8
    nseq = B // ncores
    G = 4
    ngrp = SEQ // (128 * G)
    nc, _, _ = build(nseq, ngrp, G)
    params = _prep_params(inputs)
    in_maps = []
    for i in range(ncores):
        m = dict(params)
        m["x"] = np.ascontiguousarray(x[i * nseq:(i + 1) * nseq])
        in_maps.append(m)
    res = run_bass_kernel_spmd(nc, in_maps, core_ids=list(range(ncores)))
    out = np.concatenate([np.asarray(r["y"]) for r in res.results], axis=0)
    return out.astype(np.float32)
```

```python
import math
from contextlib import ExitStack

import numpy as np
import concourse.bass as bass
import concourse.mybir as mybir
from concourse.bass_utils import run_bass_kernel_spmd

F32 = mybir.dt.float32
BF16 = mybir.dt.bfloat16
I32 = mybir.dt.int32
AF = mybir.ActivationFunctionType
ALU = mybir.AluOpType

D = 1024
NMETA = 16
SEQ = 2048
IN_W = 14368
DFF = 2816
EPS = 1e-6
C_Q, C_K, C_V, C_G, C_Z, C_XBC, C_DT, C_GATE = 0, 1024, 2048, 4096, 6144, 8192, 12288, 12320
LOGG = [math.log1p(-2.0 ** (-5.0 - h)) for h in range(4)]
NW = 3
NDMA_SEM = 8
PREFETCH = True


class Tk:
    __slots__ = ("name", "w", "r", "rw")

    def __init__(self, name, rw=False):
        self.name = name
        self.w = None
        self.r = {}
        self.rw = rw


class Eng:
    def __init__(self, name, e, sem, kind):
        self.name, self.e, self.sem, self.kind = name, e, sem, kind
        self.cnt = 0
        self.seen = {}
        self.dma_sems = []
        self.dma_n = 0


class K:
    def __init__(self, nc, es):
        self.nc, self.es = nc, es
        self.pe = self._eng("pe", nc.tensor, "pe")
        self.act = self._eng("act", nc.scalar, "c")
        self.dve = self._eng("dve", nc.vector, "c")
        self.pool = self._eng("pool", nc.gpsimd, "c")
        self.sp = self._eng("sp", nc.sync, "q")
        for q, n in ((self.sp, NDMA_SEM), (self.pool, 64)):
            q.dma_sems = [es.enter_context(nc.semaphore(f"d_{q.name}{i}")) for i in range(n)]
        self.n_instr = 0
        self.tag = "setup"
        self.tagmap = {}

    def _eng(self, name, e, kind):
        sem = self.es.enter_context(self.nc.semaphore("s_" + name))
        return Eng(name, e, sem, kind)

    def _wait(self, eng, dep):
        sem, val, src = dep
        if src is eng and eng.kind == "pe":
            return
        k = id(sem)
        if eng.seen.get(k, 0) >= val:
            return
        eng.e.wait_ge(sem, val)
        eng.seen[k] = val

    def _deps(self, eng, reads, writes):
        for t in reads:
            if t.w is not None:
                self._wait(eng, t.w)
            if t.rw:
                for d in t.r.values():
                    self._wait(eng, d)
        for t in writes:
            if t.w is not None:
                self._wait(eng, t.w)
            for d in t.r.values():
                if d[2] is eng and eng.kind != "q" and not t.rw:
                    continue
                self._wait(eng, d)

    def _record(self, reads, writes, dep):
        k = id(dep[0])
        for t in reads:
            if t.rw:
                t.w = dep
                t.r = {}
            else:
                t.r[k] = dep
        for t in writes:
            t.w = dep
            t.r = {}

    def op(self, eng, reads, writes, fn, signal=True):
        self._deps(eng, reads, writes)
        ins = fn(eng.e)
        self.n_instr += 1
        self.tagmap[ins.ins.name] = self.tag
        if signal:
            eng.cnt += 1
            ins.then_inc(eng.sem, 1)
            dep = (eng.sem, eng.cnt, eng)
        else:
            dep = (eng.sem, eng.cnt + 1, eng)
        self._record(reads, writes, dep)
        return ins

    def dma(self, q, out, in_, reads, writes, **kw):
        i = q.dma_n
        q.dma_n += 1
        ns = len(q.dma_sems)
        sem = q.dma_sems[i % ns]
        prev = 16 * (i // ns)
        if prev > 0:
            self._wait(q, (sem, prev, None))
        self._deps(q, reads, writes)
        q.e.dma_start(out=out, in_=in_, **kw).then_inc(sem, 16)
        self.n_instr += 1
        dep = (sem, prev + 16, None)
        self._record(reads, writes, dep)

    def handoff(self, olds, news):
        for n in news:
            for o in olds:
                for d in ([o.w] if o.w is not None else []) + list(o.r.values()):
                    k = id(d[0])
                    if k not in n.r or n.r[k][1] < d[1]:
                        n.r[k] = d

    def finish(self, tks):
        for t in tks:
            if t.w is not None:
                self._wait(self.sp, t.w)
            for d in t.r.values():
                self._wait(self.sp, d)


def build(NSEQ, NGRP, G, debug=()):
    T = 128 * G
    cur = {"G": G, "T": T}
    nc = bass.Bass("TRN2", target_bir_lowering=False)
    es = ExitStack()
    k = K(nc, es)
    pe, act, dve, pool, sp = k.pe, k.act, k.dve, k.pool, k.sp
    LSEQ = NGRP * T

    def din(name, shape):
        return nc.dram_tensor(name, list(shape), F32, kind="ExternalInput").ap()

    x = din("x", [NSEQ, LSEQ, D])
    meta = din("meta_tokens", [NMETA, D])
    g_mix_pre = din("norm_mix_pre", [1, D])
    w_in = din("w_in", [D, IN_W])
    conv_w = din("conv_w", [4, 4096])
    conv_b = din("conv_b", [1, 4096])
    dt_bias = din("dt_bias", [1, 32])
    a_log = din("a_log", [1, 32])
    d_skip = din("d_skip", [1, 32])
    ssd_norm = din("ssd_norm", [1, 2048])
    w_ret = din("w_ret_branch", [2048, D])
    w_ssd = din("w_ssd_branch", [2048, D])
    w_out = din("w_out", [D, D])
    g_mix_post = din("norm_mix_post", [1, D])
    g_ffn_pre = din("norm_ffn_pre", [1, D])
    w_gate = din("w_gate", [D, DFF])
    w_up = din("w_up", [D, DFF])
    w_down = din("w_down", [DFF, D])
    g_ffn_post = din("norm_ffn_post", [1, D])
    y = nc.dram_tensor("y", [NSEQ, LSEQ, D], F32, kind="ExternalOutput").ap()

    def dscr(name, shape, dt=BF16):
        return nc.dram_tensor(name, list(shape), dt, kind="Internal").ap()

    win_b = dscr("win_b", [D, IN_W])
    wret_b = dscr("wret_b", [2048, D])
    wssd_b = dscr("wssd_b", [2048, D])
    wout_b = dscr("wout_b", [D, D])
    wgate_b = dscr("wgate_b", [D, DFF])
    wup_b = dscr("wup_b", [D, DFF])
    wdown_b = dscr("wdown_b", [DFF, D])
    st_r = dscr("st_r", [128, 8 * 512], F32)
    st_s = dscr("st_s", [128, 2048], F32)
    st_h = dscr("st_h", [128, 32 * 4], BF16)
    dbg_out = {}

    def sb(name, shape, dt):
        return es.enter_context(nc.sbuf_tensor(name, list(shape), dt))

    def dump(name, ap, tks, shape=None):
        if name not in debug:
            return
        cnt = sum(1 for n in dbg_out if n.startswith(name + "#"))
        nm = f"{name}#{cnt}"
        shp = list(ap.shape)
        dt = ap.dtype
        t = nc.dram_tensor("dbg_" + nm.replace("#", "_"), shp, dt, kind="ExternalOutput").ap()
        dbg_out[nm] = t
        k.dma(sp, t, ap, tks, [dbgtk])

    dbgtk = Tk("dbg")

    ident = sb("ident", [128, 128], BF16)
    identf = sb("identf", [128, 128], F32)
    triM = sb("triM", [128, 128], F32)
    Umat = sb("Umat", [128, 128], F32)
    ones = sb("ones", [128, 128], F32)
    causT = sb("causT", [128, 128], F32)
    DT = sb("DT", [128, 4, 128], F32)
    xiT = sb("xiT", [128, 4, 128], F32)
    zeta = sb("zeta", [128, 4], F32)
    invf = sb("invf", [128, 1], F32)
    iot = sb("iot", [128, 128], I32)
    iotf = sb("iotf", [128, 128], F32)
    ipart = sb("ipart", [128, 1], I32)
    ipartf = sb("ipartf", [128, 1], F32)
    gpre = sb("gpre", [128, 8], F32)
    gfpre = sb("gfpre", [128, 8], F32)
    gpost = sb("gpost", [128, D], F32)
    gfpost = sb("gfpost", [128, D], F32)
    ssdn = sb("ssdn", [128, 16], F32)
    cw = sb("cw", [128, 32, 4], F32)
    cwraw = sb("cwraw", [128, 4, 32], F32)
    cb = sb("cb", [128, 32], F32)
    dtb = sb("dtb", [128, 32], F32)
    a_b = sb("a_b", [128, 32], F32)
    dsk = sb("dsk", [128, 16], F32)
    Dd = sb("Dd", [128, 16, 128], BF16)
    mmask = sb("mmask", [128, 1], F32)
    wdt = sb("wdt", [128, 8, 32], BF16)
    ctk = Tk("const")
    Umat_b = sb("Umat_b", [128, 128], BF16)
    triM_b = sb("triM_b", [128, 128], BF16)
    dta_hi = sb("dta_hi", [128, G, 32], BF16)
    dta_lo = sb("dta_lo", [128, G, 32], BF16)

    def c_op(eng, fn, extra_r=(), extra_w=()):
        k.op(eng, [ctk] + list(extra_r), [ctk] + list(extra_w), fn)

    c_op(pool, lambda e: e.memset(identf[:], 0.0))
    c_op(pool, lambda e: e.affine_select(out=identf[:], in_=identf[:], compare_op=ALU.not_equal, fill=1.0,
                                         base=0, pattern=[[-1, 128]], channel_multiplier=1))
    c_op(dve, lambda e: e.tensor_copy(ident[:], identf[:]))
    c_op(pool, lambda e: e.memset(triM[:], 1.0))
    c_op(pool, lambda e: e.affine_select(out=triM[:], in_=triM[:], compare_op=ALU.is_ge, fill=0.0,
                                         base=0, pattern=[[1, 128]], channel_multiplier=-1))
    c_op(dve, lambda e: e.tensor_copy(causT[:], triM[:]))
    c_op(pool, lambda e: e.memset(Umat[:], 1.0))
    c_op(pool, lambda e: e.affine_select(out=Umat[:], in_=Umat[:], compare_op=ALU.is_gt, fill=0.0,
                                         base=0, pattern=[[-1, 128]], channel_multiplier=1))
    c_op(pool, lambda e: e.memset(ones[:], 1.0))
    c_op(dve, lambda e: e.tensor_copy(Umat_b[:], Umat[:]))
    c_op(dve, lambda e: e.tensor_copy(triM_b[:], triM[:]))
    c_op(pool, lambda e: e.iota(iot[:], pattern=[[1, 128]], base=0, channel_multiplier=-1))
    c_op(dve, lambda e: e.tensor_copy(iotf[:], iot[:]))
    c_op(pool, lambda e: e.iota(ipart[:], pattern=[[0, 1]], base=0, channel_multiplier=1))
    c_op(dve, lambda e: e.tensor_copy(ipartf[:], ipart[:]))
    for h in range(4):
        c_op(act, lambda e, h=h: e.activation(out=DT[:, h, :], in_=iotf[:], func=AF.Exp, scale=LOGG[h]))
        c_op(dve, lambda e, h=h: e.scalar_tensor_tensor(DT[:, h, :], DT[:, h, :], 1.0 / 16, causT[:],
                                                        ALU.mult, ALU.mult))
    tix = sb("tix", [128, 128], I32)
    tixf = sb("tixf", [128, 128], F32)
    c_op(pool, lambda e: e.iota(tix[:], pattern=[[1, 128]], base=1, channel_multiplier=0))
    c_op(dve, lambda e: e.tensor_copy(tixf[:], tix[:]))
    for h in range(4):
        c_op(act, lambda e, h=h: e.activation(out=xiT[:, h, :], in_=tixf[:], func=AF.Exp, scale=LOGG[h]))
        c_op(act, lambda e, h=h: e.activation(out=zeta[:, h:h + 1], in_=ipartf[:], func=AF.Exp, scale=-LOGG[h]))
        c_op(dve, lambda e, h=h: e.tensor_scalar(zeta[:, h:h + 1], zeta[:, h:h + 1],
                                                 math.exp(127 * LOGG[h]) / 16.0, None, ALU.mult))
    c_op(act, lambda e: e.activation(out=invf[:], in_=ipartf[:], func=AF.Exp, scale=-math.log(10000.0) / 128))
    invf2 = sb("invf2", [128, 1], F32)
    c_op(dve, lambda e: e.tensor_scalar(invf2[:], invf[:], 1.0 / (2 * math.pi), None, ALU.mult))
    c_op(pool, lambda e: e.memset(mmask[:], 1.0))
    c_op(pool, lambda e: e.affine_select(out=mmask[:], in_=mmask[:], compare_op=ALU.is_ge, fill=0.0,
                                         base=-(128 - NMETA), pattern=[[0, 1]], channel_multiplier=1))
    k.dma(sp, gpre[:], g_mix_pre[0].rearrange("(k p) -> p k", p=128), [], [ctk], allow_slow_non_contiguous=True)
    k.dma(sp, gfpre[:], g_ffn_pre[0].rearrange("(k p) -> p k", p=128), [], [ctk], allow_slow_non_contiguous=True)
    k.dma(sp, ssdn[:], ssd_norm[0].rearrange("(k p) -> p k", p=128), [], [ctk], allow_slow_non_contiguous=True)
    k.dma(sp, cb[:], conv_b[0].rearrange("(k p) -> p k", p=128), [], [ctk], allow_slow_non_contiguous=True)
    for j in range(4):
        k.dma(sp, cwraw[:, j, :], conv_w[j].rearrange("(k p) -> p k", p=128), [], [ctk],
              allow_slow_non_contiguous=True)
    k.dma(sp, gpost[:], g_mix_post[0].partition_broadcast(128), [], [ctk])
    k.dma(sp, gfpost[:], g_ffn_post[0].partition_broadcast(128), [], [ctk])
    k.dma(sp, dtb[:], dt_bias[0].partition_broadcast(128), [], [ctk])
    k.dma(sp, a_b[:], a_log[0].partition_broadcast(128), [], [ctk])
    for hh in range(2):
        k.dma(sp, dsk[hh * 64:(hh + 1) * 64, :], d_skip[0].rearrange("(b two) -> two b", two=2)[hh]
              .partition_broadcast(64), [], [ctk], allow_slow_non_contiguous=True)
    c_op(dve, lambda e: e.tensor_copy(cw[:].rearrange("p b j -> p j b"), cwraw[:]))
    c_op(act, lambda e: e.activation(out=a_b[:], in_=a_b[:], func=AF.Exp))
    c_op(dve, lambda e: e.tensor_scalar(a_b[:], a_b[:], -1.0, None, ALU.mult))
    for b in range(16):
        c_op(dve, lambda e, b=b: e.tensor_scalar(Dd[:, b, :], identf[:], dsk[:, b:b + 1], None, ALU.mult))

    wcast = {}

    def cast(dst, src, rows, cols):
        RB = 256
        lst = []
        for r0 in range(0, rows, RB):
            r1 = min(rows, r0 + RB)
            t = Tk(f"wc{len(wcast)}_{r0}")
            k.dma(pool, dst[r0:r1, :], src[r0:r1, :], [], [t])
            lst.append(t)
        wcast[dst.tensor.name] = lst

    wcast_cols = []

    def cast_cols(c0, c1, d0=None):
        d0 = c0 if d0 is None else d0
        t = Tk(f"wcc{d0}")
        k.dma(pool, win_b[:, d0:d0 + (c1 - c0)], w_in[:, c0:c1], [ctk] if not wcast_cols else [], [t])
        wcast_cols.append((d0, d0 + (c1 - c0), t))

    for h in range(4):
        cast_cols(C_Q + h * 256, C_Q + (h + 1) * 256, h * 512)
        cast_cols(C_K + h * 256, C_K + (h + 1) * 256, h * 512 + 256)
        cast_cols(C_V + h * 512, C_V + (h + 1) * 512)
        cast_cols(C_G + h * 512, C_G + (h + 1) * 512)
    cast_cols(C_DT, C_DT + 32)
    for g in range(8):
        cast_cols(C_XBC + g * 256, C_XBC + (g + 1) * 256, C_XBC + g * 512)
        cast_cols(C_XBC + 2048 + g * 128, C_XBC + 2048 + (g + 1) * 128, C_XBC + g * 512 + 256)
        cast_cols(C_XBC + 3072 + g * 128, C_XBC + 3072 + (g + 1) * 128, C_XBC + g * 512 + 384)
    cast_cols(C_Z, C_Z + 2048)
    cast_cols(C_GATE, C_GATE + 2048)

    def wdeps(src):
        if src.tensor.name == win_b.tensor.name:
            c0 = int(src.offset) % IN_W
            c1 = c0 + src.shape[1]
            return [t for (a_, b_, t) in wcast_cols if a_ < c1 and b_ > c0]
        return wcast[src.tensor.name]
    cast(wret_b, w_ret, 2048, D)
    cast(wssd_b, w_ssd, 2048, D)
    cast(wout_b, w_out, D, D)
    cast(wgate_b, w_gate, D, DFF)
    cast(wup_b, w_up, D, DFF)
    cast(wdown_b, w_down, DFF, D)
    wdt_tk = Tk("wdt")
    wdt_loaded = [False]

    def load_wdt():
        if not wdt_loaded[0]:
            wdt_loaded[0] = True
            k.dma(sp, wdt[:], win_b[:, C_DT:C_DT + 32].rearrange("(k p) n -> p k n", p=128),
                  wdeps(win_b[:, C_DT:C_DT + 32]), [wdt_tk])

    rstate_f = sb("rstate_f", [128, 8, 512], F32)
    rstate_b = sb("rstate_b", [128, 8, 512], BF16)
    sstate_f = sb("sstate_f", [128, 2048], F32)
    sstate_b = sb("sstate_b", [128, 2048], BF16)
    halo = sb("halo", [128, 32, 4], BF16)
    rs_tk = [Tk(f"rs{i}") for i in range(8)]
    rsb_tk = [Tk(f"rsb{i}") for i in range(8)]
    ss_tk = [Tk(f"ss{i}") for i in range(8)]
    ssb_tk = [Tk(f"ssb{i}") for i in range(8)]
    halo_tk = [Tk(f"halo{i}") for i in range(32)]

    wslot = [sb(f"wslot{i}", [128, 8, 512], BF16) for i in range(NW)]
    wslot_tk = [Tk(f"wslot{i}") for i in range(NW)]
    wctr = [0]

    def wload(pieces):
        i = wctr[0] % NW
        wctr[0] += 1
        s, tk = wslot[i], wslot_tk[i]
        for src, off in pieces:
            nk = src.shape[0] // 128
            ncol = src.shape[1]
            k.dma(sp, s[:, 0:nk, off:off + ncol], src.rearrange("(k p) n -> p k n", p=128), wdeps(src), [tk])
        return s, tk

    hbuf = sb("hbuf", [128, G, D], F32)
    h_tk = [Tk(f"h{c}") for c in range(G)]
    uT = sb("uT", [128, 8, T], BF16)
    uT_tk = [Tk(f"uT{c}") for c in range(G)]
    un2 = [sb("un0", [128, D], BF16), sb("un1", [128, D], BF16)]
    un2_tk = [Tk("un0"), Tk("un1")]
    un, un_tk = un2[0], un2_tk[0]
    junk = sb("junk", [128, D], BF16)
    junk_tk = Tk("junk")
    stat = sb("stat", [128, 64], F32)
    stat_tk = Tk("stat")
    stat_slots = {}

    def stk_(col):
        if col not in stat_slots:
            stat_slots[col] = Tk(f"stat{col}")
        return stat_slots[col]
    tmpA = sb("tmpA", [128, 512], F32)
    tmpB = sb("tmpB", [128, 512], F32)
    rot_s = [(tmpA, Tk("tmpA")), (tmpB, Tk("tmpB"))]
    cosT = sb("cosT", [128, T], F32)
    sinT = sb("sinT", [128, T], F32)
    rope_tk = Tk("rope")
    retT = sb("retT", [128, 16, T], BF16)
    retT_tk = [[Tk(f"retT{b}_{c}") for c in range(G)] for b in range(16)]
    xn = retT[:].rearrange("p a b -> p (a b)").bitcast(F32).rearrange("p (c d) -> p c d", c=G)
    xn_tks = [t for l in retT_tk for t in l]
    xn_tkl = [xn_tks for _ in range(G)]
    merged = sb("merged", [128, G, D], F32)
    mg_tk = [Tk(f"mg{c}") for c in range(G)]
    sgate = sb("sgate", [128, G, D], BF16)
    sgate_tk = [Tk(f"sgate{c}") for c in range(G)]
    dtt = sb("dtt", [128, G, 32], F32)
    dta = sb("dta", [128, G, 32], F32)
    csb = sb("csb", [128, G, 32], F32)
    edec = sb("edec", [128, G, 32], F32)
    escs = sb("escs", [128, G, 32], F32)
    dchk = sb("dchk", [128, G, 32], F32)
    dtw = sb("dtw", [128, G, 32], F32)
    dt_tk = Tk("dt")

    XB = 40 * 1024
    regX = sb("regX", [128, XB // 2], BF16)

    def carve(off, shape, dt):
        n = int(np.prod(shape[1:]))
        bpe = 4 if dt in (F32, I32) else 2
        assert off % 4 == 0
        a = regX[:, off // 2: off // 2 + n * bpe // 2]
        if dt != BF16:
            a = a.bitcast(dt)
        if len(shape) == 3:
            a = a.rearrange("p (a b) -> p a b", a=shape[1])
        return a, off + n * bpe

    RH = []
    off = 0
    for i in range(2):
        d = {}
        d["qT"], off = carve(off, [128, 2, T], BF16)
        d["kT"], off = carve(off, [128, 2, T], BF16)
        d["qxT"], off = carve(off, [128, 2, T], BF16)
        d["kz"], off = carve(off, [128, G, 256], BF16)
        d["v"], off = carve(off, [128, G, 512], BF16)
        d["sg"], off = carve(off, [128, G, 512], BF16)
        d["tk"] = {n: Tk(f"rh{i}{n}") for n in ("qT", "kT", "qxT")}
        d["tkc"] = {n: [Tk(f"rh{i}{n}{c}") for c in range(G)] for n in ("kz", "v", "sg")}
        RH.append(d)
    rh_end = off
    rot = [(tmpA[:, 0:T], rot_s[0][1]), (tmpB[:, 0:T], rot_s[1][1])]
    for i in range(2):
        a, off = carve(off, [128, T], F32)
        rot.append((a, Tk(f"rot{i}")))
    STb = []
    for i in range(2):
        a, off = carve(off, [128, 128], BF16)
        STb.append((a, Tk(f"ST{i}")))
    retg = []
    for i in range(2):
        a, off = carve(off, [128, 512], BF16)
        retg.append((a, Tk(f"retg{i}")))
    assert off <= XB, off
    SG = []
    off = 0
    for i in range(2):
        d = {}
        d["xsT"], off = carve(off, [128, 2, T], BF16)
        d["bmT"], off = carve(off, [128, T], BF16)
        d["cmT"], off = carve(off, [128, T], BF16)
        d["xdt"], off = carve(off, [128, G, 256], BF16)
        d["xdec"], off = carve(off, [128, G, 256], BF16)
        d["bm"], off = carve(off, [128, G, 128], BF16)
        d["sz"], off = carve(off, [128, G, 256], BF16)
        d["tk"] = {n: Tk(f"sg{i}{n}") for n in ("xsT0", "xsT1", "bmT", "cmT", "sz")}
        d["tkc"] = {n: [Tk(f"sg{i}{n}{c}") for c in range(G)] for n in ("xdt", "xdec", "bm")}
        SG.append(d)
    xpre = []
    for i in range(3):
        a, off = carve(off, [128, T + 4], BF16)
        xpre.append((a, Tk(f"xpre{i}")))
    cdiag = []
    for i in range(2):
        a, off = carve(off, [128, 4, 128], BF16)
        cdiag.append((a, Tk(f"cdiag{i}")))
    sst = []
    for i in range(2):
        d = {}
        d["R"], off = carve(off, [128, 4, 128], F32)
        d["E"] = d["R"]
        d["CBm"], off = carve(off, [128, 128], F32)
        d["MT"], off = carve(off, [128, 4, 128], BF16)
        d["yc"], off = carve(off, [128, 256], F32)
        d["yg"], off = carve(off, [128, 256], F32)
        d["yn"], off = carve(off, [128, 256], BF16)
        d["tk"] = {n: Tk(f"sst{i}{n}") for n in ("R", "CBm", "MT", "yc", "yg", "yn")}
        d["tk"]["E"] = d["tk"]["R"]
        sst.append(d)
    assert off <= XB, off
    off = 0
    hidT, off = carve(off, [128, 22, T], BF16)
    hid_tk = [Tk(f"hid{b}") for b in range(22)]
    sgt = []
    for i in range(2):
        a, off = carve(off, [128, T], F32)
        sgt.append((a, Tk(f"sgt{i}")))
    assert off <= XB, off

    def all_tks_ret():
        r = []
        for d in RH:
            r += list(d["tk"].values()) + [t for l in d["tkc"].values() for t in l]
        return r + [t for _, t in rot] + [t for _, t in STb] + [t for _, t in retg]

    def all_tks_ssd():
        r = []
        for d in SG:
            r += list(d["tk"].values()) + [t for l in d["tkc"].values() for t in l]
        for d in sst:
            r += list(d["tk"].values())
        return r + [t for _, t in xpre] + [t for _, t in cdiag]

    def all_tks_ffn():
        return hid_tk + [t for _, t in sgt]

    psum = [es.enter_context(nc.psum_tensor(f"ps{i}", [128, 512], F32)) for i in range(8)]
    ps_tk = [Tk(f"ps{i}", rw=True) for i in range(8)]
    pctr = {"a": 0, "b": 0, "p2": 0, "p4": 0, "p3": 0}
    pools = {"a": [0, 1, 2, 3], "b": [4, 5, 6, 7], "p2": [0, 1], "p4": [0, 1, 2, 3], "p3": [0, 1, 2]}
    pmode = {"proj": "a"}

    def bank(pool_name="a"):
        if pool_name == "proj":
            pool_name = pmode["proj"]
        lst = pools[pool_name]
        i = lst[pctr[pool_name] % len(lst)]
        pctr[pool_name] += 1
        return psum[i], ps_tk[i]

    def mm(out, lhsT, rhs, reads, ptk, start, stop):
        k.op(pe, reads, [ptk], lambda e: e.matmul(out, lhsT, rhs, start=start, stop=stop), signal=stop)

    def tr(out, in_, reads, ptk, last):
        idn = ident if in_.dtype == BF16 else identf
        k.op(pe, reads, [ptk], lambda e: e.transpose(out, in_, idn[:]), signal=last)

    def cs(c):
        return slice(c * 128, (c + 1) * 128)

    def rstd(col_in, col_out, n, scale, tk=None):
        tk = tk or stat_tk
        k.op(act, [tk], [tk], lambda e: e.activation(
            out=stat[:, col_out:col_out + n], in_=stat[:, col_in:col_in + n], func=AF.Ln, scale=scale, bias=epsb[:]))
        k.op(act, [tk], [tk], lambda e: e.activation(
            out=stat[:, col_out:col_out + n], in_=stat[:, col_out:col_out + n], func=AF.Exp, scale=-0.5))

    epsb = sb("epsb", [128, 1], F32)
    oneb = sb("oneb", [128, 1], F32)
    c_op(pool, lambda e: e.memset(epsb[:], EPS))
    c_op(pool, lambda e: e.memset(oneb[:], 1.0))

    def _l(t):
        return list(t) if isinstance(t, list) else [t]

    def prenorm_stats(src, src_tks, slot=0):
        k.tag = "prenorm"
        Ga = cur["G"]
        for c in range(Ga):
            k.op(act, _l(src_tks[c]), [stk_(slot)], lambda e, c=c: e.activation(
                out=junk[:], in_=src[:, c, :], func=AF.Square, accum_out=stat[:, slot + c:slot + c + 1]))
        rstd(slot, slot + 4, Ga, 1.0 / D, stk_(slot))

    def prenorm_apply(gvec, src, src_tks, c, slot=0, on_dve=False):
        k.tag = "prenorm"
        un, un_tk = un2[c % 2], un2_tk[c % 2]
        if on_dve:
            k.op(dve, _l(src_tks[c]) + [stk_(slot)], [un_tk], lambda e, c=c, un=un: e.tensor_scalar(
                un[:], src[:, c, :], stat[:, slot + 4 + c:slot + 5 + c], None, ALU.mult))
        else:
            k.op(act, _l(src_tks[c]) + [stk_(slot)], [un_tk], lambda e, c=c, un=un: e.activation(
                out=un[:], in_=src[:, c, :], func=AF.Copy, scale=stat[:, slot + 4 + c:slot + 5 + c]))
        p, ptk = bank("a")
        pb = p[:].bitcast(BF16)
        for kc in range(8):
            tr(pb[:, kc * 128:(kc + 1) * 128], un[:, kc * 128:(kc + 1) * 128], [un_tk], ptk, kc == 7)
        k.op(dve, [ptk, ctk], [uT_tk[c]], lambda e, c=c, pb=pb: e.tensor_tensor(
            uT[:, :, cs(c)], pb.rearrange("p (a b) -> p a b", a=8),
            gvec[:, :].unsqueeze(2).to_broadcast([128, 8, 128]), ALU.mult))

    def prenorm_uT(gvec, src=None, src_tks=None):
        if src is None:
            src, src_tks = hbuf, h_tk
        prenorm_stats(src, src_tks, 0)
        for c in range(cur["G"]):
            prenorm_apply(gvec, src, src_tks, c, 0)

    def rope_tables(pos0):
        k.tag = "rope"
        Ta = cur["T"]
        posi = tmpA[:, 0:Ta].bitcast(I32)
        posf = tmpB[:, 0:Ta]
        tA, tB = rot_s[0][1], rot_s[1][1]
        k.op(pool, [], [tA], lambda e: e.iota(posi, pattern=[[1, Ta]], base=pos0, channel_multiplier=0))
        k.op(dve, [tA], [tB], lambda e: e.tensor_copy(posf, posi))
        posr = rot[2][0][:, 0:Ta]
        posn = rot[3][0][:, 0:Ta]
        tR, tN = rot[2][1], rot[3][1]
        for tab, off_turn in ((sinT, 0.0), (cosT, 0.25)):
            k.op(dve, [tB, ctk], [rope_tk], lambda e, tab=tab, off_turn=off_turn: e.tensor_scalar(
                tab[:, 0:Ta], posf, invf2[:, 0:1], off_turn + 32.0, ALU.mult, ALU.add))
            k.op(dve, [rope_tk], [tR], lambda e, tab=tab: e.tensor_copy(posr.bitcast(I32), tab[:, 0:Ta]))
            k.op(dve, [tR], [tN], lambda e: e.tensor_copy(posn, posr.bitcast(I32)))
            k.op(dve, [rope_tk, tN], [rope_tk], lambda e, tab=tab: e.tensor_tensor(
                tab[:, 0:Ta], tab[:, 0:Ta], posn, ALU.subtract))
            k.op(dve, [rope_tk], [tN], lambda e, tab=tab: e.tensor_scalar(
                posn, tab[:, 0:Ta], 0.5, None, ALU.is_gt))
            k.op(dve, [rope_tk, tN], [rope_tk], lambda e, tab=tab: e.tensor_tensor(
                tab[:, 0:Ta], tab[:, 0:Ta], posn, ALU.subtract))
            k.op(act, [rope_tk], [rope_tk], lambda e, tab=tab: e.activation(
                out=tab[:, 0:Ta], in_=tab[:, 0:Ta], func=AF.Sin, scale=2 * math.pi))

    def proj_fm(slot, stk, col, nblk_list, evac):
        Ta = cur["T"]
        for idx, co in nblk_list:
            p, ptk = bank("proj")
            for kc in range(8):
                mm(p[:, 0:Ta], slot[:, kc, co:co + 128], uT[:, kc, 0:Ta], [stk] + uT_tk, ptk, kc == 0, kc == 7)
            evac(p, ptk, idx)

    def proj_tm(slot, stk, co, ncol, c, evac):
        p, ptk = bank("proj")
        for kc in range(8):
            mm(p[:, 0:ncol], uT[:, kc, cs(c)], slot[:, kc, co:co + ncol], [stk, uT_tk[c]], ptk, kc == 0, kc == 7)
        evac(p, ptk)

    def rotary(pa, patk, pb_, pbtk, out, otk):
        Ta = cur["T"]
        (m1, t1), (m2, t2), (m3, t3), (m4, t4) = rot
        k.op(dve, [patk, rope_tk], [t1], lambda e: e.tensor_tensor(m1[:, 0:Ta], pa[:, 0:Ta], cosT[:, 0:Ta], ALU.mult))
        k.op(dve, [pbtk, rope_tk], [t2], lambda e: e.tensor_tensor(m2[:, 0:Ta], pb_[:, 0:Ta], sinT[:, 0:Ta], ALU.mult))
        k.op(dve, [patk, rope_tk], [t3], lambda e: e.tensor_tensor(m3[:, 0:Ta], pa[:, 0:Ta], sinT[:, 0:Ta], ALU.mult))
        k.op(dve, [pbtk, rope_tk], [t4], lambda e: e.tensor_tensor(m4[:, 0:Ta], pb_[:, 0:Ta], cosT[:, 0:Ta], ALU.mult))
        k.op(pool, [t1, t2], [otk], lambda e: e.tensor_tensor(out[:, 0, 0:Ta], m1[:, 0:Ta], m2[:, 0:Ta], ALU.subtract))
        k.op(pool, [t3, t4], [otk], lambda e: e.tensor_tensor(out[:, 1, 0:Ta], m3[:, 0:Ta], m4[:, 0:Ta], ALU.add))

    def ret_proj(h, d):
        k.tag = "ret_proj"
        Ga, Ta = cur["G"], cur["T"]
        s3, t3 = wload([(win_b[:, C_G + h * 512:C_G + (h + 1) * 512], 0)])
        s1, t1 = wload([(win_b[:, h * 512:(h + 1) * 512], 0)])
        s2, t2 = wload([(win_b[:, C_V + h * 512:C_V + (h + 1) * 512], 0)])
        held = {}

        def ev(p, ptk, idx):
            held[idx] = (p, ptk)

        for c in range(Ga):
            proj_tm(s3, t3, 0, 512, c, lambda p, ptk, c=c: k.op(
                act, [ptk], [d["tkc"]["sg"][c]], lambda e: e.activation(out=d["sg"][:, c, :], in_=p[:, :], func=AF.Silu)))
        proj_fm(s1, t1, 0, [(0, 0), (1, 128)], ev)
        rotary(held[0][0], held[0][1], held[1][0], held[1][1], d["qT"], d["tk"]["qT"])
        for j in range(2):
            k.op(pool, [d["tk"]["qT"], ctk], [d["tk"]["qxT"]], lambda e, j=j: e.tensor_tensor(
                d["qxT"][:, j, 0:Ta].rearrange("p (c t) -> p c t", t=128),
                d["qT"][:, j, 0:Ta].rearrange("p (c t) -> p c t", t=128),
                xiT[:, h, :].unsqueeze(1).to_broadcast([128, Ga, 128]), ALU.mult))
        proj_fm(s1, t1, 0, [(2, 256), (3, 384)], ev)
        rotary(held[2][0], held[2][1], held[3][0], held[3][1], d["kT"], d["tk"]["kT"])
        for c in range(Ga):
            proj_tm(s2, t2, 0, 512, c, lambda p, ptk, c=c: k.op(
                act, [ptk], [d["tkc"]["v"][c]], lambda e: e.activation(out=d["v"][:, c, :], in_=p[:, :], func=AF.Copy)))

    def ret_projB(h, d):
        k.tag = "ret_proj"
        Ga = cur["G"]
        for c in range(Ga):
            p, ptk = bank("proj")
            pb = p[:].bitcast(BF16)
            for j in range(2):
                tr(pb[:, j * 128:(j + 1) * 128], d["kT"][:, j, cs(c)], [d["tk"]["kT"]], ptk, j == 1)
            k.op(dve, [ptk, ctk], [d["tkc"]["kz"][c]], lambda e, c=c, pb=pb: e.tensor_scalar(
                d["kz"][:, c, :], pb[:, 0:256], zeta[:, h:h + 1], None, ALU.mult))

    def ret_R1(i, h, c, d):
        k.tag = "ret_chunk"
        X, Xtk = psum[2 + i % 2], ps_tk[2 + i % 2]
        for j in range(2):
            mm(X[:, 0:128], d["kT"][:, j, cs(c)], d["qT"][:, j, cs(c)], [d["tk"]["qT"], d["tk"]["kT"]], Xtk, j == 0, j == 1)
        ST, sttk = STb[i % 2]
        k.op(dve, [Xtk, ctk], [sttk], lambda e: e.tensor_tensor(ST[:, :], X[:, 0:128], DT[:, h, :], ALU.mult))

    def ret_Ru(i, h, c, d):
        k.tag = "ret_chunk"
        gch = math.exp(128 * LOGG[h])
        for j in range(2):
            U, Utk = psum[6 + j], ps_tk[6 + j]
            mm(U[:, :], d["kz"][:, c, j * 128:(j + 1) * 128], d["v"][:, c, :],
               [d["tkc"]["kz"][c], d["tkc"]["v"][c]], Utk, True, True)
        return gch

    def ret_Ru2(i, h, c, d, gch):
        k.tag = "ret_chunk"
        for j in range(2):
            U, Utk = psum[6 + j], ps_tk[6 + j]
            ii = h * 2 + j
            k.op(dve, [Utk, rs_tk[ii]], [rs_tk[ii]], lambda e, ii=ii, U=U: e.scalar_tensor_tensor(
                rstate_f[:, ii, :], rstate_f[:, ii, :], gch, U[:, :], ALU.mult, ALU.add))
            k.op(act, [rs_tk[ii]], [rsb_tk[ii]], lambda e, ii=ii: e.activation(
                out=rstate_b[:, ii, :], in_=rstate_f[:, ii, :], func=AF.Copy))

    def ret_R2o(i, h, c, d):
        k.tag = "ret_chunk"
        O, Otk = psum[4 + i % 2], ps_tk[4 + i % 2]
        ST, sttk = STb[i % 2]
        for j in range(2):
            mm(O[:, :], d["qxT"][:, j, cs(c)], rstate_b[:, h * 2 + j, :], [d["tk"]["qxT"], rsb_tk[h * 2 + j]], Otk, j == 0, False)
        mm(O[:, :], ST[:, :], d["v"][:, c, :], [sttk, d["tkc"]["v"][c]], Otk, False, True)

    def ret_R2a(i, h, c, d):
        k.tag = "ret_chunk"
        O, Otk = psum[4 + i % 2], ps_tk[4 + i % 2]
        scol = 16 + 2 * (i % 8)
        stt = stk_(scol)
        k.op(act, [Otk], [stt], lambda e: e.activation(
            out=junk[:, 0:512], in_=O[:, :], func=AF.Square, accum_out=stat[:, scol:scol + 1]))
        rstd(scol, scol + 1, 1, 1.0 / 512, stt)

    def ret_R2(i, h, c, d):
        k.tag = "ret_chunk"
        O, Otk = psum[4 + i % 2], ps_tk[4 + i % 2]
        scol = 16 + 2 * (i % 8)
        stt = stk_(scol)
        rg, rgtk = retg[i % 2]
        k.op(dve, [Otk, stt, d["tkc"]["sg"][c]], [rgtk], lambda e: e.scalar_tensor_tensor(
            rg[:, :], O[:, :], stat[:, scol + 1:scol + 2], d["sg"][:, c, :], ALU.mult, ALU.mult))

    def ret_R3(i, h, c, d):
        k.tag = "ret_chunk"
        X, Xtk = psum[2 + i % 2], ps_tk[2 + i % 2]
        rg, rgtk = retg[i % 2]
        pb = X[:].bitcast(BF16)
        for e4 in range(4):
            tr(pb[:, 512 + e4 * 128:512 + (e4 + 1) * 128], rg[:, e4 * 128:(e4 + 1) * 128], [rgtk], Xtk, e4 == 3)
        k.op(dve, [Xtk], [retT_tk[h * 4 + e4][c] for e4 in range(4)], lambda e: e.tensor_copy(
            retT[:, h * 4:(h + 1) * 4, cs(c)], pb[:, 512:1024].rearrange("p (a b) -> p a b", a=4)))

    def retention():
        Ga = cur["G"]
        pmode["proj"] = "p2"
        iters = [(h, c) for h in range(4) for c in range(Ga)]
        N = len(iters)
        done = set()
        doneB = set()

        def ensure(h):
            if h < 4 and h not in done:
                done.add(h)
                ret_proj(h, RH[h % 2])

        def ensureB(h):
            ensure(h)
            if h < 4 and h not in doneB:
                doneB.add(h)
                ret_projB(h, RH[h % 2])

        ensure(0)

        def A(i):
            return (i, iters[i][0], iters[i][1], RH[iters[i][0] % 2])

        for r in range(-1, N + 2):
            if 0 <= r - 1 < N:
                ret_R2a(*A(r - 1))
            if 0 <= r < N:
                ensureB(iters[r][0])
                gch = ret_Ru(*A(r))
                ret_R2o(*A(r))
                ret_Ru2(*A(r), gch)
            if 0 <= r - 1 < N:
                ret_R2(*A(r - 1))
            if 0 <= r + 1 < N:
                ensure(iters[r + 1][0])
                ret_R1(*A(r + 1))
            if 0 <= r - 2 < N:
                ret_R3(*A(r - 2))
            if 0 <= r < N and iters[r][1] == 0:
                ensure(iters[r][0] + 1)
            if 0 <= r < N and iters[r][1] == min(2, Ga - 1):
                ensureB(iters[r][0] + 1)
        pmode["proj"] = "a"

    def gates_proj(col0):
        k.tag = "gates"
        for nt in range(2):
            s, stk = wload([(win_b[:, col0 + nt * 512:col0 + (nt + 1) * 512], 0)])
            for c in range(cur["G"]):
                proj_tm(s, stk, 0, 512, c, lambda p, ptk, c=c, nt=nt: k.op(
                    act, [ptk], [sgate_tk[c]], lambda e: e.activation(
                        out=sgate[:, c, nt * 512:(nt + 1) * 512], in_=p[:, :], func=AF.Sigmoid)))

    def branch(wsrc, first):
        k.tag = "branch"
        Ga = cur["G"]
        for nt in range(2):
            banks = [bank("a") for _ in range(Ga)]
            for half in range(2):
                s, stk = wload([(wsrc[half * 1024:(half + 1) * 1024, nt * 512:(nt + 1) * 512], 0)])
                for c in range(Ga):
                    p, ptk = banks[c]
                    for kb in range(8):
                        b = half * 8 + kb
                        mm(p[:, :], retT[:, b, cs(c)], s[:, kb, :], [stk, retT_tk[b][c]], ptk,
                           half == 0 and kb == 0, half == 1 and kb == 7)
            for c in range(Ga):
                p, ptk = banks[c]
                msl = merged[:, c, nt * 512:(nt + 1) * 512]
                gsl = sgate[:, c, nt * 512:(nt + 1) * 512]
                if first:
                    k.op(dve, [ptk, sgate_tk[c]], [mg_tk[c]], lambda e, p=p, msl=msl, gsl=gsl: e.tensor_tensor(
                        msl, p[:, :], gsl, ALU.mult))
                else:
                    (m1, t1) = rot_s[c % 2]
                    k.op(dve, [ptk, sgate_tk[c]], [t1], lambda e, p=p, m1=m1, gsl=gsl: e.tensor_tensor(
                        m1[:, 0:512], p[:, :], gsl, ALU.mult))
                    k.op(dve, [t1, mg_tk[c]], [mg_tk[c]], lambda e, m1=m1, msl=msl: e.tensor_tensor(
                        msl, msl, m1[:, 0:512], ALU.add))

    def ssd_dt(is_meta):
        k.tag = "ssd_dt"
        Ga = cur["G"]
        G32 = Ga * 32
        load_wdt()
        p, ptk = bank("a")
        for c in range(Ga):
            for kc in range(8):
                mm(p[:, c * 32:(c + 1) * 32], uT[:, kc, cs(c)], wdt[:, kc, :], [wdt_tk, uT_tk[c]], ptk, kc == 0, kc == 7)
        pv = p[:, 0:G32].rearrange("p (c n) -> p c n", c=Ga)
        k.op(dve, [ptk, ctk], [dt_tk], lambda e: e.tensor_tensor(
            dtw[:, 0:Ga, :], pv, dtb[:, :].unsqueeze(1).to_broadcast([128, Ga, 32]), ALU.add))
        k.op(act, [dt_tk], [dt_tk], lambda e: e.activation(out=dtw[:, 0:Ga, :], in_=dtw[:, 0:Ga, :], func=AF.Exp))
        k.op(act, [dt_tk, ctk], [dt_tk], lambda e: e.activation(
            out=dtt[:, 0:Ga, :], in_=dtw[:, 0:Ga, :], func=AF.Ln, bias=oneb[:]))
        if is_meta:
            k.op(dve, [dt_tk, ctk], [dt_tk], lambda e: e.tensor_scalar(
                dtt[:, 0:Ga, :], dtt[:, 0:Ga, :], mmask[:, 0:1], None, ALU.mult))
        k.op(dve, [dt_tk, ctk], [dt_tk], lambda e: e.tensor_tensor(
            dta[:, 0:Ga, :], dtt[:, 0:Ga, :], a_b[:, :].unsqueeze(1).to_broadcast([128, Ga, 32]), ALU.mult))
        k.op(dve, [dt_tk], [dt_tk], lambda e: e.tensor_copy(dta_hi[:, 0:Ga, :], dta[:, 0:Ga, :]))
        k.op(dve, [dt_tk], [dt_tk], lambda e: e.tensor_copy(dtw[:, 0:Ga, :], dta_hi[:, 0:Ga, :]))
        k.op(dve, [dt_tk], [dt_tk], lambda e: e.tensor_tensor(dtw[:, 0:Ga, :], dta[:, 0:Ga, :], dtw[:, 0:Ga, :], ALU.subtract))
        k.op(dve, [dt_tk], [dt_tk], lambda e: e.tensor_copy(dta_lo[:, 0:Ga, :], dtw[:, 0:Ga, :]))
        p1, p1tk = bank("a")
        flat = lambda t: t[:, 0:Ga, :].rearrange("p c n -> p (c n)")
        mm(p1[:, 0:G32], triM[:], flat(dta), [ctk, dt_tk], p1tk, True, True)
        p2, p2tk = bank("a")
        mm(p2[:, 0:G32], ones[:], flat(dta), [ctk, dt_tk], p2tk, True, True)
        k.op(act, [p1tk], [dt_tk], lambda e: e.activation(out=flat(csb), in_=p1[:, 0:G32], func=AF.Copy))
        k.op(act, [p1tk], [dt_tk], lambda e: e.activation(out=flat(escs), in_=p1[:, 0:G32], func=AF.Exp))
        k.op(act, [p2tk], [dt_tk], lambda e: e.activation(out=flat(dchk), in_=p2[:, 0:G32], func=AF.Exp))
        k.op(dve, [p2tk, dt_tk], [dt_tk], lambda e: e.tensor_tensor(flat(edec), p2[:, 0:G32], flat(csb), ALU.subtract))
        k.op(act, [dt_tk], [dt_tk], lambda e: e.activation(out=flat(edec), in_=flat(edec), func=AF.Exp))
        dump("dtt", flat(dtt), [dt_tk])
        dump("csb", flat(csb), [dt_tk])

    def conv_in(p, ptk, blk, first_grp, ci):
        k.tag = "conv"
        T = cur["T"]
        xp, xptk = xpre[ci % 3]
        if first_grp:
            k.op(pool, [], [xptk], lambda e: e.memset(xp[:, 0:4], 0.0))
        else:
            k.op(pool, [halo_tk[blk]], [xptk], lambda e: e.tensor_copy(xp[:, 0:4], halo[:, blk, :]))
        k.op(act, [ptk], [xptk], lambda e: e.activation(out=xp[:, 4:4 + T], in_=p[:, 0:T], func=AF.Copy))
        k.op(pool, [xptk], [halo_tk[blk]], lambda e: e.tensor_copy(halo[:, blk, :], xp[:, T:T + 4]))
        dg, dgtk = cdiag[ci % 2]
        for j in range(4):
            k.op(dve, [ctk], [dgtk], lambda e, j=j: e.tensor_scalar(dg[:, j, :], ident[:], cw[:, blk, j:j + 1], None, ALU.mult))
        k.tag = "ssd_proj"

    def conv_out(blk, out_ap, out_tk, ci):
        k.tag = "conv"
        T = cur["T"]
        xp, xptk = xpre[ci % 3]
        dg, dgtk = cdiag[ci % 2]
        cp, cptk = bank("proj")
        for j in range(4):
            mm(cp[:, 0:T], dg[:, j, :], xp[:, 1 + j:1 + j + T], [dgtk, xptk], cptk, j == 0, j == 3)
        k.op(act, [cptk, ctk], [out_tk], lambda e: e.activation(
            out=out_ap[:, 0:T], in_=cp[:, 0:T], func=AF.Silu, bias=cb[:, blk:blk + 1]))
        k.tag = "ssd_proj"

    def ssd_proj(g, d, first_grp):
        k.tag = "ssd_proj"
        Ga = cur["G"]
        s2, t2 = wload([(win_b[:, C_Z + g * 256:C_Z + (g + 1) * 256], 0)])
        s1, t1 = wload([(win_b[:, C_XBC + g * 512:C_XBC + (g + 1) * 512], 0)])
        outs = [(d["xsT"][:, 0, :], d["tk"]["xsT0"], g * 2), (d["xsT"][:, 1, :], d["tk"]["xsT1"], g * 2 + 1),
                (d["bmT"][:, :], d["tk"]["bmT"], 16 + g), (d["cmT"][:, :], d["tk"]["cmT"], 24 + g)]
        for c0 in range(0, Ga, 2):
            ncz = min(2, Ga - c0)
            p, ptk = bank("proj")
            for c in range(c0, c0 + ncz):
                for kc in range(8):
                    mm(p[:, (c - c0) * 256:(c - c0 + 1) * 256], uT[:, kc, cs(c)], s2[:, kc, 0:256],
                       [t2, uT_tk[c]], ptk, kc == 0, kc == 7)
            k.op(act, [ptk], [d["tk"]["sz"]], lambda e, c0=c0, p=p, ncz=ncz: e.activation(
                out=d["sz"][:, c0:c0 + ncz, :].rearrange("p c n -> p (c n)"), in_=p[:, 0:ncz * 256], func=AF.Silu))
        for b in range(5):
            if b < 4:
                def ev(p, ptk, idx, b=b):
                    conv_in(p, ptk, outs[b][2], first_grp, g * 4 + b)
                proj_fm(s1, t1, 0, [(b, b * 128)], ev)
            if b >= 1:
                conv_out(outs[b - 1][2], outs[b - 1][0], outs[b - 1][1], g * 4 + b - 1)

    def ssd_projB(g, d):
        k.tag = "ssd_proj"
        Ga = cur["G"]
        for c in range(Ga):
            p, ptk = bank("proj")
            pb = p[:].bitcast(BF16)
            for b in range(2):
                tr(pb[:, b * 128:(b + 1) * 128], d["xsT"][:, b, cs(c)], [d["tk"][f"xsT{b}"]], ptk, False)
            tr(pb[:, 256:384], d["bmT"][:, cs(c)], [d["tk"]["bmT"]], ptk, True)
            k.op(dve, [ptk, dt_tk], [d["tkc"]["xdt"][c]], lambda e, c=c, pb=pb: e.tensor_tensor(
                d["xdt"][:, c, :].rearrange("p (h q) -> p h q", h=4), pb[:, 0:256].rearrange("p (h q) -> p h q", h=4),
                dtt[:, c, g * 4:(g + 1) * 4].unsqueeze(2).to_broadcast([128, 4, 64]), ALU.mult))
            k.op(act, [ptk], [d["tkc"]["bm"][c]], lambda e, c=c, pb=pb: e.activation(
                out=d["bm"][:, c, :], in_=pb[:, 256:384], func=AF.Copy))
            k.op(pool, [d["tkc"]["xdt"][c], dt_tk], [d["tkc"]["xdec"][c]], lambda e, c=c: e.tensor_tensor(
                d["xdec"][:, c, :].rearrange("p (h q) -> p h q", h=4), d["xdt"][:, c, :].rearrange("p (h q) -> p h q", h=4),
                edec[:, c, g * 4:(g + 1) * 4].unsqueeze(2).to_broadcast([128, 4, 64]), ALU.mult))

    def ssd_P0(i, g, c, d):
        k.tag = "ssd_chunk"
        S = sst[i % 2]
        Rb = S["R"].rearrange("p h t -> p (h t)").bitcast(BF16)
        for w, src in ((0, dta_hi), (1, dta_lo)):
            k.op(pool, [ctk, dt_tk], [S["tk"]["R"]], lambda e, w=w, src=src: e.tensor_tensor(
                Rb[:, w * 512:(w + 1) * 512].rearrange("p (h t) -> p h t", h=4),
                triM_b[:, :].unsqueeze(1).to_broadcast([128, 4, 128]),
                src[:, c, g * 4:(g + 1) * 4].unsqueeze(2).to_broadcast([128, 4, 128]), ALU.mult))

    def ssd_P1(i, g, c, d):
        k.tag = "ssd_chunk"
        S = sst[i % 2]
        tk = S["tk"]
        A, Atk = psum[3], ps_tk[3]
        B, Btk = psum[4 + i % 2], ps_tk[4 + i % 2]
        mm(B[:, 0:128], d["bmT"][:, cs(c)], d["cmT"][:, cs(c)], [d["tk"]["bmT"], d["tk"]["cmT"]], Btk, True, True)
        Rb = S["R"].rearrange("p h t -> p (h t)").bitcast(BF16)
        mm(A[:, :], Umat_b[:], Rb[:, 0:512], [ctk, tk["R"]], Atk, True, False)
        mm(A[:, :], Umat_b[:], Rb[:, 512:1024], [ctk, tk["R"]], Atk, False, True)
        k.op(dve, [Btk, ctk], [tk["CBm"]], lambda e: e.tensor_tensor(S["CBm"][:, :], B[:, 0:128], causT[:], ALU.mult))
        k.op(act, [Atk], [tk["E"]], lambda e: e.activation(
            out=S["E"].rearrange("p h t -> p (h t)"), in_=A[:, :], func=AF.Exp))

    def ssd_P1b(i, g, c, d):
        k.tag = "ssd_chunk"
        S = sst[i % 2]
        tk = S["tk"]
        k.op(dve, [tk["E"], tk["CBm"]], [tk["MT"]], lambda e: e.tensor_tensor(
            S["MT"], S["E"], S["CBm"][:, :].unsqueeze(1).to_broadcast([128, 4, 128]), ALU.mult))

    def ssd_Ps(i, g, c, d):
        k.tag = "ssd_chunk"
        C, Ctk = psum[6 + i % 2], ps_tk[6 + i % 2]
        ssl = sstate_f[:, g * 256:(g + 1) * 256]
        k.op(pool, [ss_tk[g], dt_tk], [ss_tk[g]], lambda e: e.tensor_tensor(
            ssl.rearrange("p (h q) -> p h q", h=4), ssl.rearrange("p (h q) -> p h q", h=4),
            dchk[:, c, g * 4:(g + 1) * 4].unsqueeze(2).to_broadcast([128, 4, 64]), ALU.mult))
        mm(C[:, 0:256], d["bm"][:, c, :], d["xdec"][:, c, :], [d["tkc"]["bm"][c], d["tkc"]["xdec"][c]], Ctk, True, True)
        k.op(dve, [Ctk, ss_tk[g]], [ss_tk[g]], lambda e: e.tensor_tensor(ssl, ssl, C[:, 0:256], ALU.add))
        k.op(act, [ss_tk[g]], [ssb_tk[g]], lambda e: e.activation(
            out=sstate_b[:, g * 256:(g + 1) * 256], in_=ssl, func=AF.Copy))

    def ssd_P2(i, g, c, d):
        k.tag = "ssd_chunk"
        S = sst[i % 2]
        tk = S["tk"]
        B, Btk = psum[4 + i % 2], ps_tk[4 + i % 2]
        C, Ctk = psum[6 + i % 2], ps_tk[6 + i % 2]
        for b in range(2):
            mm(C[:, 256 + b * 128:256 + (b + 1) * 128], d["xsT"][:, b, cs(c)], Dd[:, g * 2 + b, :],
               [d["tk"][f"xsT{b}"], ctk], Ctk, True, False)
            for hh in (2 * b, 2 * b + 1):
                mm(C[:, 256 + hh * 64:256 + (hh + 1) * 64], S["MT"][:, hh, :], d["xdt"][:, c, hh * 64:(hh + 1) * 64],
                   [tk["MT"], d["tkc"]["xdt"][c]], Ctk, False, hh == 3)

    def ssd_Pyb(i, g, c, d):
        k.tag = "ssd_chunk"
        B, Btk = psum[4 + i % 2], ps_tk[4 + i % 2]
        mm(B[:, 128:384], d["cmT"][:, cs(c)], sstate_b[:, g * 256:(g + 1) * 256], [d["tk"]["cmT"], ssb_tk[g]], Btk, True, True)

    def ssd_P2b1(i, g, c, d):
        k.tag = "ssd_chunk"
        S = sst[i % 2]
        tk = S["tk"]
        B, Btk = psum[4 + i % 2], ps_tk[4 + i % 2]
        C, Ctk = psum[6 + i % 2], ps_tk[6 + i % 2]
        k.op(dve, [Btk, dt_tk], [tk["yc"]], lambda e: e.tensor_tensor(
            S["yc"].rearrange("p (h q) -> p h q", h=4), B[:, 128:384].rearrange("p (h q) -> p h q", h=4),
            escs[:, c, g * 4:(g + 1) * 4].unsqueeze(2).to_broadcast([128, 4, 64]), ALU.mult))
        k.op(dve, [Ctk, tk["yc"]], [tk["yg"]], lambda e: e.tensor_tensor(S["yg"][:, :], C[:, 256:512], S["yc"][:, :], ALU.add))

    def ssd_P2bg(i, g, c, d):
        k.tag = "ssd_chunk"
        S = sst[i % 2]
        tk = S["tk"]
        k.op(pool, [tk["yg"], d["tk"]["sz"]], [tk["yg"]], lambda e: e.tensor_tensor(
            S["yg"][:, :], S["yg"][:, :], d["sz"][:, c, :], ALU.mult))

    def ssd_P2b2(i, g, c, d):
        k.tag = "ssd_chunk"
        S = sst[i % 2]
        tk = S["tk"]
        scol = 32 + 2 * (i % 8)
        stt = stk_(scol)
        k.op(act, [tk["yg"]], [stt], lambda e: e.activation(
            out=junk[:, 0:256], in_=S["yg"][:, :], func=AF.Square, accum_out=stat[:, scol:scol + 1]))
        rstd(scol, scol + 1, 1, 1.0 / 256, stt)
        k.op(act, [tk["yg"], stt], [tk["yn"]], lambda e: e.activation(
            out=S["yn"][:, :], in_=S["yg"][:, :], func=AF.Copy, scale=stat[:, scol + 1:scol + 2]))

    def ssd_P3(i, g, c, d):
        k.tag = "ssd_chunk"
        S = sst[i % 2]
        tk = S["tk"]
        B, Btk = psum[4 + i % 2], ps_tk[4 + i % 2]
        pb = B[:].bitcast(BF16)
        for b in range(2):
            tr(pb[:, 768 + b * 128:768 + (b + 1) * 128], S["yn"][:, b * 128:(b + 1) * 128], [tk["yn"]], Btk, b == 1)
        k.op(dve, [Btk, ctk], [retT_tk[g * 2][c], retT_tk[g * 2 + 1][c]], lambda e: e.tensor_tensor(
            retT[:, g * 2:g * 2 + 2, cs(c)], pb[:, 768:1024].rearrange("p (a b) -> p a b", a=2),
            ssdn[:, g * 2:g * 2 + 2].unsqueeze(2).to_broadcast([128, 2, 128]), ALU.mult))

    def ssd(is_meta, first_grp):
        ssd_dt(is_meta)
        Ga = cur["G"]
        pmode["proj"] = "p3"
        iters = [(g, c) for g in range(8) for c in range(Ga)]
        N = len(iters)
        done = set()
        doneB = set()

        def ensure(g):
            if g < 8 and g not in done:
                done.add(g)
                ssd_proj(g, SG[g % 2], first_grp)

        def ensureB(g):
            ensure(g)
            if g < 8 and g not in doneB:
                doneB.add(g)
                ssd_projB(g, SG[g % 2])

        def args(i):
            g, c = iters[i]
            return (i, g, c, SG[g % 2])

        ensure(0)
        for r in range(-2, N + 2):
            if 0 <= r + 1 < N:
                ensure(iters[r + 1][0])
            if 0 <= r < N:
                ensureB(iters[r][0])
            if 0 <= r - 1 < N:
                ssd_P2b1(*args(r - 1))
            if 0 <= r < N:
                ssd_Pyb(*args(r))
                ssd_Ps(*args(r))
            if 0 <= r - 1 < N:
                ssd_P2bg(*args(r - 1))
            if 0 <= r + 2 < N:
                ssd_P0(*args(r + 2))
            if 0 <= r + 1 < N:
                ssd_P1(*args(r + 1))
            if 0 <= r - 1 < N:
                ssd_P2b2(*args(r - 1))
            if 0 <= r < N:
                ssd_P2(*args(r))
            if 0 <= r + 1 < N:
                ssd_P1b(*args(r + 1))
            if 0 <= r - 2 < N:
                ssd_P3(*args(r - 2))
            if 0 <= r < N and iters[r][1] == 0:
                ensure(iters[r][0] + 1)
            if 0 <= r < N and iters[r][1] == min(2, Ga - 1):
                ensureB(iters[r][0] + 1)
        pmode["proj"] = "a"

    def out_proj_and_residual(gvec_tile):
        k.tag = "outproj"
        Ga = cur["G"]
        for c in range(Ga):
            un, un_tk = un2[c % 2], un2_tk[c % 2]
            k.op(act, [mg_tk[c]], [un_tk], lambda e, c=c, un=un: e.activation(out=un[:], in_=merged[:, c, :], func=AF.Copy))
            p, ptk = bank("a")
            pb = p[:].bitcast(BF16)
            for kc in range(8):
                tr(pb[:, kc * 128:(kc + 1) * 128], un[:, kc * 128:(kc + 1) * 128], [un_tk], ptk, kc == 7)
            k.op(dve, [ptk], [uT_tk[c]], lambda e, c=c, pb=pb: e.tensor_copy(
                uT[:, :, cs(c)], pb.rearrange("p (a b) -> p a b", a=8)))
        slots = [wload([(wout_b[:, nt * 512:(nt + 1) * 512], 0)]) for nt in range(2)]
        for c in range(Ga):
            bks = []
            for nt in range(2):
                s, stk = slots[nt]
                p, ptk = bank("a")
                for kc in range(8):
                    mm(p[:, :], uT[:, kc, cs(c)], s[:, kc, :], [stk, uT_tk[c]], ptk, kc == 0, kc == 7)
                bks.append((p, ptk))
            norm_residual(bks, c, gvec_tile, 48 + 4 * (c % 4))

    def norm_residual(bks, c, gvec_tile, scol):
        k.tag = "norm_res"
        for nt in range(2):
            p, ptk = bks[nt]
            k.op(act, [ptk], [stk_(scol)], lambda e, p=p, nt=nt: e.activation(
                out=junk[:, 0:512], in_=p[:, :], func=AF.Square, accum_out=stat[:, scol + nt:scol + nt + 1]))
        k.op(dve, [stk_(scol)], [stk_(scol)], lambda e: e.tensor_tensor(
            stat[:, scol + 2:scol + 3], stat[:, scol:scol + 1], stat[:, scol + 1:scol + 2], ALU.add))
        rstd(scol + 2, scol + 3, 1, 1.0 / D, stk_(scol))
        for nt in range(2):
            p, ptk = bks[nt]
            (m1, t1) = rot_s[nt]
            k.op(dve, [ptk, stk_(scol), ctk], [t1], lambda e, p=p, m1=m1, nt=nt: e.scalar_tensor_tensor(
                m1[:, :], p[:, :], stat[:, scol + 3:scol + 4], gvec_tile[:, nt * 512:(nt + 1) * 512], ALU.mult, ALU.mult))
            hs = hbuf[:, c, nt * 512:(nt + 1) * 512]
            k.op(dve, [t1, h_tk[c]], [h_tk[c]], lambda e, m1=m1, hs=hs: e.tensor_tensor(hs, hs, m1[:, :], ALU.add))

    def ffn(nxt=None):
        prenorm_uT(gfpre)
        k.tag = "ffn_up"
        Ga, T = cur["G"], cur["T"]
        nfb = DFF // 128
        fb = 0
        while fb < nfb:
            nb = min(4, nfb - fb)
            sg_, sgtk_ = wload([(wgate_b[:, fb * 128:(fb + nb) * 128], 0)])
            su_, sutk_ = wload([(wup_b[:, fb * 128:(fb + nb) * 128], 0)])
            for b in range(nb):
                pg, pgtk = bank("a")
                for kc in range(8):
                    mm(pg[:, 0:T], sg_[:, kc, b * 128:(b + 1) * 128], uT[:, kc, 0:T], [sgtk_] + uT_tk, pgtk, kc == 0, kc == 7)
                pu, putk = bank("a")
                for kc in range(8):
                    mm(pu[:, 0:T], su_[:, kc, b * 128:(b + 1) * 128], uT[:, kc, 0:T], [sutk_] + uT_tk, putk, kc == 0, kc == 7)
                sg2, sg2tk = sgt[(fb + b) % 2]
                k.op(act, [pgtk], [sg2tk], lambda e, pg=pg, sg2=sg2: e.activation(out=sg2[:, 0:T], in_=pg[:, 0:T], func=AF.Silu))
                k.op(dve, [putk, sg2tk], [hid_tk[fb + b]], lambda e, pu=pu, sg2=sg2, i=fb + b: e.tensor_tensor(
                    hidT[:, i, 0:T], pu[:, 0:T], sg2[:, 0:T], ALU.mult))
            fb += nb
        segs = [(0, 8), (8, 8), (16, 6)]
        k.tag = "ffn_down"
        if nxt is not None:
            prenorm_stats(xn, xn_tkl, 8)
            k.tag = "ffn_down"
        napplied = 0
        for nt in range(2):
            banks = {c: bank("b") for c in range(Ga)}
            for si, (f0, nf) in enumerate(segs):
                s, stk = wload([(wdown_b[f0 * 128:(f0 + nf) * 128, nt * 512:(nt + 1) * 512], 0)])
                for c in range(Ga):
                    p, ptk = banks[c]
                    for kb in range(nf):
                        mm(p[:, :], hidT[:, f0 + kb, cs(c)], s[:, kb, :], [stk, hid_tk[f0 + kb]], ptk,
                           si == 0 and kb == 0, si == 2 and kb == nf - 1)
                if nxt is not None and napplied < Ga and not (nt == 0 and si == 0):
                    prenorm_apply(gpre, xn, xn_tkl, napplied, 8, on_dve=True)
                    napplied += 1
                    k.tag = "ffn_down"
            if nt == 1 and nxt is not None:
                while napplied < Ga:
                    prenorm_apply(gpre, xn, xn_tkl, napplied, 8, on_dve=True)
                    napplied += 1
                rope_tables(nxt)
                k.tag = "ffn_down"
            for c in range(Ga):
                p, ptk = banks[c]
                scol = 48 + 4 * (c % 4)
                k.op(act, [ptk], [stk_(scol)], lambda e, p=p, nt=nt, scol=scol: e.activation(
                    out=junk[:, 0:512], in_=p[:, :], func=AF.Square, accum_out=stat[:, scol + nt:scol + nt + 1]))
                k.op(act, [ptk], [mg_tk[c]], lambda e, p=p, nt=nt, c=c: e.activation(
                    out=merged[:, c, nt * 512:(nt + 1) * 512], in_=p[:, :], func=AF.Copy))
        for c in range(Ga):
            scol = 48 + 4 * (c % 4)
            k.op(dve, [stk_(scol)], [stk_(scol)], lambda e, scol=scol: e.tensor_tensor(
                stat[:, scol + 2:scol + 3], stat[:, scol:scol + 1], stat[:, scol + 1:scol + 2], ALU.add))
            rstd(scol + 2, scol + 3, 1, 1.0 / D, stk_(scol))
            k.op(dve, [mg_tk[c], stk_(scol), ctk], [mg_tk[c]], lambda e, c=c, scol=scol: e.scalar_tensor_tensor(
                merged[:, c, :], merged[:, c, :], stat[:, scol + 3:scol + 4], gfpost[:, :], ALU.mult, ALU.mult))
            k.op(dve, [mg_tk[c], h_tk[c]], [mg_tk[c]], lambda e, c=c: e.tensor_tensor(
                merged[:, c, :], merged[:, c, :], hbuf[:, c, :], ALU.add))

    out_tk = Tk("ydram")

    def zero_states():
        for i in range(8):
            k.op(pool, [], [rs_tk[i]], lambda e, i=i: e.memset(rstate_f[:, i, :], 0.0))
            k.op(pool, [], [rsb_tk[i]], lambda e, i=i: e.memset(rstate_b[:, i, :], 0.0))
            k.op(pool, [], [ss_tk[i]], lambda e, i=i: e.memset(sstate_f[:, i * 256:(i + 1) * 256], 0.0))
            k.op(pool, [], [ssb_tk[i]], lambda e, i=i: e.memset(sstate_b[:, i * 256:(i + 1) * 256], 0.0))

    def run_group(s, gi, is_meta, prefetched=False, nxt=None):
        first_grp = is_meta
        if is_meta:
            cur["G"], cur["T"] = 1, 128
            k.op(pool, [], [h_tk[0]], lambda e: e.memset(hbuf[:, 0, :], 0.0))
            k.dma(sp, hbuf[128 - NMETA:128, 0, :], meta[:, :], [], [h_tk[0]])
            pos0 = NMETA - 128
        else:
            cur["G"], cur["T"] = G, T
            r0 = gi * T
            if not prefetched:
                for c in range(G):
                    k.dma(sp, hbuf[:, c, :], x[s, r0 + c * 128:r0 + (c + 1) * 128, :], [], [h_tk[c]])
            pos0 = NMETA + r0
        if not prefetched:
            prenorm_uT(gpre)
            rope_tables(pos0)
        dump("uT", uT[:, 0, :], uT_tk)
        k.handoff(all_tks_ffn() + all_tks_ssd(), all_tks_ret())
        retention()
        dump("retT", retT[:, 0, :], [t for l in retT_tk for t in l])
        if not is_meta:
            gates_proj(C_GATE)
            branch(wret_b, True)
        dump("merged_a", merged[:, 0, :], mg_tk)
        k.handoff(all_tks_ret(), all_tks_ssd())
        ssd(is_meta, first_grp)
        dump("yT", retT[:, 0, :], [t for l in retT_tk for t in l])
        if is_meta:
            return
        gates_proj(C_GATE + 1024)
        branch(wssd_b, False)
        if nxt is not None:
            s2, gi2 = nxt
            for c in range(G):
                k.dma(pool, xn[:, c, :], x[s2, gi2 * T + c * 128:gi2 * T + (c + 1) * 128, :], [], xn_tks)
        dump("merged", merged[:, 0, :], mg_tk)
        out_proj_and_residual(gpost)
        dump("hmid", hbuf[:, 0, :], h_tk)
        k.handoff(all_tks_ssd(), all_tks_ffn())
        ffn(None if nxt is None else NMETA + nxt[1] * T)
        for c in range(G):
            k.dma(pool, y[s, r0 + c * 128:r0 + (c + 1) * 128, :], merged[:, c, :], [mg_tk[c]], [out_tk])
        if nxt is not None:
            for c in range(G):
                k.dma(pool, hbuf[:, c, :], xn[:, c, :], xn_tks, [h_tk[c]])

    sv_tk = Tk("stsave")
    for s in range(NSEQ):
        if s == 0:
            zero_states()
            run_group(s, 0, True)
            if NSEQ > 1:
                k.dma(sp, st_r[:, :], rstate_f[:].rearrange("p a b -> p (a b)"), rs_tk, [sv_tk])
                k.dma(sp, st_s[:, :], sstate_f[:, :], ss_tk, [sv_tk])
                k.dma(sp, st_h[:, :], halo[:].rearrange("p a b -> p (a b)"), halo_tk, [sv_tk])
        else:
            k.dma(sp, rstate_f[:].rearrange("p a b -> p (a b)"), st_r[:, :], [sv_tk], rs_tk)
            k.dma(sp, sstate_f[:, :], st_s[:, :], [sv_tk], ss_tk)
            k.dma(sp, halo[:].rearrange("p a b -> p (a b)"), st_h[:, :], [sv_tk], halo_tk)
            for i in range(8):
                k.op(pool, [rs_tk[i]], [rsb_tk[i]], lambda e, i=i: e.tensor_copy(rstate_b[:, i, :], rstate_f[:, i, :]))
                k.op(act, [ss_tk[i]], [ssb_tk[i]], lambda e, i=i: e.activation(
                    out=sstate_b[:, i * 256:(i + 1) * 256], in_=sstate_f[:, i * 256:(i + 1) * 256], func=AF.Copy))
        for gi in range(NGRP):
            idx = s * NGRP + gi
            nx = None
            if PREFETCH and idx + 1 < NSEQ * NGRP:
                nx = ((idx + 1) // NGRP, (idx + 1) % NGRP)
            run_group(s, gi, False, prefetched=(PREFETCH and idx > 0), nxt=nx)
    for q in (sp, pool):
        for i, sm in enumerate(q.dma_sems):
            n = (q.dma_n - i + len(q.dma_sems) - 1) // len(q.dma_sems)
            if n > 0:
                k._wait(sp, (sm, 16 * n, None))
    for e_ in (pe, act, dve, pool):
        if e_.cnt:
            k._wait(sp, (e_.sem, e_.cnt, e_))
    es.close()
    return nc, dbg_out, k


PARAM_NAMES = ["meta_tokens", "norm_mix_pre", "w_in", "conv_w", "conv_b", "dt_bias", "a_log", "d_skip", "ssd_norm",
               "w_ret_branch", "w_ssd_branch", "w_out", "norm_mix_post", "norm_ffn_pre", "w_gate", "w_up", "w_down",
               "norm_ffn_post"]


def _prep_params(inputs):
    p = {}
    for n in PARAM_NAMES:
        a = np.ascontiguousarray(np.asarray(inputs[n], dtype=np.float32))
        if n == "meta_tokens":
            p[n] = a
        elif a.ndim == 2:
            p[n] = a
        else:
            p[n] = np.ascontiguousarray(a[0])
    return p


def kernel(**inputs):
    x = np.asarray(inputs["x"], dtype=np.float32)
    B = x.shape[0]
    ncores = 8
    nseq = B // ncores
    G = 4
    ngrp = SEQ // (128 * G)
    nc, _, _ = build(nseq, ngrp, G)
    params = _prep_params(inputs)
    in_maps = []
    for i in range(ncores):
        m = dict(params)
        m["x"] = np.ascontiguousarray(x[i * nseq:(i + 1) * nseq])
        in_maps.append(m)
    res = run_bass_kernel_spmd(nc, in_maps, core_ids=list(range(ncores)))
    out = np.concatenate([np.asarray(r["y"]) for r in res.results], axis=0)
    return out.astype(np.float32)
```

```python
import math
from contextlib import ExitStack

import numpy as np
import concourse.bass as bass
import concourse.mybir as mybir
from concourse.bass_utils import run_bass_kernel_spmd

F32 = mybir.dt.float32
BF16 = mybir.dt.bfloat16
I32 = mybir.dt.int32
AF = mybir.ActivationFunctionType
ALU = mybir.AluOpType

D = 1024
NMETA = 16
SEQ = 2048
IN_W = 14368
DFF = 2816
EPS = 1e-6
C_Q, C_K, C_V, C_G, C_Z, C_XBC, C_DT, C_GATE = 0, 1024, 2048, 4096, 6144, 8192, 12288, 12320
LOGG = [math.log1p(-2.0 ** (-5.0 - h)) for h in range(4)]
NW = 3
NDMA_SEM = 8
PREFETCH = True


class Tk:
    __slots__ = ("name", "w", "r", "rw")

    def __init__(self, name, rw=False):
        self.name = name
        self.w = None
        self.r = {}
        self.rw = rw


class Eng:
    def __init__(self, name, e, sem, kind):
        self.name, self.e, self.sem, self.kind = name, e, sem, kind
        self.cnt = 0
        self.seen = {}
        self.dma_sems = []
        self.dma_n = 0


class K:
    def __init__(self, nc, es):
        self.nc, self.es = nc, es
        self.pe = self._eng("pe", nc.tensor, "pe")
        self.act = self._eng("act", nc.scalar, "c")
        self.dve = self._eng("dve", nc.vector, "c")
        self.pool = self._eng("pool", nc.gpsimd, "c")
        self.sp = self._eng("sp", nc.sync, "q")
        for q, n in ((self.sp, NDMA_SEM), (self.pool, 64)):
            q.dma_sems = [es.enter_context(nc.semaphore(f"d_{q.name}{i}")) for i in range(n)]
        self.n_instr = 0
        self.tag = "setup"
        self.tagmap = {}

    def _eng(self, name, e, kind):
        sem = self.es.enter_context(self.nc.semaphore("s_" + name))
        return Eng(name, e, sem, kind)

    def _wait(self, eng, dep):
        sem, val, src = dep
        if src is eng and eng.kind == "pe":
            return
        k = id(sem)
        if eng.seen.get(k, 0) >= val:
            return
        eng.e.wait_ge(sem, val)
        eng.seen[k] = val

    def _deps(self, eng, reads, writes):
        for t in reads:
            if t.w is not None:
                self._wait(eng, t.w)
            if t.rw:
                for d in t.r.values():
                    self._wait(eng, d)
        for t in writes:
            if t.w is not None:
                self._wait(eng, t.w)
            for d in t.r.values():
                if d[2] is eng and eng.kind != "q" and not t.rw:
                    continue
                self._wait(eng, d)

    def _record(self, reads, writes, dep):
        k = id(dep[0])
        for t in reads:
            if t.rw:
                t.w = dep
                t.r = {}
            else:
                t.r[k] = dep
        for t in writes:
            t.w = dep
            t.r = {}

    def op(self, eng, reads, writes, fn, signal=True):
        self._deps(eng, reads, writes)
        ins = fn(eng.e)
        self.n_instr += 1
        self.tagmap[ins.ins.name] = self.tag
        if signal:
            eng.cnt += 1
            ins.then_inc(eng.sem, 1)
            dep = (eng.sem, eng.cnt, eng)
        else:
            dep = (eng.sem, eng.cnt + 1, eng)
        self._record(reads, writes, dep)
        return ins

    def dma(self, q, out, in_, reads, writes, **kw):
        i = q.dma_n
        q.dma_n += 1
        ns = len(q.dma_sems)
        sem = q.dma_sems[i % ns]
        prev = 16 * (i // ns)
        if prev > 0:
            self._wait(q, (sem, prev, None))
        self._deps(q, reads, writes)
        q.e.dma_start(out=out, in_=in_, **kw).then_inc(sem, 16)
        self.n_instr += 1
        dep = (sem, prev + 16, None)
        self._record(reads, writes, dep)

    def handoff(self, olds, news):
        for n in news:
            for o in olds:
                for d in ([o.w] if o.w is not None else []) + list(o.r.values()):
                    k = id(d[0])
                    if k not in n.r or n.r[k][1] < d[1]:
                        n.r[k] = d

    def finish(self, tks):
        for t in tks:
            if t.w is not None:
                self._wait(self.sp, t.w)
            for d in t.r.values():
                self._wait(self.sp, d)


def build(NSEQ, NGRP, G, debug=()):
    T = 128 * G
    cur = {"G": G, "T": T}
    nc = bass.Bass("TRN2", target_bir_lowering=False)
    es = ExitStack()
    k = K(nc, es)
    pe, act, dve, pool, sp = k.pe, k.act, k.dve, k.pool, k.sp
    LSEQ = NGRP * T

    def din(name, shape):
        return nc.dram_tensor(name, list(shape), F32, kind="ExternalInput").ap()

    x = din("x", [NSEQ, LSEQ, D])
    meta = din("meta_tokens", [NMETA, D])
    g_mix_pre = din("norm_mix_pre", [1, D])
    w_in = din("w_in", [D, IN_W])
    conv_w = din("conv_w", [4, 4096])
    conv_b = din("conv_b", [1, 4096])
    dt_bias = din("dt_bias", [1, 32])
    a_log = din("a_log", [1, 32])
    d_skip = din("d_skip", [1, 32])
    ssd_norm = din("ssd_norm", [1, 2048])
    w_ret = din("w_ret_branch", [2048, D])
    w_ssd = din("w_ssd_branch", [2048, D])
    w_out = din("w_out", [D, D])
    g_mix_post = din("norm_mix_post", [1, D])
    g_ffn_pre = din("norm_ffn_pre", [1, D])
    w_gate = din("w_gate", [D, DFF])
    w_up = din("w_up", [D, DFF])
    w_down = din("w_down", [DFF, D])
    g_ffn_post = din("norm_ffn_post", [1, D])
    y = nc.dram_tensor("y", [NSEQ, LSEQ, D], F32, kind="ExternalOutput").ap()

    def dscr(name, shape, dt=BF16):
        return nc.dram_tensor(name, list(shape), dt, kind="Internal").ap()

    win_b = dscr("win_b", [D, IN_W])
    wret_b = dscr("wret_b", [2048, D])
    wssd_b = dscr("wssd_b", [2048, D])
    wout_b = dscr("wout_b", [D, D])
    wgate_b = dscr("wgate_b", [D, DFF])
    wup_b = dscr("wup_b", [D, DFF])
    wdown_b = dscr("wdown_b", [DFF, D])
    st_r = dscr("st_r", [128, 8 * 512], F32)
    st_s = dscr("st_s", [128, 2048], F32)
    st_h = dscr("st_h", [128, 32 * 4], BF16)
    dbg_out = {}

    def sb(name, shape, dt):
        return es.enter_context(nc.sbuf_tensor(name, list(shape), dt))

    def dump(name, ap, tks, shape=None):
        if name not in debug:
            return
        cnt = sum(1 for n in dbg_out if n.startswith(name + "#"))
        nm = f"{name}#{cnt}"
        shp = list(ap.shape)
        dt = ap.dtype
        t = nc.dram_tensor("dbg_" + nm.replace("#", "_"), shp, dt, kind="ExternalOutput").ap()
        dbg_out[nm] = t
        k.dma(sp, t, ap, tks, [dbgtk])

    dbgtk = Tk("dbg")

    ident = sb("ident", [128, 128], BF16)
    identf = sb("identf", [128, 128], F32)
    triM = sb("triM", [128, 128], F32)
    Umat = sb("Umat", [128, 128], F32)
    ones = sb("ones", [128, 128], F32)
    causT = sb("causT", [128, 128], F32)
    DT = sb("DT", [128, 4, 128], F32)
    xiT = sb("xiT", [128, 4, 128], F32)
    zeta = sb("zeta", [128, 4], F32)
    invf = sb("invf", [128, 1], F32)
    iot = sb("iot", [128, 128], I32)
    iotf = sb("iotf", [128, 128], F32)
    ipart = sb("ipart", [128, 1], I32)
    ipartf = sb("ipartf", [128, 1], F32)
    gpre = sb("gpre", [128, 8], F32)
    gfpre = sb("gfpre", [128, 8], F32)
    gpost = sb("gpost", [128, D], F32)
    gfpost = sb("gfpost", [128, D], F32)
    ssdn = sb("ssdn", [128, 16], F32)
    cw = sb("cw", [128, 32, 4], F32)
    cwraw = sb("cwraw", [128, 4, 32], F32)
    cb = sb("cb", [128, 32], F32)
    dtb = sb("dtb", [128, 32], F32)
    a_b = sb("a_b", [128, 32], F32)
    dsk = sb("dsk", [128, 16], F32)
    Dd = sb("Dd", [128, 16, 128], BF16)
    mmask = sb("mmask", [128, 1], F32)
    wdt = sb("wdt", [128, 8, 32], BF16)
    ctk = Tk("const")
    Umat_b = sb("Umat_b", [128, 128], BF16)
    triM_b = sb("triM_b", [128, 128], BF16)
    dta_hi = sb("dta_hi", [128, G, 32], BF16)
    dta_lo = sb("dta_lo", [128, G, 32], BF16)

    def c_op(eng, fn, extra_r=(), extra_w=()):
        k.op(eng, [ctk] + list(extra_r), [ctk] + list(extra_w), fn)

    c_op(pool, lambda e: e.memset(identf[:], 0.0))
    c_op(pool, lambda e: e.affine_select(out=identf[:], in_=identf[:], compare_op=ALU.not_equal, fill=1.0,
                                         base=0, pattern=[[-1, 128]], channel_multiplier=1))
    c_op(dve, lambda e: e.tensor_copy(ident[:], identf[:]))
    c_op(pool, lambda e: e.memset(triM[:], 1.0))
    c_op(pool, lambda e: e.affine_select(out=triM[:], in_=triM[:], compare_op=ALU.is_ge, fill=0.0,
                                         base=0, pattern=[[1, 128]], channel_multiplier=-1))
    c_op(dve, lambda e: e.tensor_copy(causT[:], triM[:]))
    c_op(pool, lambda e: e.memset(Umat[:], 1.0))
    c_op(pool, lambda e: e.affine_select(out=Umat[:], in_=Umat[:], compare_op=ALU.is_gt, fill=0.0,
                                         base=0, pattern=[[-1, 128]], channel_multiplier=1))
    c_op(pool, lambda e: e.memset(ones[:], 1.0))
    c_op(dve, lambda e: e.tensor_copy(Umat_b[:], Umat[:]))
    c_op(dve, lambda e: e.tensor_copy(triM_b[:], triM[:]))
    c_op(pool, lambda e: e.iota(iot[:], pattern=[[1, 128]], base=0, channel_multiplier=-1))
    c_op(dve, lambda e: e.tensor_copy(iotf[:], iot[:]))
    c_op(pool, lambda e: e.iota(ipart[:], pattern=[[0, 1]], base=0, channel_multiplier=1))
    c_op(dve, lambda e: e.tensor_copy(ipartf[:], ipart[:]))
    for h in range(4):
        c_op(act, lambda e, h=h: e.activation(out=DT[:, h, :], in_=iotf[:], func=AF.Exp, scale=LOGG[h]))
        c_op(dve, lambda e, h=h: e.scalar_tensor_tensor(DT[:, h, :], DT[:, h, :], 1.0 / 16, causT[:],
                                                        ALU.mult, ALU.mult))
    tix = sb("tix", [128, 128], I32)
    tixf = sb("tixf", [128, 128], F32)
    c_op(pool, lambda e: e.iota(tix[:], pattern=[[1, 128]], base=1, channel_multiplier=0))
    c_op(dve, lambda e: e.tensor_copy(tixf[:], tix[:]))
    for h in range(4):
        c_op(act, lambda e, h=h: e.activation(out=xiT[:, h, :], in_=tixf[:], func=AF.Exp, scale=LOGG[h]))
        c_op(act, lambda e, h=h: e.activation(out=zeta[:, h:h + 1], in_=ipartf[:], func=AF.Exp, scale=-LOGG[h]))
        c_op(dve, lambda e, h=h: e.tensor_scalar(zeta[:, h:h + 1], zeta[:, h:h + 1],
                                                 math.exp(127 * LOGG[h]) / 16.0, None, ALU.mult))
    c_op(act, lambda e: e.activation(out=invf[:], in_=ipartf[:], func=AF.Exp, scale=-math.log(10000.0) / 128))
    invf2 = sb("invf2", [128, 1], F32)
    c_op(dve, lambda e: e.tensor_scalar(invf2[:], invf[:], 1.0 / (2 * math.pi), None, ALU.mult))
    c_op(pool, lambda e: e.memset(mmask[:], 1.0))
    c_op(pool, lambda e: e.affine_select(out=mmask[:], in_=mmask[:], compare_op=ALU.is_ge, fill=0.0,
                                         base=-(128 - NMETA), pattern=[[0, 1]], channel_multiplier=1))
    k.dma(sp, gpre[:], g_mix_pre[0].rearrange("(k p) -> p k", p=128), [], [ctk], allow_slow_non_contiguous=True)
    k.dma(sp, gfpre[:], g_ffn_pre[0].rearrange("(k p) -> p k", p=128), [], [ctk], allow_slow_non_contiguous=True)
    k.dma(sp, ssdn[:], ssd_norm[0].rearrange("(k p) -> p k", p=128), [], [ctk], allow_slow_non_contiguous=True)
    k.dma(sp, cb[:], conv_b[0].rearrange("(k p) -> p k", p=128), [], [ctk], allow_slow_non_contiguous=True)
    for j in range(4):
        k.dma(sp, cwraw[:, j, :], conv_w[j].rearrange("(k p) -> p k", p=128), [], [ctk],
              allow_slow_non_contiguous=True)
    k.dma(sp, gpost[:], g_mix_post[0].partition_broadcast(128), [], [ctk])
    k.dma(sp, gfpost[:], g_ffn_post[0].partition_broadcast(128), [], [ctk])
    k.dma(sp, dtb[:], dt_bias[0].partition_broadcast(128), [], [ctk])
    k.dma(sp, a_b[:], a_log[0].partition_broadcast(128), [], [ctk])
    for hh in range(2):
        k.dma(sp, dsk[hh * 64:(hh + 1) * 64, :], d_skip[0].rearrange("(b two) -> two b", two=2)[hh]
              .partition_broadcast(64), [], [ctk], allow_slow_non_contiguous=True)
    c_op(dve, lambda e: e.tensor_copy(cw[:].rearrange("p b j -> p j b"), cwraw[:]))
    c_op(act, lambda e: e.activation(out=a_b[:], in_=a_b[:], func=AF.Exp))
    c_op(dve, lambda e: e.tensor_scalar(a_b[:], a_b[:], -1.0, None, ALU.mult))
    for b in range(16):
        c_op(dve, lambda e, b=b: e.tensor_scalar(Dd[:, b, :], identf[:], dsk[:, b:b + 1], None, ALU.mult))

    wcast = {}

    def cast(dst, src, rows, cols):
        RB = 256
        lst = []
        for r0 in range(0, rows, RB):
            r1 = min(rows, r0 + RB)
            t = Tk(f"wc{len(wcast)}_{r0}")
            k.dma(pool, dst[r0:r1, :], src[r0:r1, :], [], [t])
            lst.append(t)
        wcast[dst.tensor.name] = lst

    wcast_cols = []

    def cast_cols(c0, c1, d0=None):
        d0 = c0 if d0 is None else d0
        t = Tk(f"wcc{d0}")
        k.dma(pool, win_b[:, d0:d0 + (c1 - c0)], w_in[:, c0:c1], [ctk] if not wcast_cols else [], [t])
        wcast_cols.append((d0, d0 + (c1 - c0), t))

    for h in range(4):
        cast_cols(C_Q + h * 256, C_Q + (h + 1) * 256, h * 512)
        cast_cols(C_K + h * 256, C_K + (h + 1) * 256, h * 512 + 256)
        cast_cols(C_V + h * 512, C_V + (h + 1) * 512)
        cast_cols(C_G + h * 512, C_G + (h + 1) * 512)
    cast_cols(C_DT, C_DT + 32)
    for g in range(8):
        cast_cols(C_XBC + g * 256, C_XBC + (g + 1) * 256, C_XBC + g * 512)
        cast_cols(C_XBC + 2048 + g * 128, C_XBC + 2048 + (g + 1) * 128, C_XBC + g * 512 + 256)
        cast_cols(C_XBC + 3072 + g * 128, C_XBC + 3072 + (g + 1) * 128, C_XBC + g * 512 + 384)
    cast_cols(C_Z, C_Z + 2048)
    cast_cols(C_GATE, C_GATE + 2048)

    def wdeps(src):
        if src.tensor.name == win_b.tensor.name:
            c0 = int(src.offset) % IN_W
            c1 = c0 + src.shape[1]
            return [t for (a_, b_, t) in wcast_cols if a_ < c1 and b_ > c0]
        return wcast[src.tensor.name]
    cast(wret_b, w_ret, 2048, D)
    cast(wssd_b, w_ssd, 2048, D)
    cast(wout_b, w_out, D, D)
    cast(wgate_b, w_gate, D, DFF)
    cast(wup_b, w_up, D, DFF)
    cast(wdown_b, w_down, DFF, D)
    wdt_tk = Tk("wdt")
    wdt_loaded = [False]

    def load_wdt():
        if not wdt_loaded[0]:
            wdt_loaded[0] = True
            k.dma(sp, wdt[:], win_b[:, C_DT:C_DT + 32].rearrange("(k p) n -> p k n", p=128),
                  wdeps(win_b[:, C_DT:C_DT + 32]), [wdt_tk])

    rstate_f = sb("rstate_f", [128, 8, 512], F32)
    rstate_b = sb("rstate_b", [128, 8, 512], BF16)
    sstate_f = sb("sstate_f", [128, 2048], F32)
    sstate_b = sb("sstate_b", [128, 2048], BF16)
    halo = sb("halo", [128, 32, 4], BF16)
    rs_tk = [Tk(f"rs{i}") for i in range(8)]
    rsb_tk = [Tk(f"rsb{i}") for i in range(8)]
    ss_tk = [Tk(f"ss{i}") for i in range(8)]
    ssb_tk = [Tk(f"ssb{i}") for i in range(8)]
    halo_tk = [Tk(f"halo{i}") for i in range(32)]

    wslot = [sb(f"wslot{i}", [128, 8, 512], BF16) for i in range(NW)]
    wslot_tk = [Tk(f"wslot{i}") for i in range(NW)]
    wctr = [0]

    def wload(pieces):
        i = wctr[0] % NW
        wctr[0] += 1
        s, tk = wslot[i], wslot_tk[i]
        for src, off in pieces:
            nk = src.shape[0] // 128
            ncol = src.shape[1]
            k.dma(sp, s[:, 0:nk, off:off + ncol], src.rearrange("(k p) n -> p k n", p=128), wdeps(src), [tk])
        return s, tk

    hbuf = sb("hbuf", [128, G, D], F32)
    h_tk = [Tk(f"h{c}") for c in range(G)]
    uT = sb("uT", [128, 8, T], BF16)
    uT_tk = [Tk(f"uT{c}") for c in range(G)]
    un2 = [sb("un0", [128, D], BF16), sb("un1", [128, D], BF16)]
    un2_tk = [Tk("un0"), Tk("un1")]
    un, un_tk = un2[0], un2_tk[0]
    junk = sb("junk", [128, D], BF16)
    junk_tk = Tk("junk")
    stat = sb("stat", [128, 64], F32)
    stat_tk = Tk("stat")
    stat_slots = {}

    def stk_(col):
        if col not in stat_slots:
            stat_slots[col] = Tk(f"stat{col}")
        return stat_slots[col]
    tmpA = sb("tmpA", [128, 512], F32)
    tmpB = sb("tmpB", [128, 512], F32)
    rot_s = [(tmpA, Tk("tmpA")), (tmpB, Tk("tmpB"))]
    cosT = sb("cosT", [128, T], F32)
    sinT = sb("sinT", [128, T], F32)
    rope_tk = Tk("rope")
    retT = sb("retT", [128, 16, T], BF16)
    retT_tk = [[Tk(f"retT{b}_{c}") for c in range(G)] for b in range(16)]
    xn = retT[:].rearrange("p a b -> p (a b)").bitcast(F32).rearrange("p (c d) -> p c d", c=G)
    xn_tks = [t for l in retT_tk for t in l]
    xn_tkl = [xn_tks for _ in range(G)]
    merged = sb("merged", [128, G, D], F32)
    mg_tk = [Tk(f"mg{c}") for c in range(G)]
    sgate = sb("sgate", [128, G, D], BF16)
    sgate_tk = [Tk(f"sgate{c}") for c in range(G)]
    dtt = sb("dtt", [128, G, 32], F32)
    dta = sb("dta", [128, G, 32], F32)
    csb = sb("csb", [128, G, 32], F32)
    edec = sb("edec", [128, G, 32], F32)
    escs = sb("escs", [128, G, 32], F32)
    dchk = sb("dchk", [128, G, 32], F32)
    dtw = sb("dtw", [128, G, 32], F32)
    dt_tk = Tk("dt")

    XB = 40 * 1024
    regX = sb("regX", [128, XB // 2], BF16)

    def carve(off, shape, dt):
        n = int(np.prod(shape[1:]))
        bpe = 4 if dt in (F32, I32) else 2
        assert off % 4 == 0
        a = regX[:, off // 2: off // 2 + n * bpe // 2]
        if dt != BF16:
            a = a.bitcast(dt)
        if len(shape) == 3:
            a = a.rearrange("p (a b) -> p a b", a=shape[1])
        return a, off + n * bpe

    RH = []
    off = 0
    for i in range(2):
        d = {}
        d["qT"], off = carve(off, [128, 2, T], BF16)
        d["kT"], off = carve(off, [128, 2, T], BF16)
        d["qxT"], off = carve(off, [128, 2, T], BF16)
        d["kz"], off = carve(off, [128, G, 256], BF16)
        d["v"], off = carve(off, [128, G, 512], BF16)
        d["sg"], off = carve(off, [128, G, 512], BF16)
        d["tk"] = {n: Tk(f"rh{i}{n}") for n in ("qT", "kT", "qxT")}
        d["tkc"] = {n: [Tk(f"rh{i}{n}{c}") for c in range(G)] for n in ("kz", "v", "sg")}
        RH.append(d)
    rh_end = off
    rot = [(tmpA[:, 0:T], rot_s[0][1]), (tmpB[:, 0:T], rot_s[1][1])]
    for i in range(2):
        a, off = carve(off, [128, T], F32)
        rot.append((a, Tk(f"rot{i}")))
    STb = []
    for i in range(2):
        a, off = carve(off, [128, 128], BF16)
        STb.append((a, Tk(f"ST{i}")))
    retg = []
    for i in range(2):
        a, off = carve(off, [128, 512], BF16)
        retg.append((a, Tk(f"retg{i}")))
    assert off <= XB, off
    SG = []
    off = 0
    for i in range(2):
        d = {}
        d["xsT"], off = carve(off, [128, 2, T], BF16)
        d["bmT"], off = carve(off, [128, T], BF16)
        d["cmT"], off = carve(off, [128, T], BF16)
        d["xdt"], off = carve(off, [128, G, 256], BF16)
        d["xdec"], off = carve(off, [128, G, 256], BF16)
        d["bm"], off = carve(off, [128, G, 128], BF16)
        d["sz"], off = carve(off, [128, G, 256], BF16)
        d["tk"] = {n: Tk(f"sg{i}{n}") for n in ("xsT0", "xsT1", "bmT", "cmT", "sz")}
        d["tkc"] = {n: [Tk(f"sg{i}{n}{c}") for c in range(G)] for n in ("xdt", "xdec", "bm")}
        SG.append(d)
    xpre = []
    for i in range(3):
        a, off = carve(off, [128, T + 4], BF16)
        xpre.append((a, Tk(f"xpre{i}")))
    cdiag = []
    for i in range(2):
        a, off = carve(off, [128, 4, 128], BF16)
        cdiag.append((a, Tk(f"cdiag{i}")))
    sst = []
    for i in range(2):
        d = {}
        d["R"], off = carve(off, [128, 4, 128], F32)
        d["E"] = d["R"]
        d["CBm"], off = carve(off, [128, 128], F32)
        d["MT"], off = carve(off, [128, 4, 128], BF16)
        d["yc"], off = carve(off, [128, 256], F32)
        d["yg"], off = carve(off, [128, 256], F32)
        d["yn"], off = carve(off, [128, 256], BF16)
        d["tk"] = {n: Tk(f"sst{i}{n}") for n in ("R", "CBm", "MT", "yc", "yg", "yn")}
        d["tk"]["E"] = d["tk"]["R"]
        sst.append(d)
    assert off <= XB, off
    off = 0
    hidT, off = carve(off, [128, 22, T], BF16)
    hid_tk = [Tk(f"hid{b}") for b in range(22)]
    sgt = []
    for i in range(2):
        a, off = carve(off, [128, T], F32)
        sgt.append((a, Tk(f"sgt{i}")))
    assert off <= XB, off

    def all_tks_ret():
        r = []
        for d in RH:
            r += list(d["tk"].values()) + [t for l in d["tkc"].values() for t in l]
        return r + [t for _, t in rot] + [t for _, t in STb] + [t for _, t in retg]

    def all_tks_ssd():
        r = []
        for d in SG:
            r += list(d["tk"].values()) + [t for l in d["tkc"].values() for t in l]
        for d in sst:
            r += list(d["tk"].values())
        return r + [t for _, t in xpre] + [t for _, t in cdiag]

    def all_tks_ffn():
        return hid_tk + [t for _, t in sgt]

    psum = [es.enter_context(nc.psum_tensor(f"ps{i}", [128, 512], F32)) for i in range(8)]
    ps_tk = [Tk(f"ps{i}", rw=True) for i in range(8)]
    pctr = {"a": 0, "b": 0, "p2": 0, "p4": 0, "p3": 0}
    pools = {"a": [0, 1, 2, 3], "b": [4, 5, 6, 7], "p2": [0, 1], "p4": [0, 1, 2, 3], "p3": [0, 1, 2]}
    pmode = {"proj": "a"}

    def bank(pool_name="a"):
        if pool_name == "proj":
            pool_name = pmode["proj"]
        lst = pools[pool_name]
        i = lst[pctr[pool_name] % len(lst)]
        pctr[pool_name] += 1
        return psum[i], ps_tk[i]

    def mm(out, lhsT, rhs, reads, ptk, start, stop):
        k.op(pe, reads, [ptk], lambda e: e.matmul(out, lhsT, rhs, start=start, stop=stop), signal=stop)

    def tr(out, in_, reads, ptk, last):
        idn = ident if in_.dtype == BF16 else identf
        k.op(pe, reads, [ptk], lambda e: e.transpose(out, in_, idn[:]), signal=last)

    def cs(c):
        return slice(c * 128, (c + 1) * 128)

    def rstd(col_in, col_out, n, scale, tk=None):
        tk = tk or stat_tk
        k.op(act, [tk], [tk], lambda e: e.activation(
            out=stat[:, col_out:col_out + n], in_=stat[:, col_in:col_in + n], func=AF.Ln, scale=scale, bias=epsb[:]))
        k.op(act, [tk], [tk], lambda e: e.activation(
            out=stat[:, col_out:col_out + n], in_=stat[:, col_out:col_out + n], func=AF.Exp, scale=-0.5))

    epsb = sb("epsb", [128, 1], F32)
    oneb = sb("oneb", [128, 1], F32)
    c_op(pool, lambda e: e.memset(epsb[:], EPS))
    c_op(pool, lambda e: e.memset(oneb[:], 1.0))

    def _l(t):
        return list(t) if isinstance(t, list) else [t]

    def prenorm_stats(src, src_tks, slot=0):
        k.tag = "prenorm"
        Ga = cur["G"]
        for c in range(Ga):
            k.op(act, _l(src_tks[c]), [stk_(slot)], lambda e, c=c: e.activation(
                out=junk[:], in_=src[:, c, :], func=AF.Square, accum_out=stat[:, slot + c:slot + c + 1]))
        rstd(slot, slot + 4, Ga, 1.0 / D, stk_(slot))

    def prenorm_apply(gvec, src, src_tks, c, slot=0, on_dve=False):
        k.tag = "prenorm"
        un, un_tk = un2[c % 2], un2_tk[c % 2]
        if on_dve:
            k.op(dve, _l(src_tks[c]) + [stk_(slot)], [un_tk], lambda e, c=c, un=un: e.tensor_scalar(
                un[:], src[:, c, :], stat[:, slot + 4 + c:slot + 5 + c], None, ALU.mult))
        else:
            k.op(act, _l(src_tks[c]) + [stk_(slot)], [un_tk], lambda e, c=c, un=un: e.activation(
                out=un[:], in_=src[:, c, :], func=AF.Copy, scale=stat[:, slot + 4 + c:slot + 5 + c]))
        p, ptk = bank("a")
        pb = p[:].bitcast(BF16)
        for kc in range(8):
            tr(pb[:, kc * 128:(kc + 1) * 128], un[:, kc * 128:(kc + 1) * 128], [un_tk], ptk, kc == 7)
        k.op(dve, [ptk, ctk], [uT_tk[c]], lambda e, c=c, pb=pb: e.tensor_tensor(
            uT[:, :, cs(c)], pb.rearrange("p (a b) -> p a b", a=8),
            gvec[:, :].unsqueeze(2).to_broadcast([128, 8, 128]), ALU.mult))

    def prenorm_uT(gvec, src=None, src_tks=None):
        if src is None:
            src, src_tks = hbuf, h_tk
        prenorm_stats(src, src_tks, 0)
        for c in range(cur["G"]):
            prenorm_apply(gvec, src, src_tks, c, 0)

    def rope_tables(pos0):
        k.tag = "rope"
        Ta = cur["T"]
        posi = tmpA[:, 0:Ta].bitcast(I32)
        posf = tmpB[:, 0:Ta]
        tA, tB = rot_s[0][1], rot_s[1][1]
        k.op(pool, [], [tA], lambda e: e.iota(posi, pattern=[[1, Ta]], base=pos0, channel_multiplier=0))
        k.op(dve, [tA], [tB], lambda e: e.tensor_copy(posf, posi))
        posr = rot[2][0][:, 0:Ta]
        posn = rot[3][0][:, 0:Ta]
        tR, tN = rot[2][1], rot[3][1]
        for tab, off_turn in ((sinT, 0.0), (cosT, 0.25)):
            k.op(dve, [tB, ctk], [rope_tk], lambda e, tab=tab, off_turn=off_turn: e.tensor_scalar(
                tab[:, 0:Ta], posf, invf2[:, 0:1], off_turn + 32.0, ALU.mult, ALU.add))
            k.op(dve, [rope_tk], [tR], lambda e, tab=tab: e.tensor_copy(posr.bitcast(I32), tab[:, 0:Ta]))
            k.op(dve, [tR], [tN], lambda e: e.tensor_copy(posn, posr.bitcast(I32)))
            k.op(dve, [rope_tk, tN], [rope_tk], lambda e, tab=tab: e.tensor_tensor(
                tab[:, 0:Ta], tab[:, 0:Ta], posn, ALU.subtract))
            k.op(dve, [rope_tk], [tN], lambda e, tab=tab: e.tensor_scalar(
                posn, tab[:, 0:Ta], 0.5, None, ALU.is_gt))
            k.op(dve, [rope_tk, tN], [rope_tk], lambda e, tab=tab: e.tensor_tensor(
                tab[:, 0:Ta], tab[:, 0:Ta], posn, ALU.subtract))
            k.op(act, [rope_tk], [rope_tk], lambda e, tab=tab: e.activation(
                out=tab[:, 0:Ta], in_=tab[:, 0:Ta], func=AF.Sin, scale=2 * math.pi))

    def proj_fm(slot, stk, col, nblk_list, evac):
        Ta = cur["T"]
        for idx, co in nblk_list:
            p, ptk = bank("proj")
            for kc in range(8):
                mm(p[:, 0:Ta], slot[:, kc, co:co + 128], uT[:, kc, 0:Ta], [stk] + uT_tk, ptk, kc == 0, kc == 7)
            evac(p, ptk, idx)

    def proj_tm(slot, stk, co, ncol, c, evac):
        p, ptk = bank("proj")
        for kc in range(8):
            mm(p[:, 0:ncol], uT[:, kc, cs(c)], slot[:, kc, co:co + ncol], [stk, uT_tk[c]], ptk, kc == 0, kc == 7)
        evac(p, ptk)

    def rotary(pa, patk, pb_, pbtk, out, otk):
        Ta = cur["T"]
        (m1, t1), (m2, t2), (m3, t3), (m4, t4) = rot
        k.op(dve, [patk, rope_tk], [t1], lambda e: e.tensor_tensor(m1[:, 0:Ta], pa[:, 0:Ta], cosT[:, 0:Ta], ALU.mult))
        k.op(dve, [pbtk, rope_tk], [t2], lambda e: e.tensor_tensor(m2[:, 0:Ta], pb_[:, 0:Ta], sinT[:, 0:Ta], ALU.mult))
        k.op(dve, [patk, rope_tk], [t3], lambda e: e.tensor_tensor(m3[:, 0:Ta], pa[:, 0:Ta], sinT[:, 0:Ta], ALU.mult))
        k.op(dve, [pbtk, rope_tk], [t4], lambda e: e.tensor_tensor(m4[:, 0:Ta], pb_[:, 0:Ta], cosT[:, 0:Ta], ALU.mult))
        k.op(pool, [t1, t2], [otk], lambda e: e.tensor_tensor(out[:, 0, 0:Ta], m1[:, 0:Ta], m2[:, 0:Ta], ALU.subtract))
        k.op(pool, [t3, t4], [otk], lambda e: e.tensor_tensor(out[:, 1, 0:Ta], m3[:, 0:Ta], m4[:, 0:Ta], ALU.add))

    def ret_proj(h, d):
        k.tag = "ret_proj"
        Ga, Ta = cur["G"], cur["T"]
        s3, t3 = wload([(win_b[:, C_G + h * 512:C_G + (h + 1) * 512], 0)])
        s1, t1 = wload([(win_b[:, h * 512:(h + 1) * 512], 0)])
        s2, t2 = wload([(win_b[:, C_V + h * 512:C_V + (h + 1) * 512], 0)])
        held = {}

        def ev(p, ptk, idx):
            held[idx] = (p, ptk)

        for c in range(Ga):
            proj_tm(s3, t3, 0, 512, c, lambda p, ptk, c=c: k.op(
                act, [ptk], [d["tkc"]["sg"][c]], lambda e: e.activation(out=d["sg"][:, c, :], in_=p[:, :], func=AF.Silu)))
        proj_fm(s1, t1, 0, [(0, 0), (1, 128)], ev)
        rotary(held[0][0], held[0][1], held[1][0], held[1][1], d["qT"], d["tk"]["qT"])
        for j in range(2):
            k.op(pool, [d["tk"]["qT"], ctk], [d["tk"]["qxT"]], lambda e, j=j: e.tensor_tensor(
                d["qxT"][:, j, 0:Ta].rearrange("p (c t) -> p c t", t=128),
                d["qT"][:, j, 0:Ta].rearrange("p (c t) -> p c t", t=128),
                xiT[:, h, :].unsqueeze(1).to_broadcast([128, Ga, 128]), ALU.mult))
        proj_fm(s1, t1, 0, [(2, 256), (3, 384)], ev)
        rotary(held[2][0], held[2][1], held[3][0], held[3][1], d["kT"], d["tk"]["kT"])
        for c in range(Ga):
            proj_tm(s2, t2, 0, 512, c, lambda p, ptk, c=c: k.op(
                act, [ptk], [d["tkc"]["v"][c]], lambda e: e.activation(out=d["v"][:, c, :], in_=p[:, :], func=AF.Copy)))

    def ret_projB(h, d):
        k.tag = "ret_proj"
        Ga = cur["G"]
        for c in range(Ga):
            p, ptk = bank("proj")
            pb = p[:].bitcast(BF16)
            for j in range(2):
                tr(pb[:, j * 128:(j + 1) * 128], d["kT"][:, j, cs(c)], [d["tk"]["kT"]], ptk, j == 1)
            k.op(dve, [ptk, ctk], [d["tkc"]["kz"][c]], lambda e, c=c, pb=pb: e.tensor_scalar(
                d["kz"][:, c, :], pb[:, 0:256], zeta[:, h:h + 1], None, ALU.mult))

    def ret_R1(i, h, c, d):
        k.tag = "ret_chunk"
        X, Xtk = psum[2 + i % 2], ps_tk[2 + i % 2]
        for j in range(2):
            mm(X[:, 0:128], d["kT"][:, j, cs(c)], d["qT"][:, j, cs(c)], [d["tk"]["qT"], d["tk"]["kT"]], Xtk, j == 0, j == 1)
        ST, sttk = STb[i % 2]
        k.op(dve, [Xtk, ctk], [sttk], lambda e: e.tensor_tensor(ST[:, :], X[:, 0:128], DT[:, h, :], ALU.mult))

    def ret_Ru(i, h, c, d):
        k.tag = "ret_chunk"
        gch = math.exp(128 * LOGG[h])
        for j in range(2):
            U, Utk = psum[6 + j], ps_tk[6 + j]
            mm(U[:, :], d["kz"][:, c, j * 128:(j + 1) * 128], d["v"][:, c, :],
               [d["tkc"]["kz"][c], d["tkc"]["v"][c]], Utk, True, True)
        return gch

    def ret_Ru2(i, h, c, d, gch):
        k.tag = "ret_chunk"
        for j in range(2):
            U, Utk = psum[6 + j], ps_tk[6 + j]
            ii = h * 2 + j
            k.op(dve, [Utk, rs_tk[ii]], [rs_tk[ii]], lambda e, ii=ii, U=U: e.scalar_tensor_tensor(
                rstate_f[:, ii, :], rstate_f[:, ii, :], gch, U[:, :], ALU.mult, ALU.add))
            k.op(act, [rs_tk[ii]], [rsb_tk[ii]], lambda e, ii=ii: e.activation(
                out=rstate_b[:, ii, :], in_=rstate_f[:, ii, :], func=AF.Copy))

    def ret_R2o(i, h, c, d):
        k.tag = "ret_chunk"
        O, Otk = psum[4 + i % 2], ps_tk[4 + i % 2]
        ST, sttk = STb[i % 2]
        for j in range(2):
            mm(O[:, :], d["qxT"][:, j, cs(c)], rstate_b[:, h * 2 + j, :], [d["tk"]["qxT"], rsb_tk[h * 2 + j]], Otk, j == 0, False)
        mm(O[:, :], ST[:, :], d["v"][:, c, :], [sttk, d["tkc"]["v"][c]], Otk, False, True)

    def ret_R2a(i, h, c, d):
        k.tag = "ret_chunk"
        O, Otk = psum[4 + i % 2], ps_tk[4 + i % 2]
        scol = 16 + 2 * (i % 8)
        stt = stk_(scol)
        k.op(act, [Otk], [stt], lambda e: e.activation(
            out=junk[:, 0:512], in_=O[:, :], func=AF.Square, accum_out=stat[:, scol:scol + 1]))
        rstd(scol, scol + 1, 1, 1.0 / 512, stt)

    def ret_R2(i, h, c, d):
        k.tag = "ret_chunk"
        O, Otk = psum[4 + i % 2], ps_tk[4 + i % 2]
        scol = 16 + 2 * (i % 8)
        stt = stk_(scol)
        rg, rgtk = retg[i % 2]
        k.op(dve, [Otk, stt, d["tkc"]["sg"][c]], [rgtk], lambda e: e.scalar_tensor_tensor(
            rg[:, :], O[:, :], stat[:, scol + 1:scol + 2], d["sg"][:, c, :], ALU.mult, ALU.mult))

    def ret_R3(i, h, c, d):
        k.tag = "ret_chunk"
        X, Xtk = psum[2 + i % 2], ps_tk[2 + i % 2]
        rg, rgtk = retg[i % 2]
        pb = X[:].bitcast(BF16)
        for e4 in range(4):
            tr(pb[:, 512 + e4 * 128:512 + (e4 + 1) * 128], rg[:, e4 * 128:(e4 + 1) * 128], [rgtk], Xtk, e4 == 3)
        k.op(dve, [Xtk], [retT_tk[h * 4 + e4][c] for e4 in range(4)], lambda e: e.tensor_copy(
            retT[:, h * 4:(h + 1) * 4, cs(c)], pb[:, 512:1024].rearrange("p (a b) -> p a b", a=4)))

    def retention():
        Ga = cur["G"]
        pmode["proj"] = "p2"
        iters = [(h, c) for h in range(4) for c in range(Ga)]
        N = len(iters)
        done = set()
        doneB = set()

        def ensure(h):
            if h < 4 and h not in done:
                done.add(h)
                ret_proj(h, RH[h % 2])

        def ensureB(h):
            ensure(h)
            if h < 4 and h not in doneB:
                doneB.add(h)
                ret_projB(h, RH[h % 2])

        ensure(0)

        def A(i):
            return (i, iters[i][0], iters[i][1], RH[iters[i][0] % 2])

        for r in range(-1, N + 2):
            if 0 <= r - 1 < N:
                ret_R2a(*A(r - 1))
            if 0 <= r < N:
                ensureB(iters[r][0])
                gch = ret_Ru(*A(r))
                ret_R2o(*A(r))
                ret_Ru2(*A(r), gch)
            if 0 <= r - 1 < N:
                ret_R2(*A(r - 1))
            if 0 <= r + 1 < N:
                ensure(iters[r + 1][0])
                ret_R1(*A(r + 1))
            if 0 <= r - 2 < N:
                ret_R3(*A(r - 2))
            if 0 <= r < N and iters[r][1] == 0:
                ensure(iters[r][0] + 1)
            if 0 <= r < N and iters[r][1] == min(2, Ga - 1):
                ensureB(iters[r][0] + 1)
        pmode["proj"] = "a"

    def gates_proj(col0):
        k.tag = "gates"
        for nt in range(2):
            s, stk = wload([(win_b[:, col0 + nt * 512:col0 + (nt + 1) * 512], 0)])
            for c in range(cur["G"]):
                proj_tm(s, stk, 0, 512, c, lambda p, ptk, c=c, nt=nt: k.op(
                    act, [ptk], [sgate_tk[c]], lambda e: e.activation(
                        out=sgate[:, c, nt * 512:(nt + 1) * 512], in_=p[:, :], func=AF.Sigmoid)))

    def branch(wsrc, first):
        k.tag = "branch"
        Ga = cur["G"]
        for nt in range(2):
            banks = [bank("a") for _ in range(Ga)]
            for half in range(2):
                s, stk = wload([(wsrc[half * 1024:(half + 1) * 1024, nt * 512:(nt + 1) * 512], 0)])
                for c in range(Ga):
                    p, ptk = banks[c]
                    for kb in range(8):
                        b = half * 8 + kb
                        mm(p[:, :], retT[:, b, cs(c)], s[:, kb, :], [stk, retT_tk[b][c]], ptk,
                           half == 0 and kb == 0, half == 1 and kb == 7)
            for c in range(Ga):
                p, ptk = banks[c]
                msl = merged[:, c, nt * 512:(nt + 1) * 512]
                gsl = sgate[:, c, nt * 512:(nt + 1) * 512]
                if first:
                    k.op(dve, [ptk, sgate_tk[c]], [mg_tk[c]], lambda e, p=p, msl=msl, gsl=gsl: e.tensor_tensor(
                        msl, p[:, :], gsl, ALU.mult))
                else:
                    (m1, t1) = rot_s[c % 2]
                    k.op(dve, [ptk, sgate_tk[c]], [t1], lambda e, p=p, m1=m1, gsl=gsl: e.tensor_tensor(
                        m1[:, 0:512], p[:, :], gsl, ALU.mult))
                    k.op(dve, [t1, mg_tk[c]], [mg_tk[c]], lambda e, m1=m1, msl=msl: e.tensor_tensor(
                        msl, msl, m1[:, 0:512], ALU.add))

    def ssd_dt(is_meta):
        k.tag = "ssd_dt"
        Ga = cur["G"]
        G32 = Ga * 32
        load_wdt()
        p, ptk = bank("a")
        for c in range(Ga):
            for kc in range(8):
                mm(p[:, c * 32:(c + 1) * 32], uT[:, kc, cs(c)], wdt[:, kc, :], [wdt_tk, uT_tk[c]], ptk, kc == 0, kc == 7)
        pv = p[:, 0:G32].rearrange("p (c n) -> p c n", c=Ga)
        k.op(dve, [ptk, ctk], [dt_tk], lambda e: e.tensor_tensor(
            dtw[:, 0:Ga, :], pv, dtb[:, :].unsqueeze(1).to_broadcast([128, Ga, 32]), ALU.add))
        k.op(act, [dt_tk], [dt_tk], lambda e: e.activation(out=dtw[:, 0:Ga, :], in_=dtw[:, 0:Ga, :], func=AF.Exp))
        k.op(act, [dt_tk, ctk], [dt_tk], lambda e: e.activation(
            out=dtt[:, 0:Ga, :], in_=dtw[:, 0:Ga, :], func=AF.Ln, bias=oneb[:]))
        if is_meta:
            k.op(dve, [dt_tk, ctk], [dt_tk], lambda e: e.tensor_scalar(
                dtt[:, 0:Ga, :], dtt[:, 0:Ga, :], mmask[:, 0:1], None, ALU.mult))
        k.op(dve, [dt_tk, ctk], [dt_tk], lambda e: e.tensor_tensor(
            dta[:, 0:Ga, :], dtt[:, 0:Ga, :], a_b[:, :].unsqueeze(1).to_broadcast([128, Ga, 32]), ALU.mult))
        k.op(dve, [dt_tk], [dt_tk], lambda e: e.tensor_copy(dta_hi[:, 0:Ga, :], dta[:, 0:Ga, :]))
        k.op(dve, [dt_tk], [dt_tk], lambda e: e.tensor_copy(dtw[:, 0:Ga, :], dta_hi[:, 0:Ga, :]))
        k.op(dve, [dt_tk], [dt_tk], lambda e: e.tensor_tensor(dtw[:, 0:Ga, :], dta[:, 0:Ga, :], dtw[:, 0:Ga, :], ALU.subtract))
        k.op(dve, [dt_tk], [dt_tk], lambda e: e.tensor_copy(dta_lo[:, 0:Ga, :], dtw[:, 0:Ga, :]))
        p1, p1tk = bank("a")
        flat = lambda t: t[:, 0:Ga, :].rearrange("p c n -> p (c n)")
        mm(p1[:, 0:G32], triM[:], flat(dta), [ctk, dt_tk], p1tk, True, True)
        p2, p2tk = bank("a")
        mm(p2[:, 0:G32], ones[:], flat(dta), [ctk, dt_tk], p2tk, True, True)
        k.op(act, [p1tk], [dt_tk], lambda e: e.activation(out=flat(csb), in_=p1[:, 0:G32], func=AF.Copy))
        k.op(act, [p1tk], [dt_tk], lambda e: e.activation(out=flat(escs), in_=p1[:, 0:G32], func=AF.Exp))
        k.op(act, [p2tk], [dt_tk], lambda e: e.activation(out=flat(dchk), in_=p2[:, 0:G32], func=AF.Exp))
        k.op(dve, [p2tk, dt_tk], [dt_tk], lambda e: e.tensor_tensor(flat(edec), p2[:, 0:G32], flat(csb), ALU.subtract))
        k.op(act, [dt_tk], [dt_tk], lambda e: e.activation(out=flat(edec), in_=flat(edec), func=AF.Exp))
        dump("dtt", flat(dtt), [dt_tk])
        dump("csb", flat(csb), [dt_tk])

    def conv_in(p, ptk, blk, first_grp, ci):
        k.tag = "conv"
        T = cur["T"]
        xp, xptk = xpre[ci % 3]
        if first_grp:
            k.op(pool, [], [xptk], lambda e: e.memset(xp[:, 0:4], 0.0))
        else:
            k.op(pool, [halo_tk[blk]], [xptk], lambda e: e.tensor_copy(xp[:, 0:4], halo[:, blk, :]))
        k.op(act, [ptk], [xptk], lambda e: e.activation(out=xp[:, 4:4 + T], in_=p[:, 0:T], func=AF.Copy))
        k.op(pool, [xptk], [halo_tk[blk]], lambda e: e.tensor_copy(halo[:, blk, :], xp[:, T:T + 4]))
        dg, dgtk = cdiag[ci % 2]
        for j in range(4):
            k.op(dve, [ctk], [dgtk], lambda e, j=j: e.tensor_scalar(dg[:, j, :], ident[:], cw[:, blk, j:j + 1], None, ALU.mult))
        k.tag = "ssd_proj"

    def conv_out(blk, out_ap, out_tk, ci):
        k.tag = "conv"
        T = cur["T"]
        xp, xptk = xpre[ci % 3]
        dg, dgtk = cdiag[ci % 2]
        cp, cptk = bank("proj")
        for j in range(4):
            mm(cp[:, 0:T], dg[:, j, :], xp[:, 1 + j:1 + j + T], [dgtk, xptk], cptk, j == 0, j == 3)
        k.op(act, [cptk, ctk], [out_tk], lambda e: e.activation(
            out=out_ap[:, 0:T], in_=cp[:, 0:T], func=AF.Silu, bias=cb[:, blk:blk + 1]))
        k.tag = "ssd_proj"

    def ssd_proj(g, d, first_grp):
        k.tag = "ssd_proj"
        Ga = cur["G"]
        s2, t2 = wload([(win_b[:, C_Z + g * 256:C_Z + (g + 1) * 256], 0)])
        s1, t1 = wload([(win_b[:, C_XBC + g * 512:C_XBC + (g + 1) * 512], 0)])
        outs = [(d["xsT"][:, 0, :], d["tk"]["xsT0"], g * 2), (d["xsT"][:, 1, :], d["tk"]["xsT1"], g * 2 + 1),
                (d["bmT"][:, :], d["tk"]["bmT"], 16 + g), (d["cmT"][:, :], d["tk"]["cmT"], 24 + g)]
        for c0 in range(0, Ga, 2):
            ncz = min(2, Ga - c0)
            p, ptk = bank("proj")
            for c in range(c0, c0 + ncz):
                for kc in range(8):
                    mm(p[:, (c - c0) * 256:(c - c0 + 1) * 256], uT[:, kc, cs(c)], s2[:, kc, 0:256],
                       [t2, uT_tk[c]], ptk, kc == 0, kc == 7)
            k.op(act, [ptk], [d["tk"]["sz"]], lambda e, c0=c0, p=p, ncz=ncz: e.activation(
                out=d["sz"][:, c0:c0 + ncz, :].rearrange("p c n -> p (c n)"), in_=p[:, 0:ncz * 256], func=AF.Silu))
        for b in range(5):
            if b < 4:
                def ev(p, ptk, idx, b=b):
                    conv_in(p, ptk, outs[b][2], first_grp, g * 4 + b)
                proj_fm(s1, t1, 0, [(b, b * 128)], ev)
            if b >= 1:
                conv_out(outs[b - 1][2], outs[b - 1][0], outs[b - 1][1], g * 4 + b - 1)

    def ssd_projB(g, d):
        k.tag = "ssd_proj"
        Ga = cur["G"]
        for c in range(Ga):
            p, ptk = bank("proj")
            pb = p[:].bitcast(BF16)
            for b in range(2):
                tr(pb[:, b * 128:(b + 1) * 128], d["xsT"][:, b, cs(c)], [d["tk"][f"xsT{b}"]], ptk, False)
            tr(pb[:, 256:384], d["bmT"][:, cs(c)], [d["tk"]["bmT"]], ptk, True)
            k.op(dve, [ptk, dt_tk], [d["tkc"]["xdt"][c]], lambda e, c=c, pb=pb: e.tensor_tensor(
                d["xdt"][:, c, :].rearrange("p (h q) -> p h q", h=4), pb[:, 0:256].rearrange("p (h q) -> p h q", h=4),
                dtt[:, c, g * 4:(g + 1) * 4].unsqueeze(2).to_broadcast([128, 4, 64]), ALU.mult))
            k.op(act, [ptk], [d["tkc"]["bm"][c]], lambda e, c=c, pb=pb: e.activation(
                out=d["bm"][:, c, :], in_=pb[:, 256:384], func=AF.Copy))
            k.op(pool, [d["tkc"]["xdt"][c], dt_tk], [d["tkc"]["xdec"][c]], lambda e, c=c: e.tensor_tensor(
                d["xdec"][:, c, :].rearrange("p (h q) -> p h q", h=4), d["xdt"][:, c, :].rearrange("p (h q) -> p h q", h=4),
                edec[:, c, g * 4:(g + 1) * 4].unsqueeze(2).to_broadcast([128, 4, 64]), ALU.mult))

    def ssd_P0(i, g, c, d):
        k.tag = "ssd_chunk"
        S = sst[i % 2]
        Rb = S["R"].rearrange("p h t -> p (h t)").bitcast(BF16)
        for w, src in ((0, dta_hi), (1, dta_lo)):
            k.op(pool, [ctk, dt_tk], [S["tk"]["R"]], lambda e, w=w, src=src: e.tensor_tensor(
                Rb[:, w * 512:(w + 1) * 512].rearrange("p (h t) -> p h t", h=4),
                triM_b[:, :].unsqueeze(1).to_broadcast([128, 4, 128]),
                src[:, c, g * 4:(g + 1) * 4].unsqueeze(2).to_broadcast([128, 4, 128]), ALU.mult))

    def ssd_P1(i, g, c, d):
        k.tag = "ssd_chunk"
        S = sst[i % 2]
        tk = S["tk"]
        A, Atk = psum[3], ps_tk[3]
        B, Btk = psum[4 + i % 2], ps_tk[4 + i % 2]
        mm(B[:, 0:128], d["bmT"][:, cs(c)], d["cmT"][:, cs(c)], [d["tk"]["bmT"], d["tk"]["cmT"]], Btk, True, True)
        Rb = S["R"].rearrange("p h t -> p (h t)").bitcast(BF16)
        mm(A[:, :], Umat_b[:], Rb[:, 0:512], [ctk, tk["R"]], Atk, True, False)
        mm(A[:, :], Umat_b[:], Rb[:, 512:1024], [ctk, tk["R"]], Atk, False, True)
        k.op(dve, [Btk, ctk], [tk["CBm"]], lambda e: e.tensor_tensor(S["CBm"][:, :], B[:, 0:128], causT[:], ALU.mult))
        k.op(act, [Atk], [tk["E"]], lambda e: e.activation(
            out=S["E"].rearrange("p h t -> p (h t)"), in_=A[:, :], func=AF.Exp))

    def ssd_P1b(i, g, c, d):
        k.tag = "ssd_chunk"
        S = sst[i % 2]
        tk = S["tk"]
        k.op(dve, [tk["E"], tk["CBm"]], [tk["MT"]], lambda e: e.tensor_tensor(
            S["MT"], S["E"], S["CBm"][:, :].unsqueeze(1).to_broadcast([128, 4, 128]), ALU.mult))

    def ssd_Ps(i, g, c, d):
        k.tag = "ssd_chunk"
        C, Ctk = psum[6 + i % 2], ps_tk[6 + i % 2]
        ssl = sstate_f[:, g * 256:(g + 1) * 256]
        k.op(pool, [ss_tk[g], dt_tk], [ss_tk[g]], lambda e: e.tensor_tensor(
            ssl.rearrange("p (h q) -> p h q", h=4), ssl.rearrange("p (h q) -> p h q", h=4),
            dchk[:, c, g * 4:(g + 1) * 4].unsqueeze(2).to_broadcast([128, 4, 64]), ALU.mult))
        mm(C[:, 0:256], d["bm"][:, c, :], d["xdec"][:, c, :], [d["tkc"]["bm"][c], d["tkc"]["xdec"][c]], Ctk, True, True)
        k.op(dve, [Ctk, ss_tk[g]], [ss_tk[g]], lambda e: e.tensor_tensor(ssl, ssl, C[:, 0:256], ALU.add))
        k.op(act, [ss_tk[g]], [ssb_tk[g]], lambda e: e.activation(
            out=sstate_b[:, g * 256:(g + 1) * 256], in_=ssl, func=AF.Copy))

    def ssd_P2(i, g, c, d):
        k.tag = "ssd_chunk"
        S = sst[i % 2]
        tk = S["tk"]
        B, Btk = psum[4 + i % 2], ps_tk[4 + i % 2]
        C, Ctk = psum[6 + i % 2], ps_tk[6 + i % 2]
        for b in range(2):
            mm(C[:, 256 + b * 128:256 + (b + 1) * 128], d["xsT"][:, b, cs(c)], Dd[:, g * 2 + b, :],
               [d["tk"][f"xsT{b}"], ctk], Ctk, True, False)
            for hh in (2 * b, 2 * b + 1):
                mm(C[:, 256 + hh * 64:256 + (hh + 1) * 64], S["MT"][:, hh, :], d["xdt"][:, c, hh * 64:(hh + 1) * 64],
                   [tk["MT"], d["tkc"]["xdt"][c]], Ctk, False, hh == 3)

    def ssd_Pyb(i, g, c, d):
        k.tag = "ssd_chunk"
        B, Btk = psum[4 + i % 2], ps_tk[4 + i % 2]
        mm(B[:, 128:384], d["cmT"][:, cs(c)], sstate_b[:, g * 256:(g + 1) * 256], [d["tk"]["cmT"], ssb_tk[g]], Btk, True, True)

    def ssd_P2b1(i, g, c, d):
        k.tag = "ssd_chunk"
        S = sst[i % 2]
        tk = S["tk"]
        B, Btk = psum[4 + i % 2], ps_tk[4 + i % 2]
        C, Ctk = psum[6 + i % 2], ps_tk[6 + i % 2]
        k.op(dve, [Btk, dt_tk], [tk["yc"]], lambda e: e.tensor_tensor(
            S["yc"].rearrange("p (h q) -> p h q", h=4), B[:, 128:384].rearrange("p (h q) -> p h q", h=4),
            escs[:, c, g * 4:(g + 1) * 4].unsqueeze(2).to_broadcast([128, 4, 64]), ALU.mult))
        k.op(dve, [Ctk, tk["yc"]], [tk["yg"]], lambda e: e.tensor_tensor(S["yg"][:, :], C[:, 256:512], S["yc"][:, :], ALU.add))

    def ssd_P2bg(i, g, c, d):
        k.tag = "ssd_chunk"
        S = sst[i % 2]
        tk = S["tk"]
        k.op(pool, [tk["yg"], d["tk"]["sz"]], [tk["yg"]], lambda e: e.tensor_tensor(
            S["yg"][:, :], S["yg"][:, :], d["sz"][:, c, :], ALU.mult))

    def ssd_P2b2(i, g, c, d):
        k.tag = "ssd_chunk"
        S = sst[i % 2]
        tk = S["tk"]
        scol = 32 + 2 * (i % 8)
        stt = stk_(scol)
        k.op(act, [tk["yg"]], [stt], lambda e: e.activation(
            out=junk[:, 0:256], in_=S["yg"][:, :], func=AF.Square, accum_out=stat[:, scol:scol + 1]))
        rstd(scol, scol + 1, 1, 1.0 / 256, stt)
        k.op(act, [tk["yg"], stt], [tk["yn"]], lambda e: e.activation(
            out=S["yn"][:, :], in_=S["yg"][:, :], func=AF.Copy, scale=stat[:, scol + 1:scol + 2]))

    def ssd_P3(i, g, c, d):
        k.tag = "ssd_chunk"
        S = sst[i % 2]
        tk = S["tk"]
        B, Btk = psum[4 + i % 2], ps_tk[4 + i % 2]
        pb = B[:].bitcast(BF16)
        for b in range(2):
            tr(pb[:, 768 + b * 128:768 + (b + 1) * 128], S["yn"][:, b * 128:(b + 1) * 128], [tk["yn"]], Btk, b == 1)
        k.op(dve, [Btk, ctk], [retT_tk[g * 2][c], retT_tk[g * 2 + 1][c]], lambda e: e.tensor_tensor(
            retT[:, g * 2:g * 2 + 2, cs(c)], pb[:, 768:1024].rearrange("p (a b) -> p a b", a=2),
            ssdn[:, g * 2:g * 2 + 2].unsqueeze(2).to_broadcast([128, 2, 128]), ALU.mult))

    def ssd(is_meta, first_grp):
        ssd_dt(is_meta)
        Ga = cur["G"]
        pmode["proj"] = "p3"
        iters = [(g, c) for g in range(8) for c in range(Ga)]
        N = len(iters)
        done = set()
        doneB = set()

        def ensure(g):
            if g < 8 and g not in done:
                done.add(g)
                ssd_proj(g, SG[g % 2], first_grp)

        def ensureB(g):
            ensure(g)
            if g < 8 and g not in doneB:
                doneB.add(g)
                ssd_projB(g, SG[g % 2])

        def args(i):
            g, c = iters[i]
            return (i, g, c, SG[g % 2])

        ensure(0)
        for r in range(-2, N + 2):
            if 0 <= r + 1 < N:
                ensure(iters[r + 1][0])
            if 0 <= r < N:
                ensureB(iters[r][0])
            if 0 <= r - 1 < N:
                ssd_P2b1(*args(r - 1))
            if 0 <= r < N:
                ssd_Pyb(*args(r))
                ssd_Ps(*args(r))
            if 0 <= r - 1 < N:
                ssd_P2bg(*args(r - 1))
            if 0 <= r + 2 < N:
                ssd_P0(*args(r + 2))
            if 0 <= r + 1 < N:
                ssd_P1(*args(r + 1))
            if 0 <= r - 1 < N:
                ssd_P2b2(*args(r - 1))
            if 0 <= r < N:
                ssd_P2(*args(r))
            if 0 <= r + 1 < N:
                ssd_P1b(*args(r + 1))
            if 0 <= r - 2 < N:
                ssd_P3(*args(r - 2))
            if 0 <= r < N and iters[r][1] == 0:
                ensure(iters[r][0] + 1)
            if 0 <= r < N and iters[r][1] == min(2, Ga - 1):
                ensureB(iters[r][0] + 1)
        pmode["proj"] = "a"

    def out_proj_and_residual(gvec_tile):
        k.tag = "outproj"
        Ga = cur["G"]
        for c in range(Ga):
            un, un_tk = un2[c % 2], un2_tk[c % 2]
            k.op(act, [mg_tk[c]], [un_tk], lambda e, c=c, un=un: e.activation(out=un[:], in_=merged[:, c, :], func=AF.Copy))
            p, ptk = bank("a")
            pb = p[:].bitcast(BF16)
            for kc in range(8):
                tr(pb[:, kc * 128:(kc + 1) * 128], un[:, kc * 128:(kc + 1) * 128], [un_tk], ptk, kc == 7)
            k.op(dve, [ptk], [uT_tk[c]], lambda e, c=c, pb=pb: e.tensor_copy(
                uT[:, :, cs(c)], pb.rearrange("p (a b) -> p a b", a=8)))
        slots = [wload([(wout_b[:, nt * 512:(nt + 1) * 512], 0)]) for nt in range(2)]
        for c in range(Ga):
            bks = []
            for nt in range(2):
                s, stk = slots[nt]
                p, ptk = bank("a")
                for kc in range(8):
                    mm(p[:, :], uT[:, kc, cs(c)], s[:, kc, :], [stk, uT_tk[c]], ptk, kc == 0, kc == 7)
                bks.append((p, ptk))
            norm_residual(bks, c, gvec_tile, 48 + 4 * (c % 4))

    def norm_residual(bks, c, gvec_tile, scol):
        k.tag = "norm_res"
        for nt in range(2):
            p, ptk = bks[nt]
            k.op(act, [ptk], [stk_(scol)], lambda e, p=p, nt=nt: e.activation(
                out=junk[:, 0:512], in_=p[:, :], func=AF.Square, accum_out=stat[:, scol + nt:scol + nt + 1]))
        k.op(dve, [stk_(scol)], [stk_(scol)], lambda e: e.tensor_tensor(
            stat[:, scol + 2:scol + 3], stat[:, scol:scol + 1], stat[:, scol + 1:scol + 2], ALU.add))
        rstd(scol + 2, scol + 3, 1, 1.0 / D, stk_(scol))
        for nt in range(2):
            p, ptk = bks[nt]
            (m1, t1) = rot_s[nt]
            k.op(dve, [ptk, stk_(scol), ctk], [t1], lambda e, p=p, m1=m1, nt=nt: e.scalar_tensor_tensor(
                m1[:, :], p[:, :], stat[:, scol + 3:scol + 4], gvec_tile[:, nt * 512:(nt + 1) * 512], ALU.mult, ALU.mult))
            hs = hbuf[:, c, nt * 512:(nt + 1) * 512]
            k.op(dve, [t1, h_tk[c]], [h_tk[c]], lambda e, m1=m1, hs=hs: e.tensor_tensor(hs, hs, m1[:, :], ALU.add))

    def ffn(nxt=None):
        prenorm_uT(gfpre)
        k.tag = "ffn_up"
        Ga, T = cur["G"], cur["T"]
        nfb = DFF // 128
        fb = 0
        while fb < nfb:
            nb = min(4, nfb - fb)
            sg_, sgtk_ = wload([(wgate_b[:, fb * 128:(fb + nb) * 128], 0)])
            su_, sutk_ = wload([(wup_b[:, fb * 128:(fb + nb) * 128], 0)])
            for b in range(nb):
                pg, pgtk = bank("a")
                for kc in range(8):
                    mm(pg[:, 0:T], sg_[:, kc, b * 128:(b + 1) * 128], uT[:, kc, 0:T], [sgtk_] + uT_tk, pgtk, kc == 0, kc == 7)
                pu, putk = bank("a")
                for kc in range(8):
                    mm(pu[:, 0:T], su_[:, kc, b * 128:(b + 1) * 128], uT[:, kc, 0:T], [sutk_] + uT_tk, putk, kc == 0, kc == 7)
                sg2, sg2tk = sgt[(fb + b) % 2]
                k.op(act, [pgtk], [sg2tk], lambda e, pg=pg, sg2=sg2: e.activation(out=sg2[:, 0:T], in_=pg[:, 0:T], func=AF.Silu))
                k.op(dve, [putk, sg2tk], [hid_tk[fb + b]], lambda e, pu=pu, sg2=sg2, i=fb + b: e.tensor_tensor(
                    hidT[:, i, 0:T], pu[:, 0:T], sg2[:, 0:T], ALU.mult))
            fb += nb
        segs = [(0, 8), (8, 8), (16, 6)]
        k.tag = "ffn_down"
        if nxt is not None:
            prenorm_stats(xn, xn_tkl, 8)
            k.tag = "ffn_down"
        napplied = 0
        for nt in range(2):
            banks = {c: bank("b") for c in range(Ga)}
            for si, (f0, nf) in enumerate(segs):
                s, stk = wload([(wdown_b[f0 * 128:(f0 + nf) * 128, nt * 512:(nt + 1) * 512], 0)])
                for c in range(Ga):
                    p, ptk = banks[c]
                    for kb in range(nf):
                        mm(p[:, :], hidT[:, f0 + kb, cs(c)], s[:, kb, :], [stk, hid_tk[f0 + kb]], ptk,
                           si == 0 and kb == 0, si == 2 and kb == nf - 1)
                if nxt is not None and napplied < Ga and not (nt == 0 and si == 0):
                    prenorm_apply(gpre, xn, xn_tkl, napplied, 8, on_dve=True)
                    napplied += 1
                    k.tag = "ffn_down"
            if nt == 1 and nxt is not None:
                while napplied < Ga:
                    prenorm_apply(gpre, xn, xn_tkl, napplied, 8, on_dve=True)
                    napplied += 1
                rope_tables(nxt)
                k.tag = "ffn_down"
            for c in range(Ga):
                p, ptk = banks[c]
                scol = 48 + 4 * (c % 4)
                k.op(act, [ptk], [stk_(scol)], lambda e, p=p, nt=nt, scol=scol: e.activation(
                    out=junk[:, 0:512], in_=p[:, :], func=AF.Square, accum_out=stat[:, scol + nt:scol + nt + 1]))
                k.op(act, [ptk], [mg_tk[c]], lambda e, p=p, nt=nt, c=c: e.activation(
                    out=merged[:, c, nt * 512:(nt + 1) * 512], in_=p[:, :], func=AF.Copy))
        for c in range(Ga):
            scol = 48 + 4 * (c % 4)
            k.op(dve, [stk_(scol)], [stk_(scol)], lambda e, scol=scol: e.tensor_tensor(
                stat[:, scol + 2:scol + 3], stat[:, scol:scol + 1], stat[:, scol + 1:scol + 2], ALU.add))
            rstd(scol + 2, scol + 3, 1, 1.0 / D, stk_(scol))
            k.op(dve, [mg_tk[c], stk_(scol), ctk], [mg_tk[c]], lambda e, c=c, scol=scol: e.scalar_tensor_tensor(
                merged[:, c, :], merged[:, c, :], stat[:, scol + 3:scol + 4], gfpost[:, :], ALU.mult, ALU.mult))
            k.op(dve, [mg_tk[c], h_tk[c]], [mg_tk[c]], lambda e, c=c: e.tensor_tensor(
                merged[:, c, :], merged[:, c, :], hbuf[:, c, :], ALU.add))

    out_tk = Tk("ydram")

    def zero_states():
        for i in range(8):
            k.op(pool, [], [rs_tk[i]], lambda e, i=i: e.memset(rstate_f[:, i, :], 0.0))
            k.op(pool, [], [rsb_tk[i]], lambda e, i=i: e.memset(rstate_b[:, i, :], 0.0))
            k.op(pool, [], [ss_tk[i]], lambda e, i=i: e.memset(sstate_f[:, i * 256:(i + 1) * 256], 0.0))
            k.op(pool, [], [ssb_tk[i]], lambda e, i=i: e.memset(sstate_b[:, i * 256:(i + 1) * 256], 0.0))

    def run_group(s, gi, is_meta, prefetched=False, nxt=None):
        first_grp = is_meta
        if is_meta:
            cur["G"], cur["T"] = 1, 128
            k.op(pool, [], [h_tk[0]], lambda e: e.memset(hbuf[:, 0, :], 0.0))
            k.dma(sp, hbuf[128 - NMETA:128, 0, :], meta[:, :], [], [h_tk[0]])
            pos0 = NMETA - 128
        else:
            cur["G"], cur["T"] = G, T
            r0 = gi * T
            if not prefetched:
                for c in range(G):
                    k.dma(sp, hbuf[:, c, :], x[s, r0 + c * 128:r0 + (c + 1) * 128, :], [], [h_tk[c]])
            pos0 = NMETA + r0
        if not prefetched:
            prenorm_uT(gpre)
            rope_tables(pos0)
        dump("uT", uT[:, 0, :], uT_tk)
        k.handoff(all_tks_ffn() + all_tks_ssd(), all_tks_ret())
        retention()
        dump("retT", retT[:, 0, :], [t for l in retT_tk for t in l])
        if not is_meta:
            gates_proj(C_GATE)
            branch(wret_b, True)
        dump("merged_a", merged[:, 0, :], mg_tk)
        k.handoff(all_tks_ret(), all_tks_ssd())
        ssd(is_meta, first_grp)
        dump("yT", retT[:, 0, :], [t for l in retT_tk for t in l])
        if is_meta:
            return
        gates_proj(C_GATE + 1024)
        branch(wssd_b, False)
        dump("merged", merged[:, 0, :], mg_tk)
        out_proj_and_residual(gpost)
        if nxt is not None:
            s2, gi2 = nxt
            for c in range(G):
                k.dma(sp, xn[:, c, :], x[s2, gi2 * T + c * 128:gi2 * T + (c + 1) * 128, :], [], xn_tks)
        dump("hmid", hbuf[:, 0, :], h_tk)
        k.handoff(all_tks_ssd(), all_tks_ffn())
        ffn(None if nxt is None else NMETA + nxt[1] * T)
        for c in range(G):
            k.dma(pool, y[s, r0 + c * 128:r0 + (c + 1) * 128, :], merged[:, c, :], [mg_tk[c]], [out_tk])
        if nxt is not None:
            for c in range(G):
                k.dma(pool, hbuf[:, c, :], xn[:, c, :], xn_tks, [h_tk[c]])

    sv_tk = Tk("stsave")
    for s in range(NSEQ):
        if s == 0:
            zero_states()
            run_group(s, 0, True)
            if NSEQ > 1:
                k.dma(sp, st_r[:, :], rstate_f[:].rearrange("p a b -> p (a b)"), rs_tk, [sv_tk])
                k.dma(sp, st_s[:, :], sstate_f[:, :], ss_tk, [sv_tk])
                k.dma(sp, st_h[:, :], halo[:].rearrange("p a b -> p (a b)"), halo_tk, [sv_tk])
        else:
            k.dma(sp, rstate_f[:].rearrange("p a b -> p (a b)"), st_r[:, :], [sv_tk], rs_tk)
            k.dma(sp, sstate_f[:, :], st_s[:, :], [sv_tk], ss_tk)
            k.dma(sp, halo[:].rearrange("p a b -> p (a b)"), st_h[:, :], [sv_tk], halo_tk)
            for i in range(8):
                k.op(pool, [rs_tk[i]], [rsb_tk[i]], lambda e, i=i: e.tensor_copy(rstate_b[:, i, :], rstate_f[:, i, :]))
                k.op(act, [ss_tk[i]], [ssb_tk[i]], lambda e, i=i: e.activation(
                    out=sstate_b[:, i * 256:(i + 1) * 256], in_=sstate_f[:, i * 256:(i + 1) * 256], func=AF.Copy))
        for gi in range(NGRP):
            idx = s * NGRP + gi
            nx = None
            if PREFETCH and idx + 1 < NSEQ * NGRP:
                nx = ((idx + 1) // NGRP, (idx + 1) % NGRP)
            run_group(s, gi, False, prefetched=(PREFETCH and idx > 0), nxt=nx)
    for q in (sp, pool):
        for i, sm in enumerate(q.dma_sems):
            n = (q.dma_n - i + len(q.dma_sems) - 1) // len(q.dma_sems)
            if n > 0:
                k._wait(sp, (sm, 16 * n, None))
    for e_ in (pe, act, dve, pool):
        if e_.cnt:
            k._wait(sp, (e_.sem, e_.cnt, e_))
    es.close()
    return nc, dbg_out, k


PARAM_NAMES = ["meta_tokens", "norm_mix_pre", "w_in", "conv_w", "conv_b", "dt_bias", "a_log", "d_skip", "ssd_norm",
               "w_ret_branch", "w_ssd_branch", "w_out", "norm_mix_post", "norm_ffn_pre", "w_gate", "w_up", "w_down",
               "norm_ffn_post"]


def _prep_params(inputs):
    p = {}
    for n in PARAM_NAMES:
        a = np.ascontiguousarray(np.asarray(inputs[n], dtype=np.float32))
        if n == "meta_tokens":
            p[n] = a
        elif a.ndim == 2:
            p[n] = a
        else:
            p[n] = np.ascontiguousarray(a[0])
    return p


def kernel(**inputs):
    x = np.asarray(inputs["x"], dtype=np.float32)
    B = x.shape[0]
    ncores = 8
    nseq = B // ncores
    G = 4
    ngrp = SEQ // (128 * G)
    nc, _, _ = build(nseq, ngrp, G)
    params = _prep_params(inputs)
    in_maps = []
    for i in range(ncores):
        m = dict(params)
        m["x"] = np.ascontiguousarray(x[i * nseq:(i + 1) * nseq])
        in_maps.append(m)
    res = run_bass_kernel_spmd(nc, in_maps, core_ids=list(range(ncores)))
    out = np.concatenate([np.asarray(r["y"]) for r in res.results], axis=0)
    return out.astype(np.float32)
```
